# Optimizing a Trainium2 kernel written in Bass

```python
import jax, jax.numpy as jnp
from jax import lax
import numpy as np

D_MODEL = 1024
BATCH = 16
SEQ = 4096
DEPTH = 2

SSD_EXPAND = 2
SSD_D_INNER = SSD_EXPAND * D_MODEL
SSD_HEAD_DIM = 64
SSD_N_HEADS = SSD_D_INNER // SSD_HEAD_DIM
SSD_N_GROUPS = 4
SSD_HEADS_PER_GROUP = SSD_N_HEADS // SSD_N_GROUPS
SSD_D_STATE = 128
SSD_CONV_WIDTH = 4
SSD_CHUNK = 128
SSD_CONV_DIM = SSD_D_INNER + 2 * SSD_N_GROUPS * SSD_D_STATE
DT_MIN = 1e-3
DT_MAX = 1e-1

SC_WIDTH = D_MODEL
SC_CONV_WIDTH = 3

N_BRANCHES = 2

D_FF = 2816
FFN_CONV_WIDTH = 3

EPS = 1e-6

IN_SPLIT_SIZES = (SSD_D_INNER, SSD_CONV_DIM, SSD_N_HEADS,
                  SC_WIDTH, SC_WIDTH, SC_WIDTH, N_BRANCHES * D_MODEL)
D_IN_PROJ = sum(IN_SPLIT_SIZES)

kernel_name = "hybrid_ssd_shortconv_adaln_block"


def rmsnorm(x, g):
    xf = x.astype(jnp.float32)
    xf = xf * lax.rsqrt(jnp.mean(xf * xf, axis=-1, keepdims=True) + EPS)
    return xf.astype(x.dtype) * g


def grouped_rmsnorm(y, g, n_groups):
    shp = y.shape
    yf = y.astype(jnp.float32).reshape(*shp[:-1], n_groups, shp[-1] // n_groups)
    yf = yf * lax.rsqrt(jnp.mean(yf * yf, axis=-1, keepdims=True) + EPS)
    return yf.reshape(shp) * g


def causal_dwconv(u, w, b=None):
    k_width = w.shape[0]
    seqlen = u.shape[1]
    up = jnp.pad(u, ((0, 0), (k_width - 1, 0), (0, 0)))
    y = up[:, 0:seqlen] * w[0]
    for k in range(1, k_width):
        y = y + up[:, k:k + seqlen] * w[k]
    if b is not None:
        y = y + b
    return y


def ssd_chunked(xh, dt, a, bmat, cmat):
    bsz, seqlen = xh.shape[0], xh.shape[1]
    nc = seqlen // SSD_CHUNK
    L, G, R, P, N = SSD_CHUNK, SSD_N_GROUPS, SSD_HEADS_PER_GROUP, SSD_HEAD_DIM, SSD_D_STATE
    x = (xh * dt[..., None]).reshape(bsz, nc, L, G, R, P)
    adt = jnp.moveaxis((dt * a).reshape(bsz, nc, L, G, R), 2, -1)
    a_cs = jnp.cumsum(adt, axis=-1)
    bc = bmat.reshape(bsz, nc, L, G, N)
    cc = cmat.reshape(bsz, nc, L, G, N)

    causal = jnp.tril(jnp.ones((L, L), dtype=bool))
    decay = jnp.exp(jnp.where(causal, a_cs[..., :, None] - a_cs[..., None, :], -jnp.inf))
    scores = jnp.einsum("bclgn,bcsgn->bcgls", cc, bc)
    m = scores[:, :, :, None] * decay
    y_diag = jnp.einsum("bcgrls,bcsgrp->bclgrp", m, x)

    decay_states = jnp.exp(a_cs[..., -1:] - a_cs)
    states = jnp.einsum("bclgn,bcgrl,bclgrp->bcgrpn", bc, decay_states, x)
    chunk_decay = jnp.exp(a_cs[..., -1])

    def step(h, inp):
        s_c, d_c = inp
        h_new = h * d_c[..., None, None] + s_c
        return h_new, h

    h0 = jnp.zeros((bsz, G, R, P, N), dtype=states.dtype)
    _, prev = lax.scan(step, h0, (jnp.moveaxis(states, 1, 0), jnp.moveaxis(chunk_decay, 1, 0)))
    prev = jnp.moveaxis(prev, 0, 1)

    y_off = jnp.einsum("bclgn,bcgrpn,bcgrl->bclgrp", cc, prev, jnp.exp(a_cs))
    return (y_diag + y_off).reshape(bsz, seqlen, SSD_N_HEADS, P)


def ssd_branch(z, xbc, dt_raw, conv_w, conv_b, dt_bias, a_log, d_skip, norm_g):
    bsz, seqlen = z.shape[0], z.shape[1]
    xbc = jax.nn.silu(causal_dwconv(xbc, conv_w, conv_b))
    gn = SSD_N_GROUPS * SSD_D_STATE
    xs, bm, cm = jnp.split(xbc, [SSD_D_INNER, SSD_D_INNER + gn], axis=-1)
    xh = xs.reshape(bsz, seqlen, SSD_N_HEADS, SSD_HEAD_DIM)
    dt = jax.nn.softplus(dt_raw.astype(jnp.float32) + dt_bias.astype(jnp.float32))
    a = -jnp.exp(a_log.astype(jnp.float32))
    y = ssd_chunked(xh.astype(jnp.float32), dt, a,
                    bm.reshape(bsz, seqlen, SSD_N_GROUPS, SSD_D_STATE).astype(jnp.float32),
                    cm.reshape(bsz, seqlen, SSD_N_GROUPS, SSD_D_STATE).astype(jnp.float32))
    y = y + d_skip.astype(jnp.float32)[:, None] * xh.astype(jnp.float32)
    y = y.reshape(bsz, seqlen, SSD_D_INNER) * jax.nn.silu(z.astype(jnp.float32))
    return grouped_rmsnorm(y, norm_g, SSD_N_GROUPS).astype(z.dtype)


def short_conv_branch(b_gate, c_gate, h, conv_w):
    return b_gate * causal_dwconv(c_gate * h, conv_w)


def hybrid_layer(x, c_act, ada_w, ada_b, mix_pre_g, mix_post_g, w_in,
                 ssd_conv_w, ssd_conv_b, ssd_dt_bias, ssd_a_log, ssd_d, ssd_norm_g,
                 w_ssd_out, sc_conv_w, w_sc_out, w_o,
                 ffn_pre_g, ffn_post_g, w_up, ffn_conv_w, ffn_conv_b, w_down):
    mod = c_act @ ada_w + ada_b
    sh1, sc1, gt1, sh2, sc2, gt2 = [m[:, None, :] for m in jnp.split(mod, 6, axis=-1)]

    h = rmsnorm(x, mix_pre_g) * (1.0 + sc1) + sh1
    proj = h @ w_in
    points, acc = [], 0
    for s in IN_SPLIT_SIZES[:-1]:
        acc += s
        points.append(acc)
    z, xbc, dt_raw, sc_b, sc_c, sc_h, gates = jnp.split(proj, points, axis=-1)
    y_ssd = ssd_branch(z, xbc, dt_raw, ssd_conv_w, ssd_conv_b, ssd_dt_bias,
                       ssd_a_log, ssd_d, ssd_norm_g) @ w_ssd_out
    y_sc = short_conv_branch(sc_b, sc_c, sc_h, sc_conv_w) @ w_sc_out
    g_ssd, g_sc = jnp.split(jax.nn.sigmoid(gates), 2, axis=-1)
    mix = (g_ssd * y_ssd + g_sc * y_sc) @ w_o
    x = x + gt1 * rmsnorm(mix, mix_post_g)

    h = rmsnorm(x, ffn_pre_g) * (1.0 + sc2) + sh2
    u = causal_dwconv(h @ w_up, ffn_conv_w, ffn_conv_b)
    u_gate, u_val = jnp.split(u, 2, axis=-1)
    f = (jax.nn.silu(u_gate) * u_val) @ w_down
    x = x + gt2 * rmsnorm(f, ffn_post_g)
    return x


def setup_inputs(seed: int = 0) -> dict:
    key = jax.random.key(seed)
    ks = jax.random.split(key, 32)
    f32 = jnp.float32

    def dense(k, fan_in, fan_out, scale=1.0):
        return jax.random.normal(k, (DEPTH, fan_in, fan_out), f32) * (scale * fan_in ** -0.5)

    def gain(k, n):
        return 1.0 + 0.05 * jax.random.normal(k, (DEPTH, n), f32)

    def small(k, shape, s=0.02):
        return s * jax.random.normal(k, shape, f32)

    u = jax.random.uniform(ks[8], (DEPTH, SSD_N_HEADS), f32)
    dt0 = jnp.exp(u * (np.log(DT_MAX) - np.log(DT_MIN)) + np.log(DT_MIN))
    dt_bias = dt0 + jnp.log(-jnp.expm1(-dt0))
    a_log = jnp.log(jax.random.uniform(ks[9], (DEPTH, SSD_N_HEADS), f32, 1.0, 16.0))

    return {
        "x": jax.random.normal(ks[0], (BATCH, SEQ, D_MODEL), f32),
        "c": jax.random.normal(ks[1], (BATCH, D_MODEL), f32),
        "ada_w": dense(ks[2], D_MODEL, 6 * D_MODEL, 0.5),
        "ada_b": small(ks[3], (DEPTH, 6 * D_MODEL)),
        "mix_pre_g": gain(ks[4], D_MODEL),
        "mix_post_g": gain(ks[5], D_MODEL),
        "w_in": dense(ks[6], D_MODEL, D_IN_PROJ),
        "ssd_conv_w": jax.random.normal(ks[7], (DEPTH, SSD_CONV_WIDTH, SSD_CONV_DIM), f32) * SSD_CONV_WIDTH ** -0.5,
        "ssd_conv_b": small(ks[10], (DEPTH, SSD_CONV_DIM)),
        "ssd_dt_bias": dt_bias,
        "ssd_a_log": a_log,
        "ssd_d": 1.0 + 0.1 * jax.random.normal(ks[11], (DEPTH, SSD_N_HEADS), f32),
        "ssd_norm_g": gain(ks[12], SSD_D_INNER),
        "w_ssd_out": dense(ks[13], SSD_D_INNER, D_MODEL),
        "sc_conv_w": jax.random.normal(ks[14], (DEPTH, SC_CONV_WIDTH, SC_WIDTH), f32) * SC_CONV_WIDTH ** -0.5,
        "w_sc_out": dense(ks[15], SC_WIDTH, D_MODEL),
        "w_o": dense(ks[16], D_MODEL, D_MODEL),
        "ffn_pre_g": gain(ks[17], D_MODEL),
        "ffn_post_g": gain(ks[18], D_MODEL),
        "w_up": dense(ks[19], D_MODEL, 2 * D_FF),
        "ffn_conv_w": jax.random.normal(ks[20], (DEPTH, FFN_CONV_WIDTH, 2 * D_FF), f32) * FFN_CONV_WIDTH ** -0.5,
        "ffn_conv_b": small(ks[21], (DEPTH, 2 * D_FF)),
        "w_down": dense(ks[22], D_FF, D_MODEL),
    }


def reference(x, c, ada_w, ada_b, mix_pre_g, mix_post_g, w_in,
              ssd_conv_w, ssd_conv_b, ssd_dt_bias, ssd_a_log, ssd_d, ssd_norm_g,
              w_ssd_out, sc_conv_w, w_sc_out, w_o,
              ffn_pre_g, ffn_post_g, w_up, ffn_conv_w, ffn_conv_b, w_down):
    c_act = jax.nn.silu(c)
    for i in range(DEPTH):
        x = hybrid_layer(x, c_act, ada_w[i], ada_b[i], mix_pre_g[i], mix_post_g[i], w_in[i],
                         ssd_conv_w[i], ssd_conv_b[i], ssd_dt_bias[i], ssd_a_log[i], ssd_d[i],
                         ssd_norm_g[i], w_ssd_out[i], sc_conv_w[i], w_sc_out[i], w_o[i],
                         ffn_pre_g[i], ffn_post_g[i], w_up[i], ffn_conv_w[i], ffn_conv_b[i],
                         w_down[i])
    return x
```

```python
import numpy as np
import concourse.bass as bass
import concourse.mybir as mybir
from concourse.bass_utils import run_bass_kernel_spmd

F32 = mybir.dt.float32
BF16 = mybir.dt.bfloat16
U8 = mybir.dt.uint8
AF = mybir.ActivationFunctionType
ALU = mybir.AluOpType

D = 1024
T = 512
EPS = 1e-6
C_Z, C_XBC, C_DT, C_SCB, C_SCC, C_SCH, C_GSSD, C_GSC = 0, 2048, 5120, 5152, 6176, 7200, 8224, 9248
WSHAPES = {"w_in": [1024, 10272], "w_ssd_out": [2048, 1024], "w_sc_out": [1024, 1024],
           "w_o": [1024, 1024], "w_up": [1024, 5632], "w_down": [2816, 1024]}
G_MIXPRE, G_MIXPOST, G_FFNPRE, G_FFNPOST = 0, 8, 16, 24
SSD_CW, SSD_CB, SSD_NG, SC_CW, FFN_CW, FFN_CB, ADA_B = 32, 128, 152, 184, 208, 340, 384
NPV = 432
NSLOT = 4
EPOCH = 4000
DEPOCH = 200
ENG_ATTR = {"pe": "tensor", "act": "scalar", "dve": "vector", "pool": "gpsimd", "sp": "sync"}


class Res:
    __slots__ = ("name", "lw", "rd")

    def __init__(self, name):
        self.name = name
        self.lw = None
        self.rd = {}


class Ins:
    __slots__ = ("eng", "meth", "kw", "key", "eidx", "waits", "signal", "seq")


class Prog:
    def __init__(self):
        self.ins = []
        self.cnt = {e: 0 for e in ENG_ATTR}
        self.nkey = 0
        self.last = {}
        self.pending = {e: [] for e in ENG_ATTR}

    def barrier(self):
        for e in ("pe", "act", "dve", "pool"):
            for e2, I in self.last.items():
                if e2 != e:
                    self.pending[e].append(I)

    def newkey(self):
        self.nkey += 1
        return "k%d" % self.nkey

    def add(self, eng, meth, R=(), W=(), key=None, **kw):
        I = Ins()
        I.eng, I.meth, I.kw, I.key = eng, meth, kw, key
        I.eidx = self.cnt[eng]
        self.cnt[eng] += 1
        I.waits, I.signal, I.seq = [], False, 0
        raw, war = {}, {}
        for r in R:
            if r.lw is not None:
                raw[id(r.lw)] = r.lw
        for w in W:
            if w.lw is not None:
                war[id(w.lw)] = w.lw
            for x in w.rd.values():
                war[id(x)] = x
        for d in raw.values():
            if d.key is None and I.key is None and d.eng == eng and eng == "pe":
                continue
            d.signal = True
            I.waits.append(d)
        for k, d in war.items():
            if k in raw or d is I:
                continue
            if d.key is None and I.key is None and d.eng == eng and eng == "pe":
                continue
            d.signal = True
            I.waits.append(d)
        if self.pending[eng]:
            for d in self.pending[eng]:
                d.signal = True
                I.waits.append(d)
            self.pending[eng] = []
        if key is None:
            self.last[eng] = I
        for r in R:
            r.rd[eng if key is None else ("dma", id(I))] = I
        for w in W:
            w.lw = I
            w.rd = {}
        self.ins.append(I)
        return I

    def mm(self, R, W, **kw):
        return self.add("pe", "matmul", R, W, **kw)

    def act(self, R, W, **kw):
        return self.add("act", "activation", R, W, **kw)

    def dma(self, q, R, W, key, **kw):
        return self.add(q, "dma_start", R, W, key=key, **kw)

    def finalize(self):
        ccnt = {e: 0 for e in ENG_ATTR}
        dcnt = {}
        semnames = set()
        for I in self.ins:
            if I.key is not None:
                dcnt[I.key] = dcnt.get(I.key, 0) + 1
                I.seq = dcnt[I.key]
                semnames.add("d_%s_%d" % (I.key, (I.seq - 1) // DEPOCH))
            elif I.signal:
                ccnt[I.eng] += 1
                I.seq = ccnt[I.eng]
        self.ccnt, self.dcnt = ccnt, dcnt
        for e, n in ccnt.items():
            for ep in range((max(n, 1) - 1) // EPOCH + 1):
                semnames.add("c_%s_%d" % (e, ep))
        return sorted(semnames)

    def replay(self, eng, handle, sems, final_waits=()):
        waited = {}
        for I in self.ins:
            if I.eng != eng:
                continue
            for d in I.waits:
                if d.key is not None:
                    src = "d_" + d.key
                    if waited.get(src, 0) >= d.seq:
                        continue
                    waited[src] = d.seq
                    handle.wait_ge(sems["%s_%d" % (src, (d.seq - 1) // DEPOCH)], 16 * ((d.seq - 1) % DEPOCH + 1))
                else:
                    if waited.get(d.eng, 0) >= d.seq:
                        continue
                    waited[d.eng] = d.seq
                    ep = (d.seq - 1) // EPOCH
                    handle.wait_ge(sems["c_%s_%d" % (d.eng, ep)], (d.seq - 1) % EPOCH + 1)
            bi = getattr(handle, I.meth)(**I.kw)
            if I.key is not None:
                bi.then_inc(sems["d_%s_%d" % (I.key, (I.seq - 1) // DEPOCH)], 16)
            elif I.signal:
                ep = (I.seq - 1) // EPOCH
                bi.then_inc(sems["c_%s_%d" % (eng, ep)], 1)
        for k in final_waits:
            if k in self.dcnt:
                n = self.dcnt[k]
                for ep in range((n - 1) // DEPOCH + 1):
                    last = min(n, (ep + 1) * DEPOCH)
                    handle.wait_ge(sems["d_%s_%d" % (k, ep)], 16 * ((last - 1) % DEPOCH + 1))


def build(NSEQ=2, NTT=8, dbg=None, NL=2, SUB2=True, STOP_SSD=False):
    nc = bass.Bass("TRN2", target_bir_lowering=False)
    P = Prog()
    xT_d = nc.dram_tensor("xT", [2, 1024, 4096], F32, kind="ExternalInput").ap()
    yT_d = nc.dram_tensor("yT", [2, 1024, 4096], F32, kind="ExternalOutput").ap()
    wd, wb = {}, {}
    for name, shp in WSHAPES.items():
        wd[name] = nc.dram_tensor(name, [2] + shp, F32, kind="ExternalInput").ap()
        wb[name] = nc.dram_tensor(name + "_bf", [2] + shp, BF16).ap()
    adaw_d = nc.dram_tensor("ada_w", [2, 1024, 6144], F32, kind="ExternalInput").ap()
    pv_d = nc.dram_tensor("pv", [128, 2, NPV], F32, kind="ExternalInput").ap()
    tokb_d = nc.dram_tensor("tokb", [128, 2, 96], F32, kind="ExternalInput").ap()
    cT_d = nc.dram_tensor("cT", [128, 8, 2], F32, kind="ExternalInput").ap()
    consts_d = nc.dram_tensor("consts", [128, 4, 128], F32, kind="ExternalInput").ap()
    dbg_out = {}

    ps_t = nc.alloc_psum_tensor("ps", [128, 8, 512], F32)
    ps = ps_t[:]
    arena = nc.alloc_sbuf_tensor("arena", [128, 211600], U8)
    aoff = [0]

    def V(shape, dt):
        n = int(np.prod(shape)) * (4 if dt == F32 else 2)
        off = (aoff[0] + 63) // 64 * 64
        aoff[0] = off + n
        ap = arena[:, off:off + n].bitcast(dt)
        if len(shape) == 2:
            ap = ap.rearrange("p (a b) -> p a b", a=shape[0])
        elif len(shape) == 3:
            ap = ap.rearrange("p (a b c) -> p a b c", a=shape[0], b=shape[1])
        return ap

    def mark():
        return aoff[0]

    def reset(m):
        aoff[0] = m

    consts = V([4, 128], F32)
    ident_f, U_f, ones_f, mask01 = consts[:, 0, :], consts[:, 1, :], consts[:, 2, :], consts[:, 3, :]
    identb = V([128], BF16)
    onesb = V([128], BF16)
    negmask = V([4, 128], BF16)
    pv = V([2, NPV], F32)
    tokb = V([2, 96], F32)
    negA = V([2, 32], F32)
    ngs = V([2, 16], F32)
    cT = V([8, 2], F32)
    cact = V([8, 2], F32)
    modT = V([2, 48, 2], F32)
    modd = V([2, 4, 16], F32)
    mtmp = V([16], F32)
    epsc = V([16], F32)
    wdt = V([2, 8, 32], BF16)
    state = V([2, 2048], F32)
    state_bf = V([2048], BF16)
    halo_ssd = V([2, 24, 4], BF16)
    halo_sc = V([2, 8, 2], BF16)
    halo_ffn = V([2, 44, 2], BF16)
    xT = V([8, T], F32)
    hT = V([8, T], BF16)
    cin = [V([516], BF16) for _ in range(3)]
    accb = [V([T], F32) for _ in range(2)]
    dgs = [V([4, 128], BF16) for _ in range(3)]
    tmpA = [V([T], F32) for _ in range(2)]
    tmpAb = [t_.bitcast(BF16)[:, 0:T] for t_ in tmpA]
    rstd = V([T], F32)
    ring_b = []
    ring_f = []
    for s in range(NSLOT):
        m0 = mark()
        ring_b.append(V([8, 512], BF16))
        reset(m0)
        ring_f.append(V([8, 256], F32))
    xs_off = (mark() + 63) // 64 * 64
    xs = V([16, T], F32)
    BT = V([4, T], BF16)
    CT = V([4, T], BF16)
    zq_off = (mark() + 63) // 64 * 64
    zq = V([4, 2048], BF16)
    r1 = (mark() + 63) // 64 * 64
    dt_all = V([4, 32], F32)
    adt_all = V([4, 32], F32)
    dtr = V([4, 32], F32)
    lndt = V([4, 32], F32)
    smalls = V([8, 32], F32)
    ssq = V([16], F32)
    xdt = V([2048], BF16)
    xdtd = V([2048], BF16)
    Btok = V([512], BF16)
    scm = V([4, 128], F32)
    Dm = [V([4, 128], F32) for _ in range(3)]
    M4 = [V([4, 128], BF16) for _ in range(3)]
    ytmp = [V([512], F32) for _ in range(2)]
    yacc = [V([512], F32) for _ in range(2)]
    DI = V([32, 128], BF16)
    ynb = [V([512], BF16) for _ in range(2)]
    junk = V([512], BF16)
    r1_ssd_end = mark()
    reset(r1)
    sc_u = V([8, T], BF16)
    gtmp = [V([T], F32) for _ in range(2)]
    mixb = V([8, T], BF16)
    obuf2 = V([8, T], F32)
    r1_sc_end = mark()
    ARENA = 211600
    assert max(r1_ssd_end, r1_sc_end) <= ARENA, (r1_ssd_end, r1_sc_end)
    print("arena use", r1_ssd_end, r1_sc_end)
    reset(xs_off)
    fbuf = V([22, T], BF16)
    reset(xs_off)
    mixacc = V([8, T], F32)
    reset(zq_off)
    ynT = V([4, 16, 128], BF16)
    reset(ARENA)

    R = {}

    def res(name):
        if name not in R:
            R[name] = Res(name)
        return R[name]

    Rb = [res("bank%d" % i) for i in range(8)]
    Rring = [res("ring%d" % i) for i in range(NSLOT)]
    bankctr = [0]

    live = set()

    def nb(hold=False):
        for _ in range(9):
            b = bankctr[0]
            bankctr[0] = (b + 1) % 8
            if b not in live:
                break
        else:
            raise RuntimeError("no free psum bank")
        if hold:
            live.add(b)
        return b

    def rel(b):
        live.discard(b)

    ringctr = [0]

    def wload(name, l, kc0, nkc, c0, ncols):
        s = ringctr[0]
        ringctr[0] = (s + 1) % NSLOT
        P.dma("sp", [res("wb_%s_%d" % (name, l))], [Rring[s]], "ring%d" % s,
              out=ring_b[s][:, 0:nkc, 0:ncols],
              in_=wb[name][l, kc0 * 128:(kc0 + nkc) * 128, c0:c0 + ncols].rearrange("(k p) n -> p k n", p=128))
        return s

    def bc_heads(ap8, n=64):
        H = ap8.shape[1]
        return ap8.unsqueeze(2).to_broadcast([128, H, n])

    def v3(ap, a):
        return ap.rearrange("p (a b) -> p a b", a=a)

    kc_ = P.newkey()
    P.dma("sp", [], [res("consts")], kc_, out=consts, in_=consts_d)
    k = P.newkey()
    P.dma("sp", [], [res("pv")], k, out=pv, in_=pv_d)
    k = P.newkey()
    P.dma("sp", [], [res("tokb")], k, out=tokb, in_=tokb_d)
    k = P.newkey()
    P.dma("sp", [], [res("cT")], k, out=cT, in_=cT_d)
    for l in range(2):
        for name in ["w_in", "w_ssd_out", "w_sc_out", "w_o", "w_up", "w_down"]:
            rows = WSHAPES[name][0]
            step = 128
            for r0 in range(0, rows, step):
                P.dma("pool", [], [res("wb_%s_%d" % (name, l))], "cast_%s_%d" % (name, l),
                      out=wb[name][l, r0:r0 + step, :], in_=wd[name][l, r0:r0 + step, :])
    P.add("dve", "tensor_copy", [res("consts")], [res("identb")], out=identb, in_=ident_f)
    P.add("dve", "tensor_copy", [res("consts")], [res("onesb")], out=onesb, in_=ones_f)
    P.add("dve", "tensor_scalar", [res("consts")], [res("negmask")], out=negmask,
          in0=mask01.unsqueeze(1).to_broadcast([128, 4, 128]), scalar1=-1.0, scalar2=30000.0,
          op0=ALU.add, op1=ALU.mult)
    P.act([res("cT")], [res("cact")], out=cact, in_=cT, func=AF.Silu)
    P.add("pool", "memset", [], [res("epsc")], ap=epsc[:, 0:1], constant=float(512 * EPS))
    P.add("pool", "memset", [], [res("epsc")], ap=epsc[:, 1:2], constant=float(1024 * EPS))
    for l in range(2):
        for j in range(24):
            s = ringctr[0]
            ringctr[0] = (s + 1) % NSLOT
            P.dma("sp", [], [Rring[s]], "ring%d" % s, out=ring_f[s],
                  in_=adaw_d[l, :, j * 256:(j + 1) * 256].rearrange("(k p) n -> p k n", p=128))
            for nt in range(2):
                ntg = j * 2 + nt
                for kc in range(8):
                    P.mm([Rring[s], res("cact")], [Rb[4 + l]], out=ps[:, 4 + l, ntg * 2:ntg * 2 + 2],
                         lhsT=ring_f[s][:, kc, nt * 128:(nt + 1) * 128], rhs=cact[:, kc, :],
                         start=(kc == 0), stop=(kc == 7))
        P.add("dve", "tensor_tensor", [Rb[4 + l], res("pv")], [res("modT")], out=modT[:, l],
              in0=v3(ps[:, 4 + l, 0:96], 48), in1=pv[:, l, ADA_B:ADA_B + 48].unsqueeze(2).to_broadcast([128, 48, 2]),
              op=ALU.add)
        for which, (m0, goff, plus1) in enumerate([(8, G_MIXPRE, True), (32, G_FFNPRE, True),
                                                   (16, G_MIXPOST, False), (40, G_FFNPOST, False)]):
            P.add("dve", "tensor_scalar", [res("modT")], [res("mtmp")], out=v3(mtmp, 8),
                  in0=modT[:, l, m0:m0 + 8, :], scalar1=(1.0 if plus1 else 0.0), scalar2=32.0,
                  op0=ALU.add, op1=ALU.mult)
            P.add("dve", "tensor_tensor", [res("mtmp"), res("pv")], [res("modd")], out=v3(modd[:, l, which, :], 8),
                  in0=v3(mtmp, 8), in1=pv[:, l, goff:goff + 8].unsqueeze(2).to_broadcast([128, 8, 2]), op=ALU.mult)
        P.add("dve", "tensor_scalar", [res("pv")], [res("ngs")], out=ngs[:, l, :], in0=pv[:, l, SSD_NG:SSD_NG + 16],
              scalar1=float(np.sqrt(512.0)), scalar2=None, op0=ALU.mult)
        P.act([res("tokb")], [res("negA")], out=negA[:, l, :], in_=tokb[:, l, 32:64], func=AF.Exp)
        P.add("dve", "tensor_scalar", [res("negA")], [res("negA")], out=negA[:, l, :], in0=negA[:, l, :],
              scalar1=-1.0, scalar2=None, op0=ALU.mult)
        k = P.newkey()
        P.dma("sp", [res("wb_w_in_%d" % l)], [res("wdt")], k, out=wdt[:, l],
              in_=wb["w_in"][l, :, C_DT:C_DT + 32].rearrange("(k p) n -> p k n", p=128))

    def G(l, which, kc, b):
        return modd[:, l, which, kc * 2 + b:kc * 2 + b + 1]

    Rx = [res("x%d" % i) for i in range(8)]
    Rh = [res("h%d" % i) for i in range(8)]
    Rsq = [res("tmpA%d" % i) for i in range(2)]

    def norm_stats(srcs, rsrcs):
        n = len(srcs)
        bk = nb()
        for i in range(n):
            P.act([rsrcs[i]], [Rsq[i % 2]], out=tmpAb[i % 2], in_=srcs[i], func=AF.Square)
            P.mm([Rsq[i % 2], res("onesb")], [Rb[bk]], out=ps[:, bk, :], lhsT=onesb, rhs=tmpAb[i % 2],
                 start=(i == 0), stop=(i == n - 1))
        assert n == 8
        P.act([Rb[bk], res("epsc")], [res("rstd")], out=rstd, in_=ps[:, bk, :], func=AF.Ln, bias=epsc[:, 1:2], scale=1.0)
        P.act([res("rstd")], [res("rstd")], out=rstd, in_=rstd, func=AF.Exp, scale=-0.5)

    def pre_norm(l, b, which, sh0):
        norm_stats([xT[:, kc, :] for kc in range(8)], Rx)
        for kc in range(8):
            P.add("dve", "tensor_tensor", [Rx[kc], res("rstd")], [Rsq[kc % 2]], out=tmpA[kc % 2], in0=xT[:, kc, :],
                  in1=rstd, op=ALU.mult)
            P.act([Rsq[kc % 2], res("modd"), res("modT")], [Rh[kc]], out=hT[:, kc, :], in_=tmpA[kc % 2],
                  func=AF.Identity, scale=G(l, which, kc, b), bias=modT[:, l, sh0 + kc, b:b + 1])

    def post_norm(l, b, which):
        norm_stats([mixacc[:, kc, :] for kc in range(8)], Rmix)
        for kc in range(8):
            P.add("dve", "tensor_tensor", [Rmix[kc], res("rstd")], [Rsq[kc % 2]], out=tmpA[kc % 2],
                  in0=mixacc[:, kc, :], in1=rstd, op=ALU.mult)
            P.add("dve", "scalar_tensor_tensor", [Rsq[kc % 2], res("modd"), Rx[kc]], [Rx[kc]], out=xT[:, kc, :],
                  in0=tmpA[kc % 2], scalar=G(l, which, kc, b), in1=xT[:, kc, :], op0=ALU.mult, op1=ALU.add)

    cinctr = [0]

    def conv_pe(l, src_fn, halo, hidx, K, w0, wstride):
        i = cinctr[0]
        cinctr[0] = (i + 1) % 3
        ci, rci = cin[i], res("cin%d" % i)
        dg, rdg = dgs[i], res("dg%d" % i)
        rh = res("halo_%s_%d" % (halo[1], l))
        P.add("pool", "tensor_tensor", [res("identb"), res("pv")], [rdg], out=dg[:, 0:K, :],
              in0=identb.unsqueeze(1).to_broadcast([128, K, 128]),
              in1=pv[:, l, w0:w0 + (K - 1) * wstride + 1:wstride].unsqueeze(2).to_broadcast([128, K, 128]), op=ALU.mult)
        src_fn(ci[:, K - 1:K - 1 + T], rci)
        P.add("pool", "tensor_copy", [rh], [rci], out=ci[:, 0:K - 1], in_=halo[0][:, l, hidx, 0:K - 1])
        P.add("pool", "tensor_copy", [rci], [rh], out=halo[0][:, l, hidx, 0:K - 1], in_=ci[:, T:T + K - 1])
        def fin():
            b2 = nb()
            for k in range(K):
                P.mm([rdg, rci], [Rb[b2]], out=ps[:, b2, :], lhsT=dg[:, k, :], rhs=ci[:, k:k + T], start=(k == 0), stop=(k == K - 1))
            return b2
        return fin

    def conv_from_bank(l, bank, halo, hidx, K, w0, wstride):
        def src(dst, rci):
            P.act([Rb[bank]], [rci], out=dst, in_=ps[:, bank, :], func=AF.Copy)
        return conv_pe(l, src, halo, hidx, K, w0, wstride)

    pend = []

    def defer(fin, post, depth=2):
        pend.append((fin, post))
        while len(pend) > depth:
            f, p = pend.pop(0)
            p(f())

    def flush():
        while pend:
            f, p = pend.pop(0)
            p(f())

    def mm8(bank, slot, j, rhs_fn, rres):
        for kc in range(8):
            P.mm([Rring[slot], rres[kc]], [Rb[bank]], out=ps[:, bank, :], lhsT=ring_b[slot][:, kc, j * 128:(j + 1) * 128],
                 rhs=rhs_fn(kc), start=(kc == 0), stop=(kc == 7))

    Rzq = [res("zq%d" % q) for q in range(4)]
    Rxs = [res("xs%d" % t) for t in range(16)]
    Rmix = Rxs[0:8]
    Rst = [[res("st%d_%d" % (l, g)) for g in range(4)] for l in range(2)]
    Rstb = [res("stb%d" % g) for g in range(4)]

    def ssd_chunk(l, q):
        qs = slice(q * 128, (q + 1) * 128)
        dt_q, adt_q = dt_all[:, q, :], adt_all[:, q, :]
        acs, negacs, dd, dst, cd, ea, dtd = [smalls[:, i, :] for i in range(7)]
        bk = nb()
        P.mm([res("adt"), res("consts")], [Rb[bk]], out=ps[:, bk, 0:32], lhsT=U_f, rhs=adt_q, start=True, stop=True)
        P.mm([res("adt"), res("consts")], [Rb[bk]], out=ps[:, bk, 32:64], lhsT=ones_f, rhs=adt_q, start=True, stop=True)
        P.add("dve", "tensor_copy", [Rb[bk]], [res("acs")], out=acs, in_=ps[:, bk, 0:32])
        P.add("dve", "tensor_tensor", [res("lndt"), res("acs")], [res("negacs")], out=negacs, in0=lndt[:, q, :],
              in1=acs, op=ALU.subtract)
        P.add("dve", "tensor_tensor", [Rb[bk], res("acs")], [res("dd")], out=dd, in0=ps[:, bk, 32:64], in1=acs,
              op=ALU.subtract)
        P.act([res("dd")], [res("dst")], out=dst, in_=dd, func=AF.Exp)
        P.act([Rb[bk]], [res("cd")], out=cd, in_=ps[:, bk, 32:64], func=AF.Exp)
        P.act([res("acs")], [res("ea")], out=ea, in_=acs, func=AF.Exp)
        P.add("dve", "tensor_tensor", [res("dt"), res("dst")], [res("dtd")], out=dtd, in0=dt_q, in1=dst, op=ALU.mult)
        P.add("pool", "memset", [], [res("ssq")], ap=ssq[:, 0:4], constant=0.0)
        bkB = nb()
        for g in range(4):
            P.mm([res("BT"), res("identb")], [Rb[bkB]], out=ps[:, bkB, g * 128:(g + 1) * 128], lhsT=BT[:, g, qs],
                 rhs=identb, start=True, stop=True)
        bkS = nb()
        for g in range(4):
            P.mm([res("BT"), res("CT")], [Rb[bkS]], out=ps[:, bkS, g * 128:(g + 1) * 128], lhsT=BT[:, g, qs],
                 rhs=CT[:, g, qs], start=True, stop=True)
        xbanks = []
        for g4 in range(4):
            bk = nb()
            xbanks.append(bk)
            for j in range(4):
                t = g4 * 4 + j
                P.add("pe", "transpose", [Rxs[t], res("consts")], [Rb[bk]], out=ps[:, bk, j * 128:(j + 1) * 128],
                      in_=xs[:, t, qs], identity=ident_f)
        P.act([Rb[bkB]], [res("Btok")], out=Btok, in_=ps[:, bkB, :], func=AF.Copy)
        P.add("dve", "tensor_tensor", [Rb[bkS], res("consts")], [res("scm")], out=scm, in0=v3(ps[:, bkS, :], 4),
              in1=mask01.unsqueeze(1).to_broadcast([128, 4, 128]), op=ALU.mult)
        for g4 in range(4):
            bk = xbanks[g4]
            gsl = slice(g4 * 512, (g4 + 1) * 512)
            hs = slice(g4 * 8, (g4 + 1) * 8)
            P.act([Rb[bk]], [res("xdt_%d" % g4)], out=xdt[:, gsl], in_=ps[:, bk, :], func=AF.Copy)
            P.add("pool", "tensor_tensor", [res("xdt_%d" % g4), res("dtd")], [res("xdtd_%d" % g4)],
                  out=v3(xdtd[:, gsl], 8), in0=v3(xdt[:, gsl], 8), in1=bc_heads(dtd[:, hs]), op=ALU.mult)

        abc_bank = {}

        def emit_abc(hq):
            bk = nb(hold=True)
            abc_bank[hq] = bk
            P.mm([res("identb"), res("negmask")], [Rb[bk]], out=ps[:, bk, :], lhsT=identb,
                 rhs=negmask.rearrange("p a b -> p (a b)"), start=True, stop=False)
            for i in range(4):
                h = hq * 4 + i
                P.mm([res("adt"), res("consts")], [Rb[bk]], out=ps[:, bk, i * 128:(i + 1) * 128],
                     lhsT=adt_q[:, h:h + 1].to_broadcast([128, 128]), rhs=U_f, start=False, stop=(i == 3))

        def emit_quad_rest(hq, ydb):
            g = hq // 2
            bk = abc_bank[hq]
            di = hq % 3
            rdm, rm4 = res("Dm%d" % di), res("M4%d" % di)
            for i in range(4):
                h = hq * 4 + i
                P.act([Rb[bk], res("negacs")], [rdm], out=Dm[di][:, i, :], in_=ps[:, bk, i * 128:(i + 1) * 128],
                      func=AF.Exp, bias=negacs[:, h:h + 1], scale=1.0)
            rel(bk)
            P.add("dve", "tensor_tensor", [rdm, res("scm")], [rm4], out=M4[di], in0=Dm[di],
                  in1=scm[:, g, :].unsqueeze(1).to_broadcast([128, 4, 128]), op=ALU.mult)
            for i in range(4):
                h = hq * 4 + i
                P.mm([rm4, res("xdt_%d" % (h // 8))], [Rb[ydb]], out=ps[:, ydb, (h % 8) * 64:(h % 8 + 1) * 64],
                     lhsT=M4[di][:, i, :], rhs=xdt[:, h * 64:(h + 1) * 64], start=True, stop=False)
                P.mm([res("DI"), res("xdt_%d" % (h // 8))], [Rb[ydb]], out=ps[:, ydb, (h % 8) * 64:(h % 8 + 1) * 64],
                     lhsT=DI[:, h, :], rhs=xdt[:, h * 64:(h + 1) * 64], start=False, stop=True)

        def tail_dve(g, ydb, yob):
            gsl = slice(g * 512, (g + 1) * 512)
            hs = slice(g * 8, (g + 1) * 8)
            yi = g % 2
            ryt, rya = res("ytmp%d" % yi), res("yacc%d" % yi)
            P.add("dve", "tensor_tensor", [Rb[yob], res("ea")], [ryt], out=v3(ytmp[yi], 8), in0=v3(ps[:, yob, :], 8),
                  in1=bc_heads(ea[:, hs]), op=ALU.mult)
            P.add("dve", "tensor_tensor", [Rb[ydb], ryt], [rya], out=yacc[yi], in0=ps[:, ydb, :], in1=ytmp[yi], op=ALU.add)
            P.add("dve", "tensor_tensor", [rya, Rzq[q]], [rya], out=yacc[yi], in0=yacc[yi], in1=zq[:, q, gsl], op=ALU.mult)

        def tail_act(g):
            yi = g % 2
            rya, ryn = res("yacc%d" % yi), res("ynb%d" % yi)
            P.act([rya], [res("junk"), res("ssq")], out=junk, in_=yacc[yi], func=AF.Square, accum_out=ssq[:, g:g + 1])
            P.act([res("ssq"), res("epsc")], [res("rsg")], out=ssq[:, 4 + g:5 + g], in_=ssq[:, g:g + 1], func=AF.Ln,
                  bias=epsc[:, 0:1], scale=1.0)
            P.act([res("rsg")], [res("rsg")], out=ssq[:, 4 + g:5 + g], in_=ssq[:, 4 + g:5 + g], func=AF.Exp, scale=-0.5)
            P.act([rya, res("rsg")], [ryn], out=ynb[yi], in_=yacc[yi], func=AF.Identity, scale=ssq[:, 4 + g:5 + g])

        def tail_pe(g):
            yi = g % 2
            ryn = res("ynb%d" % yi)
            bk = nb()
            for j in range(4):
                P.mm([ryn, res("identb")], [Rb[bk]], out=ps[:, bk, j * 128:(j + 1) * 128],
                     lhsT=ynb[yi][:, j * 128:(j + 1) * 128], rhs=identb, start=True, stop=True)
            for j in range(4):
                t = g * 4 + j
                P.act([Rb[bk], res("ngs")], [Rzq[q]], out=ynT[:, q, t, :], in_=ps[:, bk, j * 128:(j + 1) * 128],
                      func=AF.Identity, scale=ngs[:, l, t:t + 1])

        emit_abc(0)
        emit_abc(1)
        ydbs = {}

        def do_tail(g):
            yob = nb()
            gsl = slice(g * 512, (g + 1) * 512)
            P.mm([res("CT"), Rstb[g]], [Rb[yob]], out=ps[:, yob, :], lhsT=CT[:, g, qs], rhs=state_bf[:, gsl],
                 start=True, stop=True)
            tail_dve(g, ydbs[g], yob)
            rel(ydbs[g])

        for g in range(4):
            ydbs[g] = nb(hold=True)
            for hh in range(2):
                hq = g * 2 + hh
                if hq + 2 < 8:
                    emit_abc(hq + 2)
                emit_quad_rest(hq, ydbs[g])
                if hh == 0:
                    if g >= 1:
                        do_tail(g - 1)
                    if g >= 2:
                        tail_act(g - 2)
                else:
                    if g >= 2:
                        tail_pe(g - 2)
        do_tail(3)
        tail_act(2)
        tail_pe(2)
        tail_act(3)
        tail_pe(3)
        for g in range(4):
            gsl = slice(g * 512, (g + 1) * 512)
            hs = slice(g * 8, (g + 1) * 8)
            bk = nb()
            P.mm([res("Btok"), res("xdtd_%d" % g)], [Rb[bk]], out=ps[:, bk, :], lhsT=Btok[:, g * 128:(g + 1) * 128],
                 rhs=xdtd[:, gsl], start=True, stop=True)
            P.add("pool", "tensor_tensor", [Rst[l][g], res("cd")], [Rst[l][g]], out=v3(state[:, l, gsl], 8),
                  in0=v3(state[:, l, gsl], 8), in1=bc_heads(cd[:, hs]), op=ALU.mult)
            P.add("dve", "tensor_tensor", [Rb[bk], Rst[l][g]], [Rst[l][g]], out=state[:, l, gsl], in0=ps[:, bk, :],
                  in1=state[:, l, gsl], op=ALU.add)
            P.act([Rst[l][g]], [Rstb[g]], out=state_bf[:, gsl], in_=state[:, l, gsl], func=AF.Copy)

    def sub1(l, b):
        P.barrier()
        pre_norm(l, b, 0, 0)
        for zc in range(4):
            s = wload("w_in", l, 0, 8, C_Z + zc * 512, 512)
            for q in range(4):
                bk = nb()
                for kc in range(8):
                    P.mm([Rring[s], Rh[kc]], [Rb[bk]], out=ps[:, bk, :], lhsT=hT[:, kc, q * 128:(q + 1) * 128],
                         rhs=ring_b[s][:, kc, :], start=(kc == 0), stop=(kc == 7))
                P.act([Rb[bk]], [Rzq[q]], out=zq[:, q, zc * 512:(zc + 1) * 512], in_=ps[:, bk, :], func=AF.Silu)
        for xc in range(6):
            s = wload("w_in", l, 0, 8, C_XBC + xc * 512, 512)
            for j in range(4):
                t = xc * 4 + j
                bk = nb()
                mm8(bk, s, j, lambda kc: hT[:, kc, :], Rh)
                fin = conv_from_bank(l, bk, (halo_ssd, "ssd"), t, 4, SSD_CW + t, 24)

                def post(b2, t=t):
                    bias = pv[:, l, SSD_CB + t:SSD_CB + t + 1]
                    if t < 16:
                        P.act([Rb[b2], res("pv")], [Rxs[t]], out=xs[:, t, :], in_=ps[:, b2, :], func=AF.Silu, bias=bias, scale=1.0)
                    elif t < 20:
                        P.act([Rb[b2], res("pv")], [res("BT")], out=BT[:, t - 16, :], in_=ps[:, b2, :], func=AF.Silu, bias=bias, scale=1.0)
                    else:
                        P.act([Rb[b2], res("pv")], [res("CT")], out=CT[:, t - 20, :], in_=ps[:, b2, :], func=AF.Silu, bias=bias, scale=1.0)
                defer(fin, post)
        flush()
        bk = nb()
        for q in range(4):
            for kc in range(8):
                P.mm([res("wdt"), Rh[kc]], [Rb[bk]], out=ps[:, bk, q * 32:(q + 1) * 32], lhsT=hT[:, kc, q * 128:(q + 1) * 128],
                     rhs=wdt[:, l, kc, :], start=(kc == 0), stop=(kc == 7))
        P.add("dve", "tensor_tensor", [Rb[bk], res("tokb")], [res("dtr")], out=dtr, in0=v3(ps[:, bk, 0:128], 4),
              in1=tokb[:, l, 0:32].unsqueeze(1).to_broadcast([128, 4, 32]), op=ALU.add)
        P.act([res("dtr")], [res("dtr")], out=dtr, in_=dtr, func=AF.Exp)
        P.act([res("dtr")], [res("dt")], out=dt_all, in_=dtr, func=AF.Ln, bias=1.0)
        P.act([res("dt")], [res("lndt")], out=lndt, in_=dt_all, func=AF.Ln)
        P.add("dve", "tensor_tensor", [res("dt"), res("negA")], [res("adt")], out=adt_all, in0=dt_all,
              in1=negA[:, l, :].unsqueeze(1).to_broadcast([128, 4, 32]), op=ALU.mult)
        for g in range(4):
            gsl = slice(g * 512, (g + 1) * 512)
            P.act([Rst[l][g]], [Rstb[g]], out=state_bf[:, gsl], in_=state[:, l, gsl], func=AF.Copy)
        P.add("pool", "tensor_tensor", [res("identb"), res("tokb")], [res("DI")], out=DI,
              in0=identb.unsqueeze(1).to_broadcast([128, 32, 128]),
              in1=tokb[:, l, 64:96].unsqueeze(2).to_broadcast([128, 32, 128]), op=ALU.mult)
        for q in range(4):
            ssd_chunk(l, q)
        if STOP_SSD:
            return
        P.barrier()
        Rscu = [res("scu%d" % i) for i in range(8)]
        for half in range(2):
            sC = wload("w_in", l, 0, 8, C_SCC + half * 512, 512)
            sH = wload("w_in", l, 0, 8, C_SCH + half * 512, 512)
            sB = wload("w_in", l, 0, 8, C_SCB + half * 512, 512)
            for j in range(4):
                ct = half * 4 + j
                bC = nb()
                mm8(bC, sC, j, lambda kc: hT[:, kc, :], Rh)
                gi = ct % 2
                P.act([Rb[bC]], [res("gtmp%d" % gi)], out=gtmp[gi], in_=ps[:, bC, :], func=AF.Copy)
                bH = nb()
                mm8(bH, sH, j, lambda kc: hT[:, kc, :], Rh)
                def src(dst, rci, bH=bH, gi=gi):
                    P.add("dve", "tensor_tensor", [Rb[bH], res("gtmp%d" % gi)], [rci], out=dst, in0=ps[:, bH, :],
                          in1=gtmp[gi], op=ALU.mult)
                fin = conv_pe(l, src, (halo_sc, "sc"), ct, 3, SC_CW + ct, 8)
                bB = nb()
                mm8(bB, sB, j, lambda kc: hT[:, kc, :], Rh)

                def post(b2, ct=ct, bB=bB):
                    ai = ct % 2
                    P.act([Rb[b2]], [res("acc%d" % ai)], out=accb[ai], in_=ps[:, b2, :], func=AF.Copy)
                    P.add("dve", "tensor_tensor", [Rb[bB], res("acc%d" % ai)], [Rscu[ct]], out=sc_u[:, ct, :], in0=ps[:, bB, :],
                          in1=accb[ai], op=ALU.mult)
                defer(fin, post, depth=1)
        flush()
        for half in range(2):
            sO = wload("w_sc_out", l, 0, 8, half * 512, 512)
            sG = wload("w_in", l, 0, 8, C_GSC + half * 512, 512)
            for j in range(4):
                ct = half * 4 + j
                bY = nb()
                mm8(bY, sO, j, lambda kc: sc_u[:, kc, :], Rscu)
                bG = nb()
                mm8(bG, sG, j, lambda kc: hT[:, kc, :], Rh)
                gi = ct % 2
                P.act([Rb[bG]], [res("gtmp%d" % gi)], out=gtmp[gi], in_=ps[:, bG, :], func=AF.Sigmoid)
                P.add("dve", "tensor_tensor", [Rb[bY], res("gtmp%d" % gi)], [Rmix[ct]], out=mixacc[:, ct, :],
                      in0=ps[:, bY, :], in1=gtmp[gi], op=ALU.mult)
        Rmb = [res("mixb%d" % i) for i in range(8)]
        for half in range(2):
            sA = wload("w_ssd_out", l, 0, 8, half * 512, 512)
            sA2 = wload("w_ssd_out", l, 8, 8, half * 512, 512)
            sG = wload("w_in", l, 0, 8, C_GSSD + half * 512, 512)
            for j in range(4):
                ct = half * 4 + j
                bY = nb()
                for t in range(16):
                    sl = sA if t < 8 else sA2
                    P.mm([Rring[sl]] + Rzq, [Rb[bY]], out=v3(ps[:, bY, :], 4), lhsT=ring_b[sl][:, t % 8, j * 128:(j + 1) * 128],
                         rhs=ynT[:, :, t, :], start=(t == 0), stop=(t == 15))
                bG = nb()
                mm8(bG, sG, j, lambda kc: hT[:, kc, :], Rh)
                gi = ct % 2
                P.act([Rb[bG]], [res("gtmp%d" % gi)], out=gtmp[gi], in_=ps[:, bG, :], func=AF.Sigmoid)
                P.add("dve", "tensor_tensor", [Rb[bY], res("gtmp%d" % gi)], [res("gtmp%d" % gi)], out=gtmp[gi],
                      in0=ps[:, bY, :], in1=gtmp[gi], op=ALU.mult)
                P.add("pool", "tensor_tensor", [res("gtmp%d" % gi), Rmix[ct]], [Rmb[ct]], out=mixb[:, ct, :], in0=gtmp[gi],
                      in1=mixacc[:, ct, :], op=ALU.add)
        for half in range(2):
            s = wload("w_o", l, 0, 8, half * 512, 512)
            for j in range(4):
                ct = half * 4 + j
                bk = nb()
                mm8(bk, s, j, lambda kc: mixb[:, kc, :], Rmb)
                P.act([Rb[bk]], [Rmix[ct]], out=mixacc[:, ct, :], in_=ps[:, bk, :], func=AF.Copy)
        post_norm(l, b, 2)

    def sub2(l, b):
        P.barrier()
        pre_norm(l, b, 1, 24)
        Rf = [res("f%d" % i) for i in range(22)]
        blocks = [(0, 4), (4, 4), (8, 4), (12, 4), (16, 4), (20, 2)]
        for (t0, nt) in blocks:
            sg = wload("w_up", l, 0, 8, t0 * 128, nt * 128)
            sv = wload("w_up", l, 0, 8, 2816 + t0 * 128, nt * 128)
            for j in range(nt):
                i = t0 + j
                bG = nb()
                mm8(bG, sg, j, lambda kc: hT[:, kc, :], Rh)
                fing = conv_from_bank(l, bG, (halo_ffn, "ffn"), i, 3, FFN_CW + i, 44)

                def postg(b2g, i=i):
                    gi = i % 2
                    P.act([Rb[b2g], res("pv")], [res("gtmp%d" % gi)], out=gtmp[gi], in_=ps[:, b2g, :], func=AF.Silu,
                          bias=pv[:, l, FFN_CB + i:FFN_CB + i + 1], scale=1.0)
                defer(fing, postg)
                bV = nb()
                mm8(bV, sv, j, lambda kc: hT[:, kc, :], Rh)
                finv = conv_from_bank(l, bV, (halo_ffn, "ffn"), 22 + i, 3, FFN_CW + 22 + i, 44)

                def postv(b2v, i=i):
                    gi = i % 2
                    P.add("dve", "scalar_tensor_tensor", [Rb[b2v], res("pv"), res("gtmp%d" % gi)], [Rf[i]], out=fbuf[:, i, :],
                          in0=ps[:, b2v, :], scalar=pv[:, l, FFN_CB + 22 + i:FFN_CB + 22 + i + 1], in1=gtmp[gi],
                          op0=ALU.add, op1=ALU.mult)
                defer(finv, postv)
        flush()
        for half in range(2):
            ss = [wload("w_down", l, 0, 8, half * 512, 512), wload("w_down", l, 8, 8, half * 512, 512),
                  wload("w_down", l, 16, 6, half * 512, 512)]
            for j in range(4):
                ct = half * 4 + j
                bk = nb()
                for kc in range(22):
                    sl = ss[kc // 8]
                    P.mm([Rring[sl], Rf[kc]], [Rb[bk]], out=ps[:, bk, :], lhsT=ring_b[sl][:, kc % 8, j * 128:(j + 1) * 128],
                         rhs=fbuf[:, kc, :], start=(kc == 0), stop=(kc == 21))
                P.act([Rb[bk]], [res("obuf2_%d" % ct)], out=obuf2[:, ct, :], in_=ps[:, bk, :], func=AF.Copy)
        norm_stats([obuf2[:, kc, :] for kc in range(8)], [res("obuf2_%d" % kc) for kc in range(8)])
        for kc in range(8):
            P.add("dve", "tensor_tensor", [res("obuf2_%d" % kc), res("rstd")], [Rsq[kc % 2]], out=tmpA[kc % 2],
                  in0=obuf2[:, kc, :], in1=rstd, op=ALU.mult)
            P.add("dve", "scalar_tensor_tensor", [Rsq[kc % 2], res("modd"), Rx[kc]], [Rx[kc]], out=xT[:, kc, :],
                  in0=tmpA[kc % 2], scalar=G(l, 3, kc, b), in1=xT[:, kc, :], op0=ALU.mult, op1=ALU.add)

    for b in range(NSEQ):
        for l in range(2):
            for g in range(4):
                P.add("pool", "memset", [], [Rst[l][g]], ap=state[:, l, g * 512:(g + 1) * 512], constant=0.0)
            P.add("pool", "memset", [], [res("halo_ssd_%d" % l)], ap=halo_ssd[:, l], constant=0.0)
            P.add("pool", "memset", [], [res("halo_sc_%d" % l)], ap=halo_sc[:, l], constant=0.0)
            P.add("pool", "memset", [], [res("halo_ffn_%d" % l)], ap=halo_ffn[:, l], constant=0.0)
        for tt in range(NTT):
            P.dma("pool", [], Rx, "xload", out=xT,
                  in_=xT_d[b, :, tt * T:(tt + 1) * T].rearrange("(k p) t -> p k t", p=128))
            for l in range(NL):
                sub1(l, b)
                if SUB2:
                    sub2(l, b)
            P.dma("pool", Rx, [], "store", out=yT_d[b, :, tt * T:(tt + 1) * T].rearrange("(k p) t -> p k t", p=128),
                  in_=xT)

    if dbg:
        for name in dbg:
            ap = {"hT": hT, "zq": zq, "xs": xs, "BT": BT, "CT": CT, "dt": dt_all, "adt": adt_all, "ynT": ynT, "mixb": mixb,
                  "mixacc": mixacc, "xT": xT, "negmask": negmask, "identb": identb, "smalls": smalls, "ssq": ssq, "scm": scm, "Dm1": Dm[1], "M41": M4[1],
                  "yacc1": yacc[1], "ytmp1": ytmp[1], "ynb1": ynb[1], "xdt": xdt, "Btok": Btok, "modT": modT, "modd": modd, "state": state, "fbuf": fbuf, "scu": sc_u}[name]
            shp = list(ap.shape)
            dt_ = ap.dtype
            o = nc.dram_tensor("dbg_" + name, shp, dt_, kind="ExternalOutput").ap()
            P.dma("pool", list(R.values()), [], "store", out=o, in_=ap)

    semnames = P.finalize()
    import contextlib
    with contextlib.ExitStack() as es:
        sems = {n: es.enter_context(nc.semaphore(n)) for n in semnames}
        block = es.enter_context(nc.Block())

        @block.tensor
        def _(e):
            P.replay("pe", e, sems)

        @block.scalar
        def _(e):
            P.replay("act", e, sems)

        @block.vector
        def _(e):
            P.replay("dve", e, sems)

        @block.gpsimd
        def _(e):
            P.replay("pool", e, sems, final_waits=["store"])

        @block.sync
        def _(e):
            P.replay("sp", e, sems)
    return nc, P


def host_prep(inp):
    f = lambda a: np.ascontiguousarray(np.asarray(a, dtype=np.float32))
    pvs, tokbs = [], []
    for l in range(2):
        pvl = np.zeros((128, NPV), np.float32)

        def put(off, vec):
            v = np.asarray(vec, np.float32).reshape(-1, 128).T
            pvl[:, off:off + v.shape[1]] = v

        put(G_MIXPRE, inp["mix_pre_g"][l]); put(G_MIXPOST, inp["mix_post_g"][l])
        put(G_FFNPRE, inp["ffn_pre_g"][l]); put(G_FFNPOST, inp["ffn_post_g"][l])
        for k in range(4):
            put(SSD_CW + k * 24, inp["ssd_conv_w"][l][k])
        put(SSD_CB, inp["ssd_conv_b"][l]); put(SSD_NG, inp["ssd_norm_g"][l])
        for k in range(3):
            put(SC_CW + k * 8, inp["sc_conv_w"][l][k])
            put(FFN_CW + k * 44, inp["ffn_conv_w"][l][k])
        put(FFN_CB, inp["ffn_conv_b"][l]); put(ADA_B, inp["ada_b"][l])
        pvs.append(pvl)
        tb = np.zeros((128, 96), np.float32)
        tb[:, 0:32] = np.asarray(inp["ssd_dt_bias"][l])[None, :]
        tb[:, 32:64] = np.asarray(inp["ssd_a_log"][l])[None, :]
        tb[:, 64:96] = np.asarray(inp["ssd_d"][l])[None, :]
        tokbs.append(tb)
    pv = np.ascontiguousarray(np.stack(pvs, 1))
    tokb = np.ascontiguousarray(np.stack(tokbs, 1))
    consts = np.zeros((128, 4, 128), np.float32)
    consts[:, 0] = np.eye(128)
    consts[:, 1] = np.triu(np.ones((128, 128)))
    consts[:, 2] = 1.0
    consts[:, 3] = np.triu(np.ones((128, 128)))
    shared = {"pv": pv, "tokb": tokb, "consts": consts, "ada_w": f(inp["ada_w"])}
    for name in WSHAPES:
        shared[name] = f(inp[name])
    x = np.asarray(inp["x"], np.float32)
    c = np.asarray(inp["c"], np.float32)
    maps = []
    for i in range(8):
        m = dict(shared)
        m["xT"] = np.ascontiguousarray(x[2 * i:2 * i + 2].transpose(0, 2, 1))
        m["cT"] = np.ascontiguousarray(c[2 * i:2 * i + 2].reshape(2, 8, 128).transpose(2, 1, 0))
        maps.append(m)
    return maps


def kernel(**inputs):
    maps = host_prep(inputs)
    nc, _ = build()
    res = run_bass_kernel_spmd(nc, maps, core_ids=list(range(8)))
    out = np.empty((16, 4096, 1024), np.float32)
    for i in range(8):
        out[2 * i:2 * i + 2] = res.results[i]["yT"].transpose(0, 2, 1)
    return out
```

```python
import numpy as np
import concourse.bass as bass
import concourse.mybir as mybir
from concourse.bass_utils import run_bass_kernel_spmd

F32 = mybir.dt.float32
BF16 = mybir.dt.bfloat16
U8 = mybir.dt.uint8
AF = mybir.ActivationFunctionType
ALU = mybir.AluOpType

D = 1024
T = 512
EPS = 1e-6
C_Z, C_XBC, C_DT, C_SCB, C_SCC, C_SCH, C_GSSD, C_GSC = 0, 2048, 5120, 5152, 6176, 7200, 8224, 9248
WSHAPES = {"w_in": [1024, 10272], "w_ssd_out": [2048, 1024], "w_sc_out": [1024, 1024],
           "w_o": [1024, 1024], "w_up": [1024, 5632], "w_down": [2816, 1024]}
G_MIXPRE, G_MIXPOST, G_FFNPRE, G_FFNPOST = 0, 8, 16, 24
SSD_CW, SSD_CB, SSD_NG, SC_CW, FFN_CW, FFN_CB, ADA_B = 32, 128, 152, 184, 208, 340, 384
NPV = 432
NSLOT = 4
EPOCH = 4000
DEPOCH = 200
ENG_ATTR = {"pe": "tensor", "act": "scalar", "dve": "vector", "pool": "gpsimd", "sp": "sync"}


class Res:
    __slots__ = ("name", "lw", "rd")

    def __init__(self, name):
        self.name = name
        self.lw = None
        self.rd = {}


class Ins:
    __slots__ = ("eng", "meth", "kw", "key", "eidx", "waits", "signal", "seq")


class Prog:
    def __init__(self):
        self.ins = []
        self.cnt = {e: 0 for e in ENG_ATTR}
        self.nkey = 0
        self.last = {}
        self.pending = {e: [] for e in ENG_ATTR}

    def barrier(self):
        for e in ("pe", "act", "dve", "pool"):
            for e2, I in self.last.items():
                if e2 != e:
                    self.pending[e].append(I)

    def newkey(self):
        self.nkey += 1
        return "k%d" % self.nkey

    def add(self, eng, meth, R=(), W=(), key=None, **kw):
        I = Ins()
        I.eng, I.meth, I.kw, I.key = eng, meth, kw, key
        I.eidx = self.cnt[eng]
        self.cnt[eng] += 1
        I.waits, I.signal, I.seq = [], False, 0
        raw, war = {}, {}
        for r in R:
            if r.lw is not None:
                raw[id(r.lw)] = r.lw
        for w in W:
            if w.lw is not None:
                war[id(w.lw)] = w.lw
            for x in w.rd.values():
                war[id(x)] = x
        for d in raw.values():
            if d.key is None and I.key is None and d.eng == eng and eng == "pe":
                continue
            d.signal = True
            I.waits.append(d)
        for k, d in war.items():
            if k in raw or d is I:
                continue
            if d.key is None and I.key is None and d.eng == eng and eng == "pe":
                continue
            d.signal = True
            I.waits.append(d)
        if self.pending[eng]:
            for d in self.pending[eng]:
                d.signal = True
                I.waits.append(d)
            self.pending[eng] = []
        if key is None:
            self.last[eng] = I
        for r in R:
            r.rd[eng if key is None else ("dma", id(I))] = I
        for w in W:
            w.lw = I
            w.rd = {}
        self.ins.append(I)
        return I

    def mm(self, R, W, **kw):
        return self.add("pe", "matmul", R, W, **kw)

    def act(self, R, W, **kw):
        return self.add("act", "activation", R, W, **kw)

    def dma(self, q, R, W, key, **kw):
        return self.add(q, "dma_start", R, W, key=key, **kw)

    def finalize(self):
        ccnt = {e: 0 for e in ENG_ATTR}
        dcnt = {}
        semnames = set()
        for I in self.ins:
            if I.key is not None:
                dcnt[I.key] = dcnt.get(I.key, 0) + 1
                I.seq = dcnt[I.key]
                semnames.add("d_%s_%d" % (I.key, (I.seq - 1) // DEPOCH))
            elif I.signal:
                ccnt[I.eng] += 1
                I.seq = ccnt[I.eng]
        self.ccnt, self.dcnt = ccnt, dcnt
        for e, n in ccnt.items():
            for ep in range((max(n, 1) - 1) // EPOCH + 1):
                semnames.add("c_%s_%d" % (e, ep))
        return sorted(semnames)

    def replay(self, eng, handle, sems, final_waits=()):
        waited = {}
        for I in self.ins:
            if I.eng != eng:
                continue
            for d in I.waits:
                if d.key is not None:
                    src = "d_" + d.key
                    if waited.get(src, 0) >= d.seq:
                        continue
                    waited[src] = d.seq
                    handle.wait_ge(sems["%s_%d" % (src, (d.seq - 1) // DEPOCH)], 16 * ((d.seq - 1) % DEPOCH + 1))
                else:
                    if waited.get(d.eng, 0) >= d.seq:
                        continue
                    waited[d.eng] = d.seq
                    ep = (d.seq - 1) // EPOCH
                    handle.wait_ge(sems["c_%s_%d" % (d.eng, ep)], (d.seq - 1) % EPOCH + 1)
            bi = getattr(handle, I.meth)(**I.kw)
            if I.key is not None:
                bi.then_inc(sems["d_%s_%d" % (I.key, (I.seq - 1) // DEPOCH)], 16)
            elif I.signal:
                ep = (I.seq - 1) // EPOCH
                bi.then_inc(sems["c_%s_%d" % (eng, ep)], 1)
        for k in final_waits:
            if k in self.dcnt:
                n = self.dcnt[k]
                for ep in range((n - 1) // DEPOCH + 1):
                    last = min(n, (ep + 1) * DEPOCH)
                    handle.wait_ge(sems["d_%s_%d" % (k, ep)], 16 * ((last - 1) % DEPOCH + 1))


def build(NSEQ=2, NTT=8, dbg=None, NL=2, SUB2=True, STOP_SSD=False):
    nc = bass.Bass("TRN2", target_bir_lowering=False)
    P = Prog()
    xT_d = nc.dram_tensor("xT", [2, 1024, 4096], F32, kind="ExternalInput").ap()
    yT_d = nc.dram_tensor("yT", [2, 1024, 4096], F32, kind="ExternalOutput").ap()
    wd, wb = {}, {}
    for name, shp in WSHAPES.items():
        wd[name] = nc.dram_tensor(name, [2] + shp, F32, kind="ExternalInput").ap()
        wb[name] = nc.dram_tensor(name + "_bf", [2] + shp, BF16).ap()
    adaw_d = nc.dram_tensor("ada_w", [2, 1024, 6144], F32, kind="ExternalInput").ap()
    pv_d = nc.dram_tensor("pv", [128, 2, NPV], F32, kind="ExternalInput").ap()
    tokb_d = nc.dram_tensor("tokb", [128, 2, 96], F32, kind="ExternalInput").ap()
    cT_d = nc.dram_tensor("cT", [128, 8, 2], F32, kind="ExternalInput").ap()
    consts_d = nc.dram_tensor("consts", [128, 4, 128], F32, kind="ExternalInput").ap()
    dbg_out = {}

    ps_t = nc.alloc_psum_tensor("ps", [128, 8, 512], F32)
    ps = ps_t[:]
    arena = nc.alloc_sbuf_tensor("arena", [128, 211600], U8)
    aoff = [0]

    def V(shape, dt):
        n = int(np.prod(shape)) * (4 if dt == F32 else 2)
        off = (aoff[0] + 63) // 64 * 64
        aoff[0] = off + n
        ap = arena[:, off:off + n].bitcast(dt)
        if len(shape) == 2:
            ap = ap.rearrange("p (a b) -> p a b", a=shape[0])
        elif len(shape) == 3:
            ap = ap.rearrange("p (a b c) -> p a b c", a=shape[0], b=shape[1])
        return ap

    def mark():
        return aoff[0]

    def reset(m):
        aoff[0] = m

    consts = V([4, 128], F32)
    ident_f, U_f, ones_f, mask01 = consts[:, 0, :], consts[:, 1, :], consts[:, 2, :], consts[:, 3, :]
    identb = V([128], BF16)
    onesb = V([128], BF16)
    negmask = V([4, 128], BF16)
    pv = V([2, NPV], F32)
    tokb = V([2, 96], F32)
    negA = V([2, 32], F32)
    ngs = V([2, 16], F32)
    cT = V([8, 2], F32)
    cact = V([8, 2], F32)
    modT = V([2, 48, 2], F32)
    modd = V([2, 4, 16], F32)
    mtmp = V([16], F32)
    epsc = V([16], F32)
    wdt = V([2, 8, 32], BF16)
    state = V([2, 2048], F32)
    state_bf = V([2048], BF16)
    halo_ssd = V([2, 24, 4], BF16)
    halo_sc = V([2, 8, 2], BF16)
    halo_ffn = V([2, 44, 2], BF16)
    xT = V([8, T], F32)
    hT = V([8, T], BF16)
    cin = [V([516], BF16) for _ in range(3)]
    accb = [V([T], F32) for _ in range(2)]
    dgs = [V([4, 128], BF16) for _ in range(3)]
    tmpA = [V([T], F32) for _ in range(2)]
    tmpAb = [t_.bitcast(BF16)[:, 0:T] for t_ in tmpA]
    rstd = V([T], F32)
    ring_b = []
    ring_f = []
    for s in range(NSLOT):
        m0 = mark()
        ring_b.append(V([8, 512], BF16))
        reset(m0)
        ring_f.append(V([8, 256], F32))
    xs_off = (mark() + 63) // 64 * 64
    xs = V([16, T], F32)
    BT = V([4, T], BF16)
    CT = V([4, T], BF16)
    zq_off = (mark() + 63) // 64 * 64
    zq = V([4, 2048], BF16)
    r1 = (mark() + 63) // 64 * 64
    dt_all = V([4, 32], F32)
    adt_all = V([4, 32], F32)
    dtr = V([4, 32], F32)
    lndt = V([4, 32], F32)
    smalls = V([8, 32], F32)
    ssq = V([16], F32)
    xdt = V([2048], BF16)
    xdtd = V([2048], BF16)
    Btok = V([512], BF16)
    scm = V([4, 128], F32)
    Dm = [V([4, 128], F32) for _ in range(3)]
    M4 = [V([4, 128], BF16) for _ in range(3)]
    ytmp = [V([512], F32) for _ in range(2)]
    yacc = [V([512], F32) for _ in range(2)]
    DI = V([32, 128], BF16)
    ynb = [V([512], BF16) for _ in range(2)]
    junk = V([512], BF16)
    r1_ssd_end = mark()
    reset(r1)
    sc_u = V([8, T], BF16)
    gtmp = [V([T], F32) for _ in range(2)]
    mixb = V([8, T], BF16)
    obuf2 = V([8, T], F32)
    r1_sc_end = mark()
    ARENA = 211600
    assert max(r1_ssd_end, r1_sc_end) <= ARENA, (r1_ssd_end, r1_sc_end)
    print("arena use", r1_ssd_end, r1_sc_end)
    reset(xs_off)
    fbuf = V([22, T], BF16)
    reset(xs_off)
    mixacc = V([8, T], F32)
    reset(zq_off)
    ynT = V([4, 16, 128], BF16)
    reset(ARENA)

    R = {}

    def res(name):
        if name not in R:
            R[name] = Res(name)
        return R[name]

    Rb = [res("bank%d" % i) for i in range(8)]
    Rring = [res("ring%d" % i) for i in range(NSLOT)]
    bankctr = [0]

    live = set()

    def nb(hold=False):
        for _ in range(9):
            b = bankctr[0]
            bankctr[0] = (b + 1) % 8
            if b not in live:
                break
        else:
            raise RuntimeError("no free psum bank")
        if hold:
            live.add(b)
        return b

    def rel(b):
        live.discard(b)

    ringctr = [0]

    def wload(name, l, kc0, nkc, c0, ncols):
        s = ringctr[0]
        ringctr[0] = (s + 1) % NSLOT
        tag = ("_g%d" % win_grp(c0)) if name == "w_in" else ""
        P.dma("sp", [res("wb_%s_%d%s" % (name, l, tag))], [Rring[s]], "ring%d" % s,
              out=ring_b[s][:, 0:nkc, 0:ncols],
              in_=wb[name][l, kc0 * 128:(kc0 + nkc) * 128, c0:c0 + ncols].rearrange("(k p) n -> p k n", p=128))
        return s

    def bc_heads(ap8, n=64):
        H = ap8.shape[1]
        return ap8.unsqueeze(2).to_broadcast([128, H, n])

    def v3(ap, a):
        return ap.rearrange("p (a b) -> p a b", a=a)

    kc_ = P.newkey()
    P.dma("sp", [], [res("consts")], kc_, out=consts, in_=consts_d)
    k = P.newkey()
    P.dma("sp", [], [res("pv")], k, out=pv, in_=pv_d)
    k = P.newkey()
    P.dma("sp", [], [res("tokb")], k, out=tokb, in_=tokb_d)
    k = P.newkey()
    P.dma("sp", [], [res("cT")], k, out=cT, in_=cT_d)
    WIN_GRP = [0, 2048, 5120, 8224, 10272]

    def win_grp(c0):
        for gi in range(4):
            if WIN_GRP[gi] <= c0 < WIN_GRP[gi + 1]:
                return gi
        raise ValueError(c0)

    def cast(name, l, c0=None, c1=None, tag=""):
        rows = WSHAPES[name][0]
        for r0 in range(0, rows, 128):
            if c0 is None:
                o, i_ = wb[name][l, r0:r0 + 128, :], wd[name][l, r0:r0 + 128, :]
            else:
                o, i_ = wb[name][l, r0:r0 + 128, c0:c1], wd[name][l, r0:r0 + 128, c0:c1]
            P.dma("pool", [], [res("wb_%s_%d%s" % (name, l, tag))], "cast_%s_%d%s" % (name, l, tag), out=o, in_=i_)

    for l in range(2):
        for gi in (0, 1, 2):
            cast("w_in", l, WIN_GRP[gi], WIN_GRP[gi + 1], "_g%d" % gi)
        cast("w_sc_out", l)
        cast("w_in", l, WIN_GRP[3], WIN_GRP[4], "_g3")
        for name in ["w_ssd_out", "w_o", "w_up", "w_down"]:
            cast(name, l)
    P.add("dve", "tensor_copy", [res("consts")], [res("identb")], out=identb, in_=ident_f)
    P.add("dve", "tensor_copy", [res("consts")], [res("onesb")], out=onesb, in_=ones_f)
    P.add("dve", "tensor_scalar", [res("consts")], [res("negmask")], out=negmask,
          in0=mask01.unsqueeze(1).to_broadcast([128, 4, 128]), scalar1=-1.0, scalar2=30000.0,
          op0=ALU.add, op1=ALU.mult)
    P.act([res("cT")], [res("cact")], out=cact, in_=cT, func=AF.Silu)
    P.add("pool", "memset", [], [res("epsc")], ap=epsc[:, 0:1], constant=float(512 * EPS))
    P.add("pool", "memset", [], [res("epsc")], ap=epsc[:, 1:2], constant=float(1024 * EPS))
    def prologue_layer(l):
        for j in range(24):
            s = ringctr[0]
            ringctr[0] = (s + 1) % NSLOT
            P.dma("sp", [], [Rring[s]], "ring%d" % s, out=ring_f[s],
                  in_=adaw_d[l, :, j * 256:(j + 1) * 256].rearrange("(k p) n -> p k n", p=128))
            for nt in range(2):
                ntg = j * 2 + nt
                for kc in range(8):
                    P.mm([Rring[s], res("cact")], [Rb[4 + l]], out=ps[:, 4 + l, ntg * 2:ntg * 2 + 2],
                         lhsT=ring_f[s][:, kc, nt * 128:(nt + 1) * 128], rhs=cact[:, kc, :],
                         start=(kc == 0), stop=(kc == 7))
        P.add("dve", "tensor_tensor", [Rb[4 + l], res("pv")], [res("modT")], out=modT[:, l],
              in0=v3(ps[:, 4 + l, 0:96], 48), in1=pv[:, l, ADA_B:ADA_B + 48].unsqueeze(2).to_broadcast([128, 48, 2]),
              op=ALU.add)
        for which, (m0, goff, plus1) in enumerate([(8, G_MIXPRE, True), (32, G_FFNPRE, True),
                                                   (16, G_MIXPOST, False), (40, G_FFNPOST, False)]):
            P.add("dve", "tensor_scalar", [res("modT")], [res("mtmp")], out=v3(mtmp, 8),
                  in0=modT[:, l, m0:m0 + 8, :], scalar1=(1.0 if plus1 else 0.0), scalar2=32.0,
                  op0=ALU.add, op1=ALU.mult)
            P.add("dve", "tensor_tensor", [res("mtmp"), res("pv")], [res("modd")], out=v3(modd[:, l, which, :], 8),
                  in0=v3(mtmp, 8), in1=pv[:, l, goff:goff + 8].unsqueeze(2).to_broadcast([128, 8, 2]), op=ALU.mult)
        P.add("dve", "tensor_scalar", [res("pv")], [res("ngs")], out=ngs[:, l, :], in0=pv[:, l, SSD_NG:SSD_NG + 16],
              scalar1=float(np.sqrt(512.0)), scalar2=None, op0=ALU.mult)
        P.act([res("tokb")], [res("negA")], out=negA[:, l, :], in_=tokb[:, l, 32:64], func=AF.Exp)
        P.add("dve", "tensor_scalar", [res("negA")], [res("negA")], out=negA[:, l, :], in0=negA[:, l, :],
              scalar1=-1.0, scalar2=None, op0=ALU.mult)
        k = P.newkey()
        P.dma("sp", [res("wb_w_in_%d_g2" % l)], [res("wdt")], k, out=wdt[:, l],
              in_=wb["w_in"][l, :, C_DT:C_DT + 32].rearrange("(k p) n -> p k n", p=128))

    prologue_layer(0)
    prologue_layer(1)

    def G(l, which, kc, b):
        return modd[:, l, which, kc * 2 + b:kc * 2 + b + 1]

    Rx = [res("x%d" % i) for i in range(8)]
    Rh = [res("h%d" % i) for i in range(8)]
    Rsq = [res("tmpA%d" % i) for i in range(2)]

    def norm_stats(srcs, rsrcs):
        n = len(srcs)
        bk = nb()
        for i in range(n):
            P.act([rsrcs[i]], [Rsq[i % 2]], out=tmpAb[i % 2], in_=srcs[i], func=AF.Square)
            P.mm([Rsq[i % 2], res("onesb")], [Rb[bk]], out=ps[:, bk, :], lhsT=onesb, rhs=tmpAb[i % 2],
                 start=(i == 0), stop=(i == n - 1))
        assert n == 8
        P.act([Rb[bk], res("epsc")], [res("rstd")], out=rstd, in_=ps[:, bk, :], func=AF.Ln, bias=epsc[:, 1:2], scale=1.0)
        P.act([res("rstd")], [res("rstd")], out=rstd, in_=rstd, func=AF.Exp, scale=-0.5)

    def pre_norm(l, b, which, sh0):
        norm_stats([xT[:, kc, :] for kc in range(8)], Rx)
        for kc in range(8):
            P.add("dve", "tensor_tensor", [Rx[kc], res("rstd")], [Rsq[kc % 2]], out=tmpA[kc % 2], in0=xT[:, kc, :],
                  in1=rstd, op=ALU.mult)
            P.act([Rsq[kc % 2], res("modd"), res("modT")], [Rh[kc]], out=hT[:, kc, :], in_=tmpA[kc % 2],
                  func=AF.Identity, scale=G(l, which, kc, b), bias=modT[:, l, sh0 + kc, b:b + 1])

    def post_norm(l, b, which):
        norm_stats([mixacc[:, kc, :] for kc in range(8)], Rmix)
        for kc in range(8):
            P.add("dve", "tensor_tensor", [Rmix[kc], res("rstd")], [Rsq[kc % 2]], out=tmpA[kc % 2],
                  in0=mixacc[:, kc, :], in1=rstd, op=ALU.mult)
            P.add("dve", "scalar_tensor_tensor", [Rsq[kc % 2], res("modd"), Rx[kc]], [Rx[kc]], out=xT[:, kc, :],
                  in0=tmpA[kc % 2], scalar=G(l, which, kc, b), in1=xT[:, kc, :], op0=ALU.mult, op1=ALU.add)

    cinctr = [0]

    def conv_pe(l, src_fn, halo, hidx, K, w0, wstride):
        i = cinctr[0]
        cinctr[0] = (i + 1) % 3
        ci, rci = cin[i], res("cin%d" % i)
        dg, rdg = dgs[i], res("dg%d" % i)
        rh = res("halo_%s_%d" % (halo[1], l))
        P.add("pool", "tensor_tensor", [res("identb"), res("pv")], [rdg], out=dg[:, 0:K, :],
              in0=identb.unsqueeze(1).to_broadcast([128, K, 128]),
              in1=pv[:, l, w0:w0 + (K - 1) * wstride + 1:wstride].unsqueeze(2).to_broadcast([128, K, 128]), op=ALU.mult)
        src_fn(ci[:, K - 1:K - 1 + T], rci)
        P.add("pool", "tensor_copy", [rh], [rci], out=ci[:, 0:K - 1], in_=halo[0][:, l, hidx, 0:K - 1])
        P.add("pool", "tensor_copy", [rci], [rh], out=halo[0][:, l, hidx, 0:K - 1], in_=ci[:, T:T + K - 1])
        def fin():
            b2 = nb()
            for k in range(K):
                P.mm([rdg, rci], [Rb[b2]], out=ps[:, b2, :], lhsT=dg[:, k, :], rhs=ci[:, k:k + T], start=(k == 0), stop=(k == K - 1))
            return b2
        return fin

    def conv_from_bank(l, bank, halo, hidx, K, w0, wstride):
        def src(dst, rci):
            P.act([Rb[bank]], [rci], out=dst, in_=ps[:, bank, :], func=AF.Copy)
        return conv_pe(l, src, halo, hidx, K, w0, wstride)

    pend = []

    def defer(fin, post, depth=2):
        pend.append((fin, post))
        while len(pend) > depth:
            f, p = pend.pop(0)
            p(f())

    def flush():
        while pend:
            f, p = pend.pop(0)
            p(f())

    def mm8(bank, slot, j, rhs_fn, rres):
        for kc in range(8):
            P.mm([Rring[slot], rres[kc]], [Rb[bank]], out=ps[:, bank, :], lhsT=ring_b[slot][:, kc, j * 128:(j + 1) * 128],
                 rhs=rhs_fn(kc), start=(kc == 0), stop=(kc == 7))

    Rzq = [res("zq%d" % q) for q in range(4)]
    Rxs = [res("xs%d" % t) for t in range(16)]
    Rmix = Rxs[0:8]
    Rst = [[res("st%d_%d" % (l, g)) for g in range(4)] for l in range(2)]
    Rstb = [res("stb%d" % g) for g in range(4)]

    def ssd_chunk(l, q):
        qs = slice(q * 128, (q + 1) * 128)
        dt_q, adt_q = dt_all[:, q, :], adt_all[:, q, :]
        acs, negacs, dd, dst, cd, ea, dtd = [smalls[:, i, :] for i in range(7)]
        bk = nb()
        P.mm([res("adt"), res("consts")], [Rb[bk]], out=ps[:, bk, 0:32], lhsT=U_f, rhs=adt_q, start=True, stop=True)
        P.mm([res("adt"), res("consts")], [Rb[bk]], out=ps[:, bk, 32:64], lhsT=ones_f, rhs=adt_q, start=True, stop=True)
        P.add("dve", "tensor_copy", [Rb[bk]], [res("acs")], out=acs, in_=ps[:, bk, 0:32])
        P.add("dve", "tensor_tensor", [res("lndt"), res("acs")], [res("negacs")], out=negacs, in0=lndt[:, q, :],
              in1=acs, op=ALU.subtract)
        P.add("dve", "tensor_tensor", [Rb[bk], res("acs")], [res("dd")], out=dd, in0=ps[:, bk, 32:64], in1=acs,
              op=ALU.subtract)
        P.act([res("dd")], [res("dst")], out=dst, in_=dd, func=AF.Exp)
        P.act([Rb[bk]], [res("cd")], out=cd, in_=ps[:, bk, 32:64], func=AF.Exp)
        P.act([res("acs")], [res("ea")], out=ea, in_=acs, func=AF.Exp)
        P.add("dve", "tensor_tensor", [res("dt"), res("dst")], [res("dtd")], out=dtd, in0=dt_q, in1=dst, op=ALU.mult)
        P.add("pool", "memset", [], [res("ssq")], ap=ssq[:, 0:4], constant=0.0)
        bkB = nb()
        for g in range(4):
            P.mm([res("BT"), res("identb")], [Rb[bkB]], out=ps[:, bkB, g * 128:(g + 1) * 128], lhsT=BT[:, g, qs],
                 rhs=identb, start=True, stop=True)
        bkS = nb()
        for g in range(4):
            P.mm([res("BT"), res("CT")], [Rb[bkS]], out=ps[:, bkS, g * 128:(g + 1) * 128], lhsT=BT[:, g, qs],
                 rhs=CT[:, g, qs], start=True, stop=True)
        xbanks = []
        for g4 in range(4):
            bk = nb()
            xbanks.append(bk)
            for j in range(4):
                t = g4 * 4 + j
                P.add("pe", "transpose", [Rxs[t], res("consts")], [Rb[bk]], out=ps[:, bk, j * 128:(j + 1) * 128],
                      in_=xs[:, t, qs], identity=ident_f)
        P.act([Rb[bkB]], [res("Btok")], out=Btok, in_=ps[:, bkB, :], func=AF.Copy)
        P.add("dve", "tensor_tensor", [Rb[bkS], res("consts")], [res("scm")], out=scm, in0=v3(ps[:, bkS, :], 4),
              in1=mask01.unsqueeze(1).to_broadcast([128, 4, 128]), op=ALU.mult)
        for g4 in range(4):
            bk = xbanks[g4]
            gsl = slice(g4 * 512, (g4 + 1) * 512)
            hs = slice(g4 * 8, (g4 + 1) * 8)
            P.act([Rb[bk]], [res("xdt_%d" % g4)], out=xdt[:, gsl], in_=ps[:, bk, :], func=AF.Copy)
            P.add("pool", "tensor_tensor", [res("xdt_%d" % g4), res("dtd")], [res("xdtd_%d" % g4)],
                  out=v3(xdtd[:, gsl], 8), in0=v3(xdt[:, gsl], 8), in1=bc_heads(dtd[:, hs]), op=ALU.mult)

        abc_bank = {}

        def emit_abc(hq):
            bk = nb(hold=True)
            abc_bank[hq] = bk
            P.mm([res("identb"), res("negmask")], [Rb[bk]], out=ps[:, bk, :], lhsT=identb,
                 rhs=negmask.rearrange("p a b -> p (a b)"), start=True, stop=False)
            for i in range(4):
                h = hq * 4 + i
                P.mm([res("adt"), res("consts")], [Rb[bk]], out=ps[:, bk, i * 128:(i + 1) * 128],
                     lhsT=adt_q[:, h:h + 1].to_broadcast([128, 128]), rhs=U_f, start=False, stop=(i == 3))

        def emit_quad_rest(hq, ydb):
            g = hq // 2
            bk = abc_bank[hq]
            di = hq % 3
            rdm, rm4 = res("Dm%d" % di), res("M4%d" % di)
            for i in range(4):
                h = hq * 4 + i
                P.act([Rb[bk], res("negacs")], [rdm], out=Dm[di][:, i, :], in_=ps[:, bk, i * 128:(i + 1) * 128],
                      func=AF.Exp, bias=negacs[:, h:h + 1], scale=1.0)
            rel(bk)
            P.add("dve", "tensor_tensor", [rdm, res("scm")], [rm4], out=M4[di], in0=Dm[di],
                  in1=scm[:, g, :].unsqueeze(1).to_broadcast([128, 4, 128]), op=ALU.mult)
            for i in range(4):
                h = hq * 4 + i
                P.mm([rm4, res("xdt_%d" % (h // 8))], [Rb[ydb]], out=ps[:, ydb, (h % 8) * 64:(h % 8 + 1) * 64],
                     lhsT=M4[di][:, i, :], rhs=xdt[:, h * 64:(h + 1) * 64], start=True, stop=False)
                P.mm([res("DI"), res("xdt_%d" % (h // 8))], [Rb[ydb]], out=ps[:, ydb, (h % 8) * 64:(h % 8 + 1) * 64],
                     lhsT=DI[:, h, :], rhs=xdt[:, h * 64:(h + 1) * 64], start=False, stop=True)

        def tail_dve(g, ydb, yob):
            gsl = slice(g * 512, (g + 1) * 512)
            hs = slice(g * 8, (g + 1) * 8)
            yi = g % 2
            ryt, rya = res("ytmp%d" % yi), res("yacc%d" % yi)
            P.add("dve", "tensor_tensor", [Rb[yob], res("ea")], [ryt], out=v3(ytmp[yi], 8), in0=v3(ps[:, yob, :], 8),
                  in1=bc_heads(ea[:, hs]), op=ALU.mult)
            P.add("dve", "tensor_tensor", [Rb[ydb], ryt], [rya], out=yacc[yi], in0=ps[:, ydb, :], in1=ytmp[yi], op=ALU.add)
            P.add("dve", "tensor_tensor", [rya, Rzq[q]], [rya], out=yacc[yi], in0=yacc[yi], in1=zq[:, q, gsl], op=ALU.mult)

        def tail_act(g):
            yi = g % 2
            rya, ryn = res("yacc%d" % yi), res("ynb%d" % yi)
            P.act([rya], [res("junk"), res("ssq")], out=junk, in_=yacc[yi], func=AF.Square, accum_out=ssq[:, g:g + 1])
            P.act([res("ssq"), res("epsc")], [res("rsg")], out=ssq[:, 4 + g:5 + g], in_=ssq[:, g:g + 1], func=AF.Ln,
                  bias=epsc[:, 0:1], scale=1.0)
            P.act([res("rsg")], [res("rsg")], out=ssq[:, 4 + g:5 + g], in_=ssq[:, 4 + g:5 + g], func=AF.Exp, scale=-0.5)
            P.act([rya, res("rsg")], [ryn], out=ynb[yi], in_=yacc[yi], func=AF.Identity, scale=ssq[:, 4 + g:5 + g])

        def tail_pe(g):
            yi = g % 2
            ryn = res("ynb%d" % yi)
            bk = nb()
            for j in range(4):
                P.mm([ryn, res("identb")], [Rb[bk]], out=ps[:, bk, j * 128:(j + 1) * 128],
                     lhsT=ynb[yi][:, j * 128:(j + 1) * 128], rhs=identb, start=True, stop=True)
            for j in range(4):
                t = g * 4 + j
                P.act([Rb[bk], res("ngs")], [Rzq[q]], out=ynT[:, q, t, :], in_=ps[:, bk, j * 128:(j + 1) * 128],
                      func=AF.Identity, scale=ngs[:, l, t:t + 1])

        emit_abc(0)
        emit_abc(1)
        ydbs = {}

        def do_tail(g):
            yob = nb()
            gsl = slice(g * 512, (g + 1) * 512)
            P.mm([res("CT"), Rstb[g]], [Rb[yob]], out=ps[:, yob, :], lhsT=CT[:, g, qs], rhs=state_bf[:, gsl],
                 start=True, stop=True)
            tail_dve(g, ydbs[g], yob)
            rel(ydbs[g])

        for g in range(4):
            ydbs[g] = nb(hold=True)
            for hh in range(2):
                hq = g * 2 + hh
                if hq + 2 < 8:
                    emit_abc(hq + 2)
                emit_quad_rest(hq, ydbs[g])
                if hh == 0:
                    if g >= 1:
                        do_tail(g - 1)
                    if g >= 2:
                        tail_act(g - 2)
                else:
                    if g >= 2:
                        tail_pe(g - 2)
        do_tail(3)
        tail_act(2)
        tail_pe(2)
        tail_act(3)
        tail_pe(3)
        for g in range(4):
            gsl = slice(g * 512, (g + 1) * 512)
            hs = slice(g * 8, (g + 1) * 8)
            bk = nb()
            P.mm([res("Btok"), res("xdtd_%d" % g)], [Rb[bk]], out=ps[:, bk, :], lhsT=Btok[:, g * 128:(g + 1) * 128],
                 rhs=xdtd[:, gsl], start=True, stop=True)
            P.add("pool", "tensor_tensor", [Rst[l][g], res("cd")], [Rst[l][g]], out=v3(state[:, l, gsl], 8),
                  in0=v3(state[:, l, gsl], 8), in1=bc_heads(cd[:, hs]), op=ALU.mult)
            P.add("dve", "tensor_tensor", [Rb[bk], Rst[l][g]], [Rst[l][g]], out=state[:, l, gsl], in0=ps[:, bk, :],
                  in1=state[:, l, gsl], op=ALU.add)
            P.act([Rst[l][g]], [Rstb[g]], out=state_bf[:, gsl], in_=state[:, l, gsl], func=AF.Copy)

    def sub1(l, b):
        P.barrier()
        pre_norm(l, b, 0, 0)
        for zc in range(4):
            s = wload("w_in", l, 0, 8, C_Z + zc * 512, 512)
            for q in range(4):
                bk = nb()
                for kc in range(8):
                    P.mm([Rring[s], Rh[kc]], [Rb[bk]], out=ps[:, bk, :], lhsT=hT[:, kc, q * 128:(q + 1) * 128],
                         rhs=ring_b[s][:, kc, :], start=(kc == 0), stop=(kc == 7))
                P.act([Rb[bk]], [Rzq[q]], out=zq[:, q, zc * 512:(zc + 1) * 512], in_=ps[:, bk, :], func=AF.Silu)
        for xc in range(6):
            s = wload("w_in", l, 0, 8, C_XBC + xc * 512, 512)
            for j in range(4):
                t = xc * 4 + j
                bk = nb()
                mm8(bk, s, j, lambda kc: hT[:, kc, :], Rh)
                fin = conv_from_bank(l, bk, (halo_ssd, "ssd"), t, 4, SSD_CW + t, 24)

                def post(b2, t=t):
                    bias = pv[:, l, SSD_CB + t:SSD_CB + t + 1]
                    if t < 16:
                        P.act([Rb[b2], res("pv")], [Rxs[t]], out=xs[:, t, :], in_=ps[:, b2, :], func=AF.Silu, bias=bias, scale=1.0)
                    elif t < 20:
                        P.act([Rb[b2], res("pv")], [res("BT")], out=BT[:, t - 16, :], in_=ps[:, b2, :], func=AF.Silu, bias=bias, scale=1.0)
                    else:
                        P.act([Rb[b2], res("pv")], [res("CT")], out=CT[:, t - 20, :], in_=ps[:, b2, :], func=AF.Silu, bias=bias, scale=1.0)
                defer(fin, post)
        flush()
        bk = nb()
        for q in range(4):
            for kc in range(8):
                P.mm([res("wdt"), Rh[kc]], [Rb[bk]], out=ps[:, bk, q * 32:(q + 1) * 32], lhsT=hT[:, kc, q * 128:(q + 1) * 128],
                     rhs=wdt[:, l, kc, :], start=(kc == 0), stop=(kc == 7))
        P.add("dve", "tensor_tensor", [Rb[bk], res("tokb")], [res("dtr")], out=dtr, in0=v3(ps[:, bk, 0:128], 4),
              in1=tokb[:, l, 0:32].unsqueeze(1).to_broadcast([128, 4, 32]), op=ALU.add)
        P.act([res("dtr")], [res("dtr")], out=dtr, in_=dtr, func=AF.Exp)
        P.act([res("dtr")], [res("dt")], out=dt_all, in_=dtr, func=AF.Ln, bias=1.0)
        P.act([res("dt")], [res("lndt")], out=lndt, in_=dt_all, func=AF.Ln)
        P.add("dve", "tensor_tensor", [res("dt"), res("negA")], [res("adt")], out=adt_all, in0=dt_all,
              in1=negA[:, l, :].unsqueeze(1).to_broadcast([128, 4, 32]), op=ALU.mult)
        for g in range(4):
            gsl = slice(g * 512, (g + 1) * 512)
            P.act([Rst[l][g]], [Rstb[g]], out=state_bf[:, gsl], in_=state[:, l, gsl], func=AF.Copy)
        P.add("pool", "tensor_tensor", [res("identb"), res("tokb")], [res("DI")], out=DI,
              in0=identb.unsqueeze(1).to_broadcast([128, 32, 128]),
              in1=tokb[:, l, 64:96].unsqueeze(2).to_broadcast([128, 32, 128]), op=ALU.mult)
        for q in range(4):
            ssd_chunk(l, q)
        if STOP_SSD:
            return
        P.barrier()
        Rscu = [res("scu%d" % i) for i in range(8)]
        for half in range(2):
            sC = wload("w_in", l, 0, 8, C_SCC + half * 512, 512)
            sH = wload("w_in", l, 0, 8, C_SCH + half * 512, 512)
            sB = wload("w_in", l, 0, 8, C_SCB + half * 512, 512)
            for j in range(4):
                ct = half * 4 + j
                bC = nb()
                mm8(bC, sC, j, lambda kc: hT[:, kc, :], Rh)
                gi = ct % 2
                P.act([Rb[bC]], [res("gtmp%d" % gi)], out=gtmp[gi], in_=ps[:, bC, :], func=AF.Copy)
                bH = nb()
                mm8(bH, sH, j, lambda kc: hT[:, kc, :], Rh)
                def src(dst, rci, bH=bH, gi=gi):
                    P.add("dve", "tensor_tensor", [Rb[bH], res("gtmp%d" % gi)], [rci], out=dst, in0=ps[:, bH, :],
                          in1=gtmp[gi], op=ALU.mult)
                fin = conv_pe(l, src, (halo_sc, "sc"), ct, 3, SC_CW + ct, 8)
                bB = nb()
                mm8(bB, sB, j, lambda kc: hT[:, kc, :], Rh)

                def post(b2, ct=ct, bB=bB):
                    ai = ct % 2
                    P.act([Rb[b2]], [res("acc%d" % ai)], out=accb[ai], in_=ps[:, b2, :], func=AF.Copy)
                    P.add("dve", "tensor_tensor", [Rb[bB], res("acc%d" % ai)], [Rscu[ct]], out=sc_u[:, ct, :], in0=ps[:, bB, :],
                          in1=accb[ai], op=ALU.mult)
                defer(fin, post, depth=1)
        flush()
        for half in range(2):
            sO = wload("w_sc_out", l, 0, 8, half * 512, 512)
            sG = wload("w_in", l, 0, 8, C_GSC + half * 512, 512)
            for j in range(4):
                ct = half * 4 + j
                bY = nb()
                mm8(bY, sO, j, lambda kc: sc_u[:, kc, :], Rscu)
                bG = nb()
                mm8(bG, sG, j, lambda kc: hT[:, kc, :], Rh)
                gi = ct % 2
                P.act([Rb[bG]], [res("gtmp%d" % gi)], out=gtmp[gi], in_=ps[:, bG, :], func=AF.Sigmoid)
                P.add("dve", "tensor_tensor", [Rb[bY], res("gtmp%d" % gi)], [Rmix[ct]], out=mixacc[:, ct, :],
                      in0=ps[:, bY, :], in1=gtmp[gi], op=ALU.mult)
        Rmb = [res("mixb%d" % i) for i in range(8)]
        for half in range(2):
            sA = wload("w_ssd_out", l, 0, 8, half * 512, 512)
            sA2 = wload("w_ssd_out", l, 8, 8, half * 512, 512)
            sG = wload("w_in", l, 0, 8, C_GSSD + half * 512, 512)
            for j in range(4):
                ct = half * 4 + j
                bY = nb()
                for t in range(16):
                    sl = sA if t < 8 else sA2
                    P.mm([Rring[sl]] + Rzq, [Rb[bY]], out=v3(ps[:, bY, :], 4), lhsT=ring_b[sl][:, t % 8, j * 128:(j + 1) * 128],
                         rhs=ynT[:, :, t, :], start=(t == 0), stop=(t == 15))
                bG = nb()
                mm8(bG, sG, j, lambda kc: hT[:, kc, :], Rh)
                gi = ct % 2
                P.act([Rb[bG]], [res("gtmp%d" % gi)], out=gtmp[gi], in_=ps[:, bG, :], func=AF.Sigmoid)
                P.add("dve", "tensor_tensor", [Rb[bY], res("gtmp%d" % gi)], [res("gtmp%d" % gi)], out=gtmp[gi],
                      in0=ps[:, bY, :], in1=gtmp[gi], op=ALU.mult)
                P.add("pool", "tensor_tensor", [res("gtmp%d" % gi), Rmix[ct]], [Rmb[ct]], out=mixb[:, ct, :], in0=gtmp[gi],
                      in1=mixacc[:, ct, :], op=ALU.add)
        for half in range(2):
            s = wload("w_o", l, 0, 8, half * 512, 512)
            for j in range(4):
                ct = half * 4 + j
                bk = nb()
                mm8(bk, s, j, lambda kc: mixb[:, kc, :], Rmb)
                P.act([Rb[bk]], [Rmix[ct]], out=mixacc[:, ct, :], in_=ps[:, bk, :], func=AF.Copy)
        post_norm(l, b, 2)

    def sub2(l, b):
        P.barrier()
        pre_norm(l, b, 1, 24)
        Rf = [res("f%d" % i) for i in range(22)]
        blocks = [(0, 4), (4, 4), (8, 4), (12, 4), (16, 4), (20, 2)]
        for (t0, nt) in blocks:
            sg = wload("w_up", l, 0, 8, t0 * 128, nt * 128)
            sv = wload("w_up", l, 0, 8, 2816 + t0 * 128, nt * 128)
            for j in range(nt):
                i = t0 + j
                bG = nb()
                mm8(bG, sg, j, lambda kc: hT[:, kc, :], Rh)
                fing = conv_from_bank(l, bG, (halo_ffn, "ffn"), i, 3, FFN_CW + i, 44)

                def postg(b2g, i=i):
                    gi = i % 2
                    P.act([Rb[b2g], res("pv")], [res("gtmp%d" % gi)], out=gtmp[gi], in_=ps[:, b2g, :], func=AF.Silu,
                          bias=pv[:, l, FFN_CB + i:FFN_CB + i + 1], scale=1.0)
                defer(fing, postg)
                bV = nb()
                mm8(bV, sv, j, lambda kc: hT[:, kc, :], Rh)
                finv = conv_from_bank(l, bV, (halo_ffn, "ffn"), 22 + i, 3, FFN_CW + 22 + i, 44)

                def postv(b2v, i=i):
                    gi = i % 2
                    P.add("dve", "scalar_tensor_tensor", [Rb[b2v], res("pv"), res("gtmp%d" % gi)], [Rf[i]], out=fbuf[:, i, :],
                          in0=ps[:, b2v, :], scalar=pv[:, l, FFN_CB + 22 + i:FFN_CB + 22 + i + 1], in1=gtmp[gi],
                          op0=ALU.add, op1=ALU.mult)
                defer(finv, postv)
        flush()
        for half in range(2):
            ss = [wload("w_down", l, 0, 8, half * 512, 512), wload("w_down", l, 8, 8, half * 512, 512),
                  wload("w_down", l, 16, 6, half * 512, 512)]
            for j in range(4):
                ct = half * 4 + j
                bk = nb()
                for kc in range(22):
                    sl = ss[kc // 8]
                    P.mm([Rring[sl], Rf[kc]], [Rb[bk]], out=ps[:, bk, :], lhsT=ring_b[sl][:, kc % 8, j * 128:(j + 1) * 128],
                         rhs=fbuf[:, kc, :], start=(kc == 0), stop=(kc == 21))
                P.act([Rb[bk]], [res("obuf2_%d" % ct)], out=obuf2[:, ct, :], in_=ps[:, bk, :], func=AF.Copy)
        norm_stats([obuf2[:, kc, :] for kc in range(8)], [res("obuf2_%d" % kc) for kc in range(8)])
        for kc in range(8):
            P.add("dve", "tensor_tensor", [res("obuf2_%d" % kc), res("rstd")], [Rsq[kc % 2]], out=tmpA[kc % 2],
                  in0=obuf2[:, kc, :], in1=rstd, op=ALU.mult)
            P.add("dve", "scalar_tensor_tensor", [Rsq[kc % 2], res("modd"), Rx[kc]], [Rx[kc]], out=xT[:, kc, :],
                  in0=tmpA[kc % 2], scalar=G(l, 3, kc, b), in1=xT[:, kc, :], op0=ALU.mult, op1=ALU.add)

    for b in range(NSEQ):
        for l in range(2):
            for g in range(4):
                P.add("pool", "memset", [], [Rst[l][g]], ap=state[:, l, g * 512:(g + 1) * 512], constant=0.0)
            P.add("pool", "memset", [], [res("halo_ssd_%d" % l)], ap=halo_ssd[:, l], constant=0.0)
            P.add("pool", "memset", [], [res("halo_sc_%d" % l)], ap=halo_sc[:, l], constant=0.0)
            P.add("pool", "memset", [], [res("halo_ffn_%d" % l)], ap=halo_ffn[:, l], constant=0.0)
        for tt in range(NTT):
            P.dma("pool", [], Rx, "xload", out=xT,
                  in_=xT_d[b, :, tt * T:(tt + 1) * T].rearrange("(k p) t -> p k t", p=128))
            for l in range(NL):
                sub1(l, b)
                if SUB2:
                    sub2(l, b)
            P.dma("pool", Rx, [], "store", out=yT_d[b, :, tt * T:(tt + 1) * T].rearrange("(k p) t -> p k t", p=128),
                  in_=xT)

    if dbg:
        for name in dbg:
            ap = {"hT": hT, "zq": zq, "xs": xs, "BT": BT, "CT": CT, "dt": dt_all, "adt": adt_all, "ynT": ynT, "mixb": mixb,
                  "mixacc": mixacc, "xT": xT, "negmask": negmask, "identb": identb, "smalls": smalls, "ssq": ssq, "scm": scm, "Dm1": Dm[1], "M41": M4[1],
                  "yacc1": yacc[1], "ytmp1": ytmp[1], "ynb1": ynb[1], "xdt": xdt, "Btok": Btok, "modT": modT, "modd": modd, "state": state, "fbuf": fbuf, "scu": sc_u}[name]
            shp = list(ap.shape)
            dt_ = ap.dtype
            o = nc.dram_tensor("dbg_" + name, shp, dt_, kind="ExternalOutput").ap()
            P.dma("pool", list(R.values()), [], "store", out=o, in_=ap)

    semnames = P.finalize()
    import contextlib
    with contextlib.ExitStack() as es:
        sems = {n: es.enter_context(nc.semaphore(n)) for n in semnames}
        block = es.enter_context(nc.Block())

        @block.tensor
        def _(e):
            P.replay("pe", e, sems)

        @block.scalar
        def _(e):
            P.replay("act", e, sems)

        @block.vector
        def _(e):
            P.replay("dve", e, sems)

        @block.gpsimd
        def _(e):
            P.replay("pool", e, sems, final_waits=["store"])

        @block.sync
        def _(e):
            P.replay("sp", e, sems)
    return nc, P


def host_prep(inp):
    f = lambda a: np.ascontiguousarray(np.asarray(a, dtype=np.float32))
    pvs, tokbs = [], []
    for l in range(2):
        pvl = np.zeros((128, NPV), np.float32)

        def put(off, vec):
            v = np.asarray(vec, np.float32).reshape(-1, 128).T
            pvl[:, off:off + v.shape[1]] = v

        put(G_MIXPRE, inp["mix_pre_g"][l]); put(G_MIXPOST, inp["mix_post_g"][l])
        put(G_FFNPRE, inp["ffn_pre_g"][l]); put(G_FFNPOST, inp["ffn_post_g"][l])
        for k in range(4):
            put(SSD_CW + k * 24, inp["ssd_conv_w"][l][k])
        put(SSD_CB, inp["ssd_conv_b"][l]); put(SSD_NG, inp["ssd_norm_g"][l])
        for k in range(3):
            put(SC_CW + k * 8, inp["sc_conv_w"][l][k])
            put(FFN_CW + k * 44, inp["ffn_conv_w"][l][k])
        put(FFN_CB, inp["ffn_conv_b"][l]); put(ADA_B, inp["ada_b"][l])
        pvs.append(pvl)
        tb = np.zeros((128, 96), np.float32)
        tb[:, 0:32] = np.asarray(inp["ssd_dt_bias"][l])[None, :]
        tb[:, 32:64] = np.asarray(inp["ssd_a_log"][l])[None, :]
        tb[:, 64:96] = np.asarray(inp["ssd_d"][l])[None, :]
        tokbs.append(tb)
    pv = np.ascontiguousarray(np.stack(pvs, 1))
    tokb = np.ascontiguousarray(np.stack(tokbs, 1))
    consts = np.zeros((128, 4, 128), np.float32)
    consts[:, 0] = np.eye(128)
    consts[:, 1] = np.triu(np.ones((128, 128)))
    consts[:, 2] = 1.0
    consts[:, 3] = np.triu(np.ones((128, 128)))
    shared = {"pv": pv, "tokb": tokb, "consts": consts, "ada_w": f(inp["ada_w"])}
    for name in WSHAPES:
        shared[name] = f(inp[name])
    x = np.asarray(inp["x"], np.float32)
    c = np.asarray(inp["c"], np.float32)
    maps = []
    for i in range(8):
        m = dict(shared)
        m["xT"] = np.ascontiguousarray(x[2 * i:2 * i + 2].transpose(0, 2, 1))
        m["cT"] = np.ascontiguousarray(c[2 * i:2 * i + 2].reshape(2, 8, 128).transpose(2, 1, 0))
        maps.append(m)
    return maps


def kernel(**inputs):
    maps = host_prep(inputs)
    nc, _ = build()
    res = run_bass_kernel_spmd(nc, maps, core_ids=list(range(8)))
    out = np.empty((16, 4096, 1024), np.float32)
    for i in range(8):
        out[2 * i:2 * i + 2] = res.results[i]["yT"].transpose(0, 2, 1)
    return out
```

```python
import numpy as np
import concourse.bass as bass
import concourse.mybir as mybir
from concourse.bass_utils import run_bass_kernel_spmd

F32 = mybir.dt.float32
BF16 = mybir.dt.bfloat16
U8 = mybir.dt.uint8
AF = mybir.ActivationFunctionType
ALU = mybir.AluOpType

D = 1024
T = 512
EPS = 1e-6
C_Z, C_XBC, C_DT, C_SCB, C_SCC, C_SCH, C_GSSD, C_GSC = 0, 2048, 5120, 5152, 6176, 7200, 8224, 9248
WSHAPES = {"w_in": [1024, 10272], "w_ssd_out": [2048, 1024], "w_sc_out": [1024, 1024],
           "w_o": [1024, 1024], "w_up": [1024, 5632], "w_down": [2816, 1024]}
G_MIXPRE, G_MIXPOST, G_FFNPRE, G_FFNPOST = 0, 8, 16, 24
SSD_CW, SSD_CB, SSD_NG, SC_CW, FFN_CW, FFN_CB, ADA_B = 32, 128, 152, 184, 208, 340, 384
NPV = 432
NSLOT = 4
EPOCH = 4000
DEPOCH = 200
ENG_ATTR = {"pe": "tensor", "act": "scalar", "dve": "vector", "pool": "gpsimd", "sp": "sync"}


class Res:
    __slots__ = ("name", "lw", "rd")

    def __init__(self, name):
        self.name = name
        self.lw = None
        self.rd = {}


class Ins:
    __slots__ = ("eng", "meth", "kw", "key", "eidx", "waits", "signal", "seq")


class Prog:
    def __init__(self):
        self.ins = []
        self.cnt = {e: 0 for e in ENG_ATTR}
        self.nkey = 0
        self.last = {}
        self.pending = {e: [] for e in ENG_ATTR}

    def barrier(self):
        for e in ("pe", "act", "dve", "pool"):
            for e2, I in self.last.items():
                if e2 != e:
                    self.pending[e].append(I)

    def newkey(self):
        self.nkey += 1
        return "k%d" % self.nkey

    def add(self, eng, meth, R=(), W=(), key=None, **kw):
        I = Ins()
        I.eng, I.meth, I.kw, I.key = eng, meth, kw, key
        I.eidx = self.cnt[eng]
        self.cnt[eng] += 1
        I.waits, I.signal, I.seq = [], False, 0
        raw, war = {}, {}
        for r in R:
            if r.lw is not None:
                raw[id(r.lw)] = r.lw
        for w in W:
            if w.lw is not None:
                war[id(w.lw)] = w.lw
            for x in w.rd.values():
                war[id(x)] = x
        for d in raw.values():
            if d.key is None and I.key is None and d.eng == eng and eng == "pe":
                continue
            d.signal = True
            I.waits.append(d)
        for k, d in war.items():
            if k in raw or d is I:
                continue
            if d.key is None and I.key is None and d.eng == eng and eng == "pe":
                continue
            d.signal = True
            I.waits.append(d)
        if self.pending[eng]:
            for d in self.pending[eng]:
                d.signal = True
                I.waits.append(d)
            self.pending[eng] = []
        if key is None:
            self.last[eng] = I
        for r in R:
            r.rd[eng if key is None else ("dma", id(I))] = I
        for w in W:
            w.lw = I
            w.rd = {}
        self.ins.append(I)
        return I

    def mm(self, R, W, **kw):
        return self.add("pe", "matmul", R, W, **kw)

    def act(self, R, W, **kw):
        return self.add("act", "activation", R, W, **kw)

    def dma(self, q, R, W, key, **kw):
        return self.add(q, "dma_start", R, W, key=key, **kw)

    def finalize(self):
        ccnt = {e: 0 for e in ENG_ATTR}
        dcnt = {}
        semnames = set()
        for I in self.ins:
            if I.key is not None:
                dcnt[I.key] = dcnt.get(I.key, 0) + 1
                I.seq = dcnt[I.key]
                semnames.add("d_%s_%d" % (I.key, (I.seq - 1) // DEPOCH))
            elif I.signal:
                ccnt[I.eng] += 1
                I.seq = ccnt[I.eng]
        self.ccnt, self.dcnt = ccnt, dcnt
        for e, n in ccnt.items():
            for ep in range((max(n, 1) - 1) // EPOCH + 1):
                semnames.add("c_%s_%d" % (e, ep))
        return sorted(semnames)

    def replay(self, eng, handle, sems, final_waits=()):
        waited = {}
        for I in self.ins:
            if I.eng != eng:
                continue
            for d in I.waits:
                if d.key is not None:
                    src = "d_" + d.key
                    if waited.get(src, 0) >= d.seq:
                        continue
                    waited[src] = d.seq
                    handle.wait_ge(sems["%s_%d" % (src, (d.seq - 1) // DEPOCH)], 16 * ((d.seq - 1) % DEPOCH + 1))
                else:
                    if waited.get(d.eng, 0) >= d.seq:
                        continue
                    waited[d.eng] = d.seq
                    ep = (d.seq - 1) // EPOCH
                    handle.wait_ge(sems["c_%s_%d" % (d.eng, ep)], (d.seq - 1) % EPOCH + 1)
            bi = getattr(handle, I.meth)(**I.kw)
            if I.key is not None:
                bi.then_inc(sems["d_%s_%d" % (I.key, (I.seq - 1) // DEPOCH)], 16)
            elif I.signal:
                ep = (I.seq - 1) // EPOCH
                bi.then_inc(sems["c_%s_%d" % (eng, ep)], 1)
        for k in final_waits:
            if k in self.dcnt:
                n = self.dcnt[k]
                for ep in range((n - 1) // DEPOCH + 1):
                    last = min(n, (ep + 1) * DEPOCH)
                    handle.wait_ge(sems["d_%s_%d" % (k, ep)], 16 * ((last - 1) % DEPOCH + 1))


def build(NSEQ=2, NTT=8, dbg=None, NL=2, SUB2=True, STOP_SSD=False):
    nc = bass.Bass("TRN2", target_bir_lowering=False)
    P = Prog()
    xT_d = nc.dram_tensor("xT", [2, 1024, 4096], F32, kind="ExternalInput").ap()
    yT_d = nc.dram_tensor("yT", [2, 1024, 4096], F32, kind="ExternalOutput").ap()
    wd, wb = {}, {}
    for name, shp in WSHAPES.items():
        wd[name] = nc.dram_tensor(name, [2] + shp, F32, kind="ExternalInput").ap()
        wb[name] = nc.dram_tensor(name + "_bf", [2] + shp, BF16).ap()
    adaw_d = nc.dram_tensor("ada_w", [2, 1024, 6144], F32, kind="ExternalInput").ap()
    pv_d = nc.dram_tensor("pv", [128, 2, NPV], F32, kind="ExternalInput").ap()
    tokb_d = nc.dram_tensor("tokb", [128, 2, 96], F32, kind="ExternalInput").ap()
    cT_d = nc.dram_tensor("cT", [128, 8, 2], F32, kind="ExternalInput").ap()
    consts_d = nc.dram_tensor("consts", [128, 4, 128], F32, kind="ExternalInput").ap()
    dbg_out = {}

    ps_t = nc.alloc_psum_tensor("ps", [128, 8, 512], F32)
    ps = ps_t[:]
    arena = nc.alloc_sbuf_tensor("arena", [128, 211600], U8)
    aoff = [0]

    def V(shape, dt):
        n = int(np.prod(shape)) * (4 if dt == F32 else 2)
        off = (aoff[0] + 63) // 64 * 64
        aoff[0] = off + n
        ap = arena[:, off:off + n].bitcast(dt)
        if len(shape) == 2:
            ap = ap.rearrange("p (a b) -> p a b", a=shape[0])
        elif len(shape) == 3:
            ap = ap.rearrange("p (a b c) -> p a b c", a=shape[0], b=shape[1])
        return ap

    def mark():
        return aoff[0]

    def reset(m):
        aoff[0] = m

    consts = V([4, 128], F32)
    ident_f, U_f, ones_f, mask01 = consts[:, 0, :], consts[:, 1, :], consts[:, 2, :], consts[:, 3, :]
    identb = V([128], BF16)
    onesb = V([128], BF16)
    negmask = V([4, 128], BF16)
    pv = V([2, NPV], F32)
    tokb = V([2, 96], F32)
    negA = V([2, 32], F32)
    ngs = V([2, 16], F32)
    cT = V([8, 2], F32)
    cact = V([8, 2], F32)
    modT = V([2, 48, 2], F32)
    modd = V([2, 4, 16], F32)
    mtmp = V([16], F32)
    epsc = V([16], F32)
    wdt = V([2, 8, 32], BF16)
    state = V([2, 2048], F32)
    state_bf = V([2048], BF16)
    halo_ssd = V([2, 24, 4], BF16)
    halo_sc = V([2, 8, 2], BF16)
    halo_ffn = V([2, 44, 2], BF16)
    xT = V([8, T], F32)
    hT = V([8, T], BF16)
    cin = [V([516], BF16) for _ in range(3)]
    accb = [V([T], F32) for _ in range(2)]
    dgs = [V([4, 128], BF16) for _ in range(3)]
    tmpA = [V([T], F32) for _ in range(2)]
    tmpAb = [t_.bitcast(BF16)[:, 0:T] for t_ in tmpA]
    rstd = V([T], F32)
    ring_b = []
    ring_f = []
    for s in range(NSLOT):
        m0 = mark()
        ring_b.append(V([8, 512], BF16))
        reset(m0)
        ring_f.append(V([8, 256], F32))
    xs_off = (mark() + 63) // 64 * 64
    xs = V([16, T], F32)
    BT = V([4, T], BF16)
    CT = V([4, T], BF16)
    zq_off = (mark() + 63) // 64 * 64
    zq = V([4, 2048], BF16)
    r1 = (mark() + 63) // 64 * 64
    dt_all = V([4, 32], F32)
    adt_all = V([4, 32], F32)
    dtr = V([4, 32], F32)
    lndt = V([4, 32], F32)
    smalls = V([8, 32], F32)
    ssq = V([16], F32)
    xdt = V([2048], BF16)
    xdtd = V([2048], BF16)
    Btok = V([512], BF16)
    scm = V([4, 128], F32)
    Dm = [V([4, 128], F32) for _ in range(3)]
    M4 = [V([4, 128], BF16) for _ in range(3)]
    ytmp = [V([512], F32) for _ in range(2)]
    yacc = [V([512], F32) for _ in range(2)]
    DI = V([32, 128], BF16)
    ynb = [V([512], BF16) for _ in range(2)]
    junk = V([512], BF16)
    r1_ssd_end = mark()
    reset(r1)
    sc_u = V([8, T], BF16)
    gtmp = [V([T], F32) for _ in range(2)]
    mixb = V([8, T], BF16)
    obuf2 = V([8, T], F32)
    r1_sc_end = mark()
    ARENA = 211600
    assert max(r1_ssd_end, r1_sc_end) <= ARENA, (r1_ssd_end, r1_sc_end)
    print("arena use", r1_ssd_end, r1_sc_end)
    reset(xs_off)
    fbuf = V([22, T], BF16)
    reset(xs_off)
    mixacc = V([8, T], F32)
    reset(zq_off)
    ynT = V([4, 16, 128], BF16)
    reset(ARENA)

    R = {}

    def res(name):
        if name not in R:
            R[name] = Res(name)
        return R[name]

    Rb = [res("bank%d" % i) for i in range(8)]
    Rring = [res("ring%d" % i) for i in range(NSLOT)]
    bankctr = [0]

    live = set()

    def nb(hold=False):
        for _ in range(9):
            b = bankctr[0]
            bankctr[0] = (b + 1) % 8
            if b not in live:
                break
        else:
            raise RuntimeError("no free psum bank")
        if hold:
            live.add(b)
        return b

    def rel(b):
        live.discard(b)

    ringctr = [0]

    def wload(name, l, kc0, nkc, c0, ncols):
        s = ringctr[0]
        ringctr[0] = (s + 1) % NSLOT
        tag = ("_g%d" % win_grp(c0)) if name == "w_in" else ""
        P.dma("sp", [res("wb_%s_%d%s" % (name, l, tag))], [Rring[s]], "ring%d" % s,
              out=ring_b[s][:, 0:nkc, 0:ncols],
              in_=wb[name][l, kc0 * 128:(kc0 + nkc) * 128, c0:c0 + ncols].rearrange("(k p) n -> p k n", p=128))
        return s

    def bc_heads(ap8, n=64):
        H = ap8.shape[1]
        return ap8.unsqueeze(2).to_broadcast([128, H, n])

    def v3(ap, a):
        return ap.rearrange("p (a b) -> p a b", a=a)

    kc_ = P.newkey()
    P.dma("sp", [], [res("consts")], kc_, out=consts, in_=consts_d)
    k = P.newkey()
    P.dma("sp", [], [res("pv")], k, out=pv, in_=pv_d)
    k = P.newkey()
    P.dma("sp", [], [res("tokb")], k, out=tokb, in_=tokb_d)
    k = P.newkey()
    P.dma("sp", [], [res("cT")], k, out=cT, in_=cT_d)
    WIN_GRP = [0, 2048, 5120, 8224, 10272]

    def win_grp(c0):
        for gi in range(4):
            if WIN_GRP[gi] <= c0 < WIN_GRP[gi + 1]:
                return gi
        raise ValueError(c0)

    def cast(name, l, c0=None, c1=None, tag=""):
        rows = WSHAPES[name][0]
        for r0 in range(0, rows, 256):
            r1_ = min(rows, r0 + 256)
            if c0 is None:
                o, i_ = wb[name][l, r0:r1_, :], wd[name][l, r0:r1_, :]
            else:
                o, i_ = wb[name][l, r0:r1_, c0:c1], wd[name][l, r0:r1_, c0:c1]
            P.dma("pool", [], [res("wb_%s_%d%s" % (name, l, tag))], "cast_%s_%d%s" % (name, l, tag), out=o, in_=i_)

    for l in range(2):
        for gi in (0, 1, 2):
            cast("w_in", l, WIN_GRP[gi], WIN_GRP[gi + 1], "_g%d" % gi)
        cast("w_sc_out", l)
        cast("w_in", l, WIN_GRP[3], WIN_GRP[4], "_g3")
        for name in ["w_ssd_out", "w_o", "w_up", "w_down"]:
            cast(name, l)
    P.add("dve", "tensor_copy", [res("consts")], [res("identb")], out=identb, in_=ident_f)
    P.add("dve", "tensor_copy", [res("consts")], [res("onesb")], out=onesb, in_=ones_f)
    P.add("dve", "tensor_scalar", [res("consts")], [res("negmask")], out=negmask,
          in0=mask01.unsqueeze(1).to_broadcast([128, 4, 128]), scalar1=-1.0, scalar2=30000.0,
          op0=ALU.add, op1=ALU.mult)
    P.act([res("cT")], [res("cact")], out=cact, in_=cT, func=AF.Silu)
    P.add("pool", "memset", [], [res("epsc")], ap=epsc[:, 0:1], constant=float(512 * EPS))
    P.add("pool", "memset", [], [res("epsc")], ap=epsc[:, 1:2], constant=float(1024 * EPS))
    def prologue_layer(l):
        for j in range(24):
            s = ringctr[0]
            ringctr[0] = (s + 1) % NSLOT
            P.dma("sp", [], [Rring[s]], "ring%d" % s, out=ring_f[s],
                  in_=adaw_d[l, :, j * 256:(j + 1) * 256].rearrange("(k p) n -> p k n", p=128))
            for nt in range(2):
                ntg = j * 2 + nt
                for kc in range(8):
                    P.mm([Rring[s], res("cact")], [Rb[4 + l]], out=ps[:, 4 + l, ntg * 2:ntg * 2 + 2],
                         lhsT=ring_f[s][:, kc, nt * 128:(nt + 1) * 128], rhs=cact[:, kc, :],
                         start=(kc == 0), stop=(kc == 7))
        P.add("dve", "tensor_tensor", [Rb[4 + l], res("pv")], [res("modT")], out=modT[:, l],
              in0=v3(ps[:, 4 + l, 0:96], 48), in1=pv[:, l, ADA_B:ADA_B + 48].unsqueeze(2).to_broadcast([128, 48, 2]),
              op=ALU.add)
        for which, (m0, goff, plus1) in enumerate([(8, G_MIXPRE, True), (32, G_FFNPRE, True),
                                                   (16, G_MIXPOST, False), (40, G_FFNPOST, False)]):
            P.add("dve", "tensor_scalar", [res("modT")], [res("mtmp")], out=v3(mtmp, 8),
                  in0=modT[:, l, m0:m0 + 8, :], scalar1=(1.0 if plus1 else 0.0), scalar2=32.0,
                  op0=ALU.add, op1=ALU.mult)
            P.add("dve", "tensor_tensor", [res("mtmp"), res("pv")], [res("modd")], out=v3(modd[:, l, which, :], 8),
                  in0=v3(mtmp, 8), in1=pv[:, l, goff:goff + 8].unsqueeze(2).to_broadcast([128, 8, 2]), op=ALU.mult)
        P.add("dve", "tensor_scalar", [res("pv")], [res("ngs")], out=ngs[:, l, :], in0=pv[:, l, SSD_NG:SSD_NG + 16],
              scalar1=float(np.sqrt(512.0)), scalar2=None, op0=ALU.mult)
        P.act([res("tokb")], [res("negA")], out=negA[:, l, :], in_=tokb[:, l, 32:64], func=AF.Exp)
        P.add("dve", "tensor_scalar", [res("negA")], [res("negA")], out=negA[:, l, :], in0=negA[:, l, :],
              scalar1=-1.0, scalar2=None, op0=ALU.mult)
        k = P.newkey()
        P.dma("sp", [res("wb_w_in_%d_g2" % l)], [res("wdt")], k, out=wdt[:, l],
              in_=wb["w_in"][l, :, C_DT:C_DT + 32].rearrange("(k p) n -> p k n", p=128))

    prologue_layer(0)
    prologue_layer(1)

    def G(l, which, kc, b):
        return modd[:, l, which, kc * 2 + b:kc * 2 + b + 1]

    Rx = [res("x%d" % i) for i in range(8)]
    Rh = [res("h%d" % i) for i in range(8)]
    Rsq = [res("tmpA%d" % i) for i in range(2)]

    def norm_stats(srcs, rsrcs):
        n = len(srcs)
        bk = nb()
        for i in range(n):
            P.act([rsrcs[i]], [Rsq[i % 2]], out=tmpAb[i % 2], in_=srcs[i], func=AF.Square)
            P.mm([Rsq[i % 2], res("onesb")], [Rb[bk]], out=ps[:, bk, :], lhsT=onesb, rhs=tmpAb[i % 2],
                 start=(i == 0), stop=(i == n - 1))
        assert n == 8
        P.act([Rb[bk], res("epsc")], [res("rstd")], out=rstd, in_=ps[:, bk, :], func=AF.Ln, bias=epsc[:, 1:2], scale=1.0)
        P.act([res("rstd")], [res("rstd")], out=rstd, in_=rstd, func=AF.Exp, scale=-0.5)

    def pre_norm(l, b, which, sh0):
        norm_stats([xT[:, kc, :] for kc in range(8)], Rx)
        for kc in range(8):
            P.add("dve", "tensor_tensor", [Rx[kc], res("rstd")], [Rsq[kc % 2]], out=tmpA[kc % 2], in0=xT[:, kc, :],
                  in1=rstd, op=ALU.mult)
            P.act([Rsq[kc % 2], res("modd"), res("modT")], [Rh[kc]], out=hT[:, kc, :], in_=tmpA[kc % 2],
                  func=AF.Identity, scale=G(l, which, kc, b), bias=modT[:, l, sh0 + kc, b:b + 1])

    def post_norm(l, b, which):
        norm_stats([mixacc[:, kc, :] for kc in range(8)], Rmix)
        for kc in range(8):
            P.add("dve", "tensor_tensor", [Rmix[kc], res("rstd")], [Rsq[kc % 2]], out=tmpA[kc % 2],
                  in0=mixacc[:, kc, :], in1=rstd, op=ALU.mult)
            P.add("dve", "scalar_tensor_tensor", [Rsq[kc % 2], res("modd"), Rx[kc]], [Rx[kc]], out=xT[:, kc, :],
                  in0=tmpA[kc % 2], scalar=G(l, which, kc, b), in1=xT[:, kc, :], op0=ALU.mult, op1=ALU.add)

    cinctr = [0]

    def conv_pe(l, src_fn, halo, hidx, K, w0, wstride):
        i = cinctr[0]
        cinctr[0] = (i + 1) % 3
        ci, rci = cin[i], res("cin%d" % i)
        dg, rdg = dgs[i], res("dg%d" % i)
        rh = res("halo_%s_%d" % (halo[1], l))
        P.add("pool", "tensor_tensor", [res("identb"), res("pv")], [rdg], out=dg[:, 0:K, :],
              in0=identb.unsqueeze(1).to_broadcast([128, K, 128]),
              in1=pv[:, l, w0:w0 + (K - 1) * wstride + 1:wstride].unsqueeze(2).to_broadcast([128, K, 128]), op=ALU.mult)
        src_fn(ci[:, K - 1:K - 1 + T], rci)
        P.add("pool", "tensor_copy", [rh], [rci], out=ci[:, 0:K - 1], in_=halo[0][:, l, hidx, 0:K - 1])
        P.add("pool", "tensor_copy", [rci], [rh], out=halo[0][:, l, hidx, 0:K - 1], in_=ci[:, T:T + K - 1])
        def fin():
            b2 = nb()
            for k in range(K):
                P.mm([rdg, rci], [Rb[b2]], out=ps[:, b2, :], lhsT=dg[:, k, :], rhs=ci[:, k:k + T], start=(k == 0), stop=(k == K - 1))
            return b2
        return fin

    def conv_from_bank(l, bank, halo, hidx, K, w0, wstride):
        def src(dst, rci):
            P.act([Rb[bank]], [rci], out=dst, in_=ps[:, bank, :], func=AF.Copy)
        return conv_pe(l, src, halo, hidx, K, w0, wstride)

    pend = []

    def defer(fin, post, depth=2):
        pend.append((fin, post))
        while len(pend) > depth:
            f, p = pend.pop(0)
            p(f())

    def flush():
        while pend:
            f, p = pend.pop(0)
            p(f())

    def mm8(bank, slot, j, rhs_fn, rres):
        for kc in range(8):
            P.mm([Rring[slot], rres[kc]], [Rb[bank]], out=ps[:, bank, :], lhsT=ring_b[slot][:, kc, j * 128:(j + 1) * 128],
                 rhs=rhs_fn(kc), start=(kc == 0), stop=(kc == 7))

    Rzq = [res("zq%d" % q) for q in range(4)]
    Rxs = [res("xs%d" % t) for t in range(16)]
    Rmix = Rxs[0:8]
    Rst = [[res("st%d_%d" % (l, g)) for g in range(4)] for l in range(2)]
    Rstb = [res("stb%d" % g) for g in range(4)]

    def ssd_chunk(l, q):
        qs = slice(q * 128, (q + 1) * 128)
        dt_q, adt_q = dt_all[:, q, :], adt_all[:, q, :]
        acs, negacs, dd, dst, cd, ea, dtd = [smalls[:, i, :] for i in range(7)]
        bk = nb()
        P.mm([res("adt"), res("consts")], [Rb[bk]], out=ps[:, bk, 0:32], lhsT=U_f, rhs=adt_q, start=True, stop=True)
        P.mm([res("adt"), res("consts")], [Rb[bk]], out=ps[:, bk, 32:64], lhsT=ones_f, rhs=adt_q, start=True, stop=True)
        P.add("dve", "tensor_copy", [Rb[bk]], [res("acs")], out=acs, in_=ps[:, bk, 0:32])
        P.add("dve", "tensor_tensor", [res("lndt"), res("acs")], [res("negacs")], out=negacs, in0=lndt[:, q, :],
              in1=acs, op=ALU.subtract)
        P.add("dve", "tensor_tensor", [Rb[bk], res("acs")], [res("dd")], out=dd, in0=ps[:, bk, 32:64], in1=acs,
              op=ALU.subtract)
        P.act([res("dd")], [res("dst")], out=dst, in_=dd, func=AF.Exp)
        P.act([Rb[bk]], [res("cd")], out=cd, in_=ps[:, bk, 32:64], func=AF.Exp)
        P.act([res("acs")], [res("ea")], out=ea, in_=acs, func=AF.Exp)
        P.add("dve", "tensor_tensor", [res("dt"), res("dst")], [res("dtd")], out=dtd, in0=dt_q, in1=dst, op=ALU.mult)
        P.add("pool", "memset", [], [res("ssq")], ap=ssq[:, 0:4], constant=0.0)
        bkB = nb()
        for g in range(4):
            P.mm([res("BT"), res("identb")], [Rb[bkB]], out=ps[:, bkB, g * 128:(g + 1) * 128], lhsT=BT[:, g, qs],
                 rhs=identb, start=True, stop=True)
        bkS = nb()
        for g in range(4):
            P.mm([res("BT"), res("CT")], [Rb[bkS]], out=ps[:, bkS, g * 128:(g + 1) * 128], lhsT=BT[:, g, qs],
                 rhs=CT[:, g, qs], start=True, stop=True)
        xbanks = []
        for g4 in range(4):
            bk = nb()
            xbanks.append(bk)
            for j in range(4):
                t = g4 * 4 + j
                P.add("pe", "transpose", [Rxs[t], res("consts")], [Rb[bk]], out=ps[:, bk, j * 128:(j + 1) * 128],
                      in_=xs[:, t, qs], identity=ident_f)
        P.act([Rb[bkB]], [res("Btok")], out=Btok, in_=ps[:, bkB, :], func=AF.Copy)
        P.add("dve", "tensor_tensor", [Rb[bkS], res("consts")], [res("scm")], out=scm, in0=v3(ps[:, bkS, :], 4),
              in1=mask01.unsqueeze(1).to_broadcast([128, 4, 128]), op=ALU.mult)
        for g4 in range(4):
            bk = xbanks[g4]
            gsl = slice(g4 * 512, (g4 + 1) * 512)
            hs = slice(g4 * 8, (g4 + 1) * 8)
            P.act([Rb[bk]], [res("xdt_%d" % g4)], out=xdt[:, gsl], in_=ps[:, bk, :], func=AF.Copy)
            P.add("pool", "tensor_tensor", [res("xdt_%d" % g4), res("dtd")], [res("xdtd_%d" % g4)],
                  out=v3(xdtd[:, gsl], 8), in0=v3(xdt[:, gsl], 8), in1=bc_heads(dtd[:, hs]), op=ALU.mult)

        abc_bank = {}

        def emit_abc(hq):
            bk = nb(hold=True)
            abc_bank[hq] = bk
            P.mm([res("identb"), res("negmask")], [Rb[bk]], out=ps[:, bk, :], lhsT=identb,
                 rhs=negmask.rearrange("p a b -> p (a b)"), start=True, stop=False)
            for i in range(4):
                h = hq * 4 + i
                P.mm([res("adt"), res("consts")], [Rb[bk]], out=ps[:, bk, i * 128:(i + 1) * 128],
                     lhsT=adt_q[:, h:h + 1].to_broadcast([128, 128]), rhs=U_f, start=False, stop=(i == 3))

        def emit_quad_rest(hq, ydb):
            g = hq // 2
            bk = abc_bank[hq]
            di = hq % 3
            rdm, rm4 = res("Dm%d" % di), res("M4%d" % di)
            for i in range(4):
                h = hq * 4 + i
                P.act([Rb[bk], res("negacs")], [rdm], out=Dm[di][:, i, :], in_=ps[:, bk, i * 128:(i + 1) * 128],
                      func=AF.Exp, bias=negacs[:, h:h + 1], scale=1.0)
            rel(bk)
            P.add("dve", "tensor_tensor", [rdm, res("scm")], [rm4], out=M4[di], in0=Dm[di],
                  in1=scm[:, g, :].unsqueeze(1).to_broadcast([128, 4, 128]), op=ALU.mult)
            for i in range(4):
                h = hq * 4 + i
                P.mm([rm4, res("xdt_%d" % (h // 8))], [Rb[ydb]], out=ps[:, ydb, (h % 8) * 64:(h % 8 + 1) * 64],
                     lhsT=M4[di][:, i, :], rhs=xdt[:, h * 64:(h + 1) * 64], start=True, stop=False)
                P.mm([res("DI"), res("xdt_%d" % (h // 8))], [Rb[ydb]], out=ps[:, ydb, (h % 8) * 64:(h % 8 + 1) * 64],
                     lhsT=DI[:, h, :], rhs=xdt[:, h * 64:(h + 1) * 64], start=False, stop=True)

        def tail_dve(g, ydb, yob):
            gsl = slice(g * 512, (g + 1) * 512)
            hs = slice(g * 8, (g + 1) * 8)
            yi = g % 2
            ryt, rya = res("ytmp%d" % yi), res("yacc%d" % yi)
            P.add("dve", "tensor_tensor", [Rb[yob], res("ea")], [ryt], out=v3(ytmp[yi], 8), in0=v3(ps[:, yob, :], 8),
                  in1=bc_heads(ea[:, hs]), op=ALU.mult)
            P.add("dve", "tensor_tensor", [Rb[ydb], ryt], [rya], out=yacc[yi], in0=ps[:, ydb, :], in1=ytmp[yi], op=ALU.add)
            P.add("dve", "tensor_tensor", [rya, Rzq[q]], [rya], out=yacc[yi], in0=yacc[yi], in1=zq[:, q, gsl], op=ALU.mult)

        def tail_act(g):
            yi = g % 2
            rya, ryn = res("yacc%d" % yi), res("ynb%d" % yi)
            P.act([rya], [res("junk"), res("ssq")], out=junk, in_=yacc[yi], func=AF.Square, accum_out=ssq[:, g:g + 1])
            P.act([res("ssq"), res("epsc")], [res("rsg")], out=ssq[:, 4 + g:5 + g], in_=ssq[:, g:g + 1], func=AF.Ln,
                  bias=epsc[:, 0:1], scale=1.0)
            P.act([res("rsg")], [res("rsg")], out=ssq[:, 4 + g:5 + g], in_=ssq[:, 4 + g:5 + g], func=AF.Exp, scale=-0.5)
            P.act([rya, res("rsg")], [ryn], out=ynb[yi], in_=yacc[yi], func=AF.Identity, scale=ssq[:, 4 + g:5 + g])

        def tail_pe(g):
            yi = g % 2
            ryn = res("ynb%d" % yi)
            bk = nb()
            for j in range(4):
                P.mm([ryn, res("identb")], [Rb[bk]], out=ps[:, bk, j * 128:(j + 1) * 128],
                     lhsT=ynb[yi][:, j * 128:(j + 1) * 128], rhs=identb, start=True, stop=True)
            for j in range(4):
                t = g * 4 + j
                P.act([Rb[bk], res("ngs")], [Rzq[q]], out=ynT[:, q, t, :], in_=ps[:, bk, j * 128:(j + 1) * 128],
                      func=AF.Identity, scale=ngs[:, l, t:t + 1])

        emit_abc(0)
        emit_abc(1)
        ydbs = {}

        def do_tail(g):
            yob = nb()
            gsl = slice(g * 512, (g + 1) * 512)
            P.mm([res("CT"), Rstb[g]], [Rb[yob]], out=ps[:, yob, :], lhsT=CT[:, g, qs], rhs=state_bf[:, gsl],
                 start=True, stop=True)
            tail_dve(g, ydbs[g], yob)
            rel(ydbs[g])

        for g in range(4):
            ydbs[g] = nb(hold=True)
            for hh in range(2):
                hq = g * 2 + hh
                if hq + 2 < 8:
                    emit_abc(hq + 2)
                emit_quad_rest(hq, ydbs[g])
                if hh == 0:
                    if g >= 1:
                        do_tail(g - 1)
                    if g >= 2:
                        tail_act(g - 2)
                else:
                    if g >= 2:
                        tail_pe(g - 2)
        do_tail(3)
        tail_act(2)
        tail_pe(2)
        tail_act(3)
        tail_pe(3)
        for g in range(4):
            gsl = slice(g * 512, (g + 1) * 512)
            hs = slice(g * 8, (g + 1) * 8)
            bk = nb()
            P.mm([res("Btok"), res("xdtd_%d" % g)], [Rb[bk]], out=ps[:, bk, :], lhsT=Btok[:, g * 128:(g + 1) * 128],
                 rhs=xdtd[:, gsl], start=True, stop=True)
            P.add("pool", "tensor_tensor", [Rst[l][g], res("cd")], [Rst[l][g]], out=v3(state[:, l, gsl], 8),
                  in0=v3(state[:, l, gsl], 8), in1=bc_heads(cd[:, hs]), op=ALU.mult)
            P.add("dve", "tensor_tensor", [Rb[bk], Rst[l][g]], [Rst[l][g]], out=state[:, l, gsl], in0=ps[:, bk, :],
                  in1=state[:, l, gsl], op=ALU.add)
            P.act([Rst[l][g]], [Rstb[g]], out=state_bf[:, gsl], in_=state[:, l, gsl], func=AF.Copy)

    def sub1(l, b):
        P.barrier()
        pre_norm(l, b, 0, 0)
        for zc in range(4):
            s = wload("w_in", l, 0, 8, C_Z + zc * 512, 512)
            for q in range(4):
                bk = nb()
                for kc in range(8):
                    P.mm([Rring[s], Rh[kc]], [Rb[bk]], out=ps[:, bk, :], lhsT=hT[:, kc, q * 128:(q + 1) * 128],
                         rhs=ring_b[s][:, kc, :], start=(kc == 0), stop=(kc == 7))
                P.act([Rb[bk]], [Rzq[q]], out=zq[:, q, zc * 512:(zc + 1) * 512], in_=ps[:, bk, :], func=AF.Silu)
        for xc in range(6):
            s = wload("w_in", l, 0, 8, C_XBC + xc * 512, 512)
            for j in range(4):
                t = xc * 4 + j
                bk = nb()
                mm8(bk, s, j, lambda kc: hT[:, kc, :], Rh)
                fin = conv_from_bank(l, bk, (halo_ssd, "ssd"), t, 4, SSD_CW + t, 24)

                def post(b2, t=t):
                    bias = pv[:, l, SSD_CB + t:SSD_CB + t + 1]
                    if t < 16:
                        P.act([Rb[b2], res("pv")], [Rxs[t]], out=xs[:, t, :], in_=ps[:, b2, :], func=AF.Silu, bias=bias, scale=1.0)
                    elif t < 20:
                        P.act([Rb[b2], res("pv")], [res("BT")], out=BT[:, t - 16, :], in_=ps[:, b2, :], func=AF.Silu, bias=bias, scale=1.0)
                    else:
                        P.act([Rb[b2], res("pv")], [res("CT")], out=CT[:, t - 20, :], in_=ps[:, b2, :], func=AF.Silu, bias=bias, scale=1.0)
                defer(fin, post)
        flush()
        bk = nb()
        for q in range(4):
            for kc in range(8):
                P.mm([res("wdt"), Rh[kc]], [Rb[bk]], out=ps[:, bk, q * 32:(q + 1) * 32], lhsT=hT[:, kc, q * 128:(q + 1) * 128],
                     rhs=wdt[:, l, kc, :], start=(kc == 0), stop=(kc == 7))
        P.add("dve", "tensor_tensor", [Rb[bk], res("tokb")], [res("dtr")], out=dtr, in0=v3(ps[:, bk, 0:128], 4),
              in1=tokb[:, l, 0:32].unsqueeze(1).to_broadcast([128, 4, 32]), op=ALU.add)
        P.act([res("dtr")], [res("dtr")], out=dtr, in_=dtr, func=AF.Exp)
        P.act([res("dtr")], [res("dt")], out=dt_all, in_=dtr, func=AF.Ln, bias=1.0)
        P.act([res("dt")], [res("lndt")], out=lndt, in_=dt_all, func=AF.Ln)
        P.add("dve", "tensor_tensor", [res("dt"), res("negA")], [res("adt")], out=adt_all, in0=dt_all,
              in1=negA[:, l, :].unsqueeze(1).to_broadcast([128, 4, 32]), op=ALU.mult)
        for g in range(4):
            gsl = slice(g * 512, (g + 1) * 512)
            P.act([Rst[l][g]], [Rstb[g]], out=state_bf[:, gsl], in_=state[:, l, gsl], func=AF.Copy)
        P.add("pool", "tensor_tensor", [res("identb"), res("tokb")], [res("DI")], out=DI,
              in0=identb.unsqueeze(1).to_broadcast([128, 32, 128]),
              in1=tokb[:, l, 64:96].unsqueeze(2).to_broadcast([128, 32, 128]), op=ALU.mult)
        for q in range(4):
            ssd_chunk(l, q)
        if STOP_SSD:
            return
        P.barrier()
        Rscu = [res("scu%d" % i) for i in range(8)]
        for half in range(2):
            sC = wload("w_in", l, 0, 8, C_SCC + half * 512, 512)
            sH = wload("w_in", l, 0, 8, C_SCH + half * 512, 512)
            sB = wload("w_in", l, 0, 8, C_SCB + half * 512, 512)
            for j in range(4):
                ct = half * 4 + j
                bC = nb()
                mm8(bC, sC, j, lambda kc: hT[:, kc, :], Rh)
                gi = ct % 2
                P.act([Rb[bC]], [res("gtmp%d" % gi)], out=gtmp[gi], in_=ps[:, bC, :], func=AF.Copy)
                bH = nb()
                mm8(bH, sH, j, lambda kc: hT[:, kc, :], Rh)
                def src(dst, rci, bH=bH, gi=gi):
                    P.add("dve", "tensor_tensor", [Rb[bH], res("gtmp%d" % gi)], [rci], out=dst, in0=ps[:, bH, :],
                          in1=gtmp[gi], op=ALU.mult)
                fin = conv_pe(l, src, (halo_sc, "sc"), ct, 3, SC_CW + ct, 8)
                bB = nb()
                mm8(bB, sB, j, lambda kc: hT[:, kc, :], Rh)

                def post(b2, ct=ct, bB=bB):
                    ai = ct % 2
                    P.act([Rb[b2]], [res("acc%d" % ai)], out=accb[ai], in_=ps[:, b2, :], func=AF.Copy)
                    P.add("dve", "tensor_tensor", [Rb[bB], res("acc%d" % ai)], [Rscu[ct]], out=sc_u[:, ct, :], in0=ps[:, bB, :],
                          in1=accb[ai], op=ALU.mult)
                defer(fin, post, depth=1)
        flush()
        for half in range(2):
            sO = wload("w_sc_out", l, 0, 8, half * 512, 512)
            sG = wload("w_in", l, 0, 8, C_GSC + half * 512, 512)
            for j in range(4):
                ct = half * 4 + j
                bY = nb()
                mm8(bY, sO, j, lambda kc: sc_u[:, kc, :], Rscu)
                bG = nb()
                mm8(bG, sG, j, lambda kc: hT[:, kc, :], Rh)
                gi = ct % 2
                P.act([Rb[bG]], [res("gtmp%d" % gi)], out=gtmp[gi], in_=ps[:, bG, :], func=AF.Sigmoid)
                P.add("dve", "tensor_tensor", [Rb[bY], res("gtmp%d" % gi)], [Rmix[ct]], out=mixacc[:, ct, :],
                      in0=ps[:, bY, :], in1=gtmp[gi], op=ALU.mult)
        Rmb = [res("mixb%d" % i) for i in range(8)]
        for half in range(2):
            sA = wload("w_ssd_out", l, 0, 8, half * 512, 512)
            sA2 = wload("w_ssd_out", l, 8, 8, half * 512, 512)
            sG = wload("w_in", l, 0, 8, C_GSSD + half * 512, 512)
            for j in range(4):
                ct = half * 4 + j
                bY = nb()
                for t in range(16):
                    sl = sA if t < 8 else sA2
                    P.mm([Rring[sl]] + Rzq, [Rb[bY]], out=v3(ps[:, bY, :], 4), lhsT=ring_b[sl][:, t % 8, j * 128:(j + 1) * 128],
                         rhs=ynT[:, :, t, :], start=(t == 0), stop=(t == 15))
                bG = nb()
                mm8(bG, sG, j, lambda kc: hT[:, kc, :], Rh)
                gi = ct % 2
                P.act([Rb[bG]], [res("gtmp%d" % gi)], out=gtmp[gi], in_=ps[:, bG, :], func=AF.Sigmoid)
                P.add("dve", "tensor_tensor", [Rb[bY], res("gtmp%d" % gi)], [res("gtmp%d" % gi)], out=gtmp[gi],
                      in0=ps[:, bY, :], in1=gtmp[gi], op=ALU.mult)
                P.add("pool", "tensor_tensor", [res("gtmp%d" % gi), Rmix[ct]], [Rmb[ct]], out=mixb[:, ct, :], in0=gtmp[gi],
                      in1=mixacc[:, ct, :], op=ALU.add)
        for half in range(2):
            s = wload("w_o", l, 0, 8, half * 512, 512)
            for j in range(4):
                ct = half * 4 + j
                bk = nb()
                mm8(bk, s, j, lambda kc: mixb[:, kc, :], Rmb)
                P.act([Rb[bk]], [Rmix[ct]], out=mixacc[:, ct, :], in_=ps[:, bk, :], func=AF.Copy)
        post_norm(l, b, 2)

    def sub2(l, b):
        P.barrier()
        pre_norm(l, b, 1, 24)
        Rf = [res("f%d" % i) for i in range(22)]
        blocks = [(0, 4), (4, 4), (8, 4), (12, 4), (16, 4), (20, 2)]
        for (t0, nt) in blocks:
            sg = wload("w_up", l, 0, 8, t0 * 128, nt * 128)
            sv = wload("w_up", l, 0, 8, 2816 + t0 * 128, nt * 128)
            for j in range(nt):
                i = t0 + j
                bG = nb()
                mm8(bG, sg, j, lambda kc: hT[:, kc, :], Rh)
                fing = conv_from_bank(l, bG, (halo_ffn, "ffn"), i, 3, FFN_CW + i, 44)

                def postg(b2g, i=i):
                    gi = i % 2
                    P.act([Rb[b2g], res("pv")], [res("gtmp%d" % gi)], out=gtmp[gi], in_=ps[:, b2g, :], func=AF.Silu,
                          bias=pv[:, l, FFN_CB + i:FFN_CB + i + 1], scale=1.0)
                defer(fing, postg)
                bV = nb()
                mm8(bV, sv, j, lambda kc: hT[:, kc, :], Rh)
                finv = conv_from_bank(l, bV, (halo_ffn, "ffn"), 22 + i, 3, FFN_CW + 22 + i, 44)

                def postv(b2v, i=i):
                    gi = i % 2
                    P.add("dve", "scalar_tensor_tensor", [Rb[b2v], res("pv"), res("gtmp%d" % gi)], [Rf[i]], out=fbuf[:, i, :],
                          in0=ps[:, b2v, :], scalar=pv[:, l, FFN_CB + 22 + i:FFN_CB + 22 + i + 1], in1=gtmp[gi],
                          op0=ALU.add, op1=ALU.mult)
                defer(finv, postv)
        flush()
        for half in range(2):
            ss = [wload("w_down", l, 0, 8, half * 512, 512), wload("w_down", l, 8, 8, half * 512, 512),
                  wload("w_down", l, 16, 6, half * 512, 512)]
            for j in range(4):
                ct = half * 4 + j
                bk = nb()
                for kc in range(22):
                    sl = ss[kc // 8]
                    P.mm([Rring[sl], Rf[kc]], [Rb[bk]], out=ps[:, bk, :], lhsT=ring_b[sl][:, kc % 8, j * 128:(j + 1) * 128],
                         rhs=fbuf[:, kc, :], start=(kc == 0), stop=(kc == 21))
                P.act([Rb[bk]], [res("obuf2_%d" % ct)], out=obuf2[:, ct, :], in_=ps[:, bk, :], func=AF.Copy)
        norm_stats([obuf2[:, kc, :] for kc in range(8)], [res("obuf2_%d" % kc) for kc in range(8)])
        for kc in range(8):
            P.add("dve", "tensor_tensor", [res("obuf2_%d" % kc), res("rstd")], [Rsq[kc % 2]], out=tmpA[kc % 2],
                  in0=obuf2[:, kc, :], in1=rstd, op=ALU.mult)
            P.add("dve", "scalar_tensor_tensor", [Rsq[kc % 2], res("modd"), Rx[kc]], [Rx[kc]], out=xT[:, kc, :],
                  in0=tmpA[kc % 2], scalar=G(l, 3, kc, b), in1=xT[:, kc, :], op0=ALU.mult, op1=ALU.add)

    for b in range(NSEQ):
        for l in range(2):
            for g in range(4):
                P.add("pool", "memset", [], [Rst[l][g]], ap=state[:, l, g * 512:(g + 1) * 512], constant=0.0)
            P.add("pool", "memset", [], [res("halo_ssd_%d" % l)], ap=halo_ssd[:, l], constant=0.0)
            P.add("pool", "memset", [], [res("halo_sc_%d" % l)], ap=halo_sc[:, l], constant=0.0)
            P.add("pool", "memset", [], [res("halo_ffn_%d" % l)], ap=halo_ffn[:, l], constant=0.0)
        for tt in range(NTT):
            P.dma("sp", [], Rx, "xload", out=xT,
                  in_=xT_d[b, :, tt * T:(tt + 1) * T].rearrange("(k p) t -> p k t", p=128))
            for l in range(NL):
                sub1(l, b)
                if SUB2:
                    sub2(l, b)
            P.dma("pool", Rx, [], "store", out=yT_d[b, :, tt * T:(tt + 1) * T].rearrange("(k p) t -> p k t", p=128),
                  in_=xT)

    if dbg:
        for name in dbg:
            ap = {"hT": hT, "zq": zq, "xs": xs, "BT": BT, "CT": CT, "dt": dt_all, "adt": adt_all, "ynT": ynT, "mixb": mixb,
                  "mixacc": mixacc, "xT": xT, "negmask": negmask, "identb": identb, "smalls": smalls, "ssq": ssq, "scm": scm, "Dm1": Dm[1], "M41": M4[1],
                  "yacc1": yacc[1], "ytmp1": ytmp[1], "ynb1": ynb[1], "xdt": xdt, "Btok": Btok, "modT": modT, "modd": modd, "state": state, "fbuf": fbuf, "scu": sc_u}[name]
            shp = list(ap.shape)
            dt_ = ap.dtype
            o = nc.dram_tensor("dbg_" + name, shp, dt_, kind="ExternalOutput").ap()
            P.dma("pool", list(R.values()), [], "store", out=o, in_=ap)

    semnames = P.finalize()
    import contextlib
    with contextlib.ExitStack() as es:
        sems = {n: es.enter_context(nc.semaphore(n)) for n in semnames}
        block = es.enter_context(nc.Block())

        @block.tensor
        def _(e):
            P.replay("pe", e, sems)

        @block.scalar
        def _(e):
            P.replay("act", e, sems)

        @block.vector
        def _(e):
            P.replay("dve", e, sems)

        @block.gpsimd
        def _(e):
            P.replay("pool", e, sems, final_waits=["store"])

        @block.sync
        def _(e):
            P.replay("sp", e, sems)
    return nc, P


def host_prep(inp):
    f = lambda a: np.ascontiguousarray(np.asarray(a, dtype=np.float32))
    pvs, tokbs = [], []
    for l in range(2):
        pvl = np.zeros((128, NPV), np.float32)

        def put(off, vec):
            v = np.asarray(vec, np.float32).reshape(-1, 128).T
            pvl[:, off:off + v.shape[1]] = v

        put(G_MIXPRE, inp["mix_pre_g"][l]); put(G_MIXPOST, inp["mix_post_g"][l])
        put(G_FFNPRE, inp["ffn_pre_g"][l]); put(G_FFNPOST, inp["ffn_post_g"][l])
        for k in range(4):
            put(SSD_CW + k * 24, inp["ssd_conv_w"][l][k])
        put(SSD_CB, inp["ssd_conv_b"][l]); put(SSD_NG, inp["ssd_norm_g"][l])
        for k in range(3):
            put(SC_CW + k * 8, inp["sc_conv_w"][l][k])
            put(FFN_CW + k * 44, inp["ffn_conv_w"][l][k])
        put(FFN_CB, inp["ffn_conv_b"][l]); put(ADA_B, inp["ada_b"][l])
        pvs.append(pvl)
        tb = np.zeros((128, 96), np.float32)
        tb[:, 0:32] = np.asarray(inp["ssd_dt_bias"][l])[None, :]
        tb[:, 32:64] = np.asarray(inp["ssd_a_log"][l])[None, :]
        tb[:, 64:96] = np.asarray(inp["ssd_d"][l])[None, :]
        tokbs.append(tb)
    pv = np.ascontiguousarray(np.stack(pvs, 1))
    tokb = np.ascontiguousarray(np.stack(tokbs, 1))
    consts = np.zeros((128, 4, 128), np.float32)
    consts[:, 0] = np.eye(128)
    consts[:, 1] = np.triu(np.ones((128, 128)))
    consts[:, 2] = 1.0
    consts[:, 3] = np.triu(np.ones((128, 128)))
    shared = {"pv": pv, "tokb": tokb, "consts": consts, "ada_w": f(inp["ada_w"])}
    for name in WSHAPES:
        shared[name] = f(inp[name])
    x = np.asarray(inp["x"], np.float32)
    c = np.asarray(inp["c"], np.float32)
    maps = []
    for i in range(8):
        m = dict(shared)
        m["xT"] = np.ascontiguousarray(x[2 * i:2 * i + 2].transpose(0, 2, 1))
        m["cT"] = np.ascontiguousarray(c[2 * i:2 * i + 2].reshape(2, 8, 128).transpose(2, 1, 0))
        maps.append(m)
    return maps


def kernel(**inputs):
    maps = host_prep(inputs)
    nc, _ = build()
    res = run_bass_kernel_spmd(nc, maps, core_ids=list(range(8)))
    out = np.empty((16, 4096, 1024), np.float32)
    for i in range(8):
        out[2 * i:2 * i + 2] = res.results[i]["yT"].transpose(0, 2, 1)
    return out
```

```python
import numpy as np
import concourse.bass as bass
import concourse.mybir as mybir
from concourse.bass_utils import run_bass_kernel_spmd

F32 = mybir.dt.float32
BF16 = mybir.dt.bfloat16
U8 = mybir.dt.uint8
AF = mybir.ActivationFunctionType
ALU = mybir.AluOpType

D = 1024
T = 512
EPS = 1e-6
C_Z, C_XBC, C_DT, C_SCB, C_SCC, C_SCH, C_GSSD, C_GSC = 0, 2048, 5120, 5152, 6176, 7200, 8224, 9248
WSHAPES = {"w_in": [1024, 10272], "w_ssd_out": [2048, 1024], "w_sc_out": [1024, 1024],
           "w_o": [1024, 1024], "w_up": [1024, 5632], "w_down": [2816, 1024]}
G_MIXPRE, G_MIXPOST, G_FFNPRE, G_FFNPOST = 0, 8, 16, 24
SSD_CW, SSD_CB, SSD_NG, SC_CW, FFN_CW, FFN_CB, ADA_B = 32, 128, 152, 184, 208, 340, 384
NPV = 432
NSLOT = 4
EPOCH = 4000
DEPOCH = 200
ENG_ATTR = {"pe": "tensor", "act": "scalar", "dve": "vector", "pool": "gpsimd", "sp": "sync"}


class Res:
    __slots__ = ("name", "lw", "rd")

    def __init__(self, name):
        self.name = name
        self.lw = None
        self.rd = {}


class Ins:
    __slots__ = ("eng", "meth", "kw", "key", "eidx", "waits", "signal", "seq")


class Prog:
    def __init__(self):
        self.ins = []
        self.cnt = {e: 0 for e in ENG_ATTR}
        self.nkey = 0
        self.last = {}
        self.pending = {e: [] for e in ENG_ATTR}

    def barrier(self):
        for e in ("pe", "act", "dve", "pool"):
            for e2, I in self.last.items():
                if e2 != e:
                    self.pending[e].append(I)

    def newkey(self):
        self.nkey += 1
        return "k%d" % self.nkey

    def add(self, eng, meth, R=(), W=(), key=None, **kw):
        I = Ins()
        I.eng, I.meth, I.kw, I.key = eng, meth, kw, key
        I.eidx = self.cnt[eng]
        self.cnt[eng] += 1
        I.waits, I.signal, I.seq = [], False, 0
        raw, war = {}, {}
        for r in R:
            if r.lw is not None:
                raw[id(r.lw)] = r.lw
        for w in W:
            if w.lw is not None:
                war[id(w.lw)] = w.lw
            for x in w.rd.values():
                war[id(x)] = x
        for d in raw.values():
            if d.key is None and I.key is None and d.eng == eng and eng == "pe":
                continue
            d.signal = True
            I.waits.append(d)
        for k, d in war.items():
            if k in raw or d is I:
                continue
            if d.key is None and I.key is None and d.eng == eng and eng == "pe":
                continue
            d.signal = True
            I.waits.append(d)
        if self.pending[eng]:
            for d in self.pending[eng]:
                d.signal = True
                I.waits.append(d)
            self.pending[eng] = []
        if key is None:
            self.last[eng] = I
        for r in R:
            r.rd[eng if key is None else ("dma", id(I))] = I
        for w in W:
            w.lw = I
            w.rd = {}
        self.ins.append(I)
        return I

    def mm(self, R, W, **kw):
        return self.add("pe", "matmul", R, W, **kw)

    def act(self, R, W, **kw):
        return self.add("act", "activation", R, W, **kw)

    def dma(self, q, R, W, key, **kw):
        return self.add(q, "dma_start", R, W, key=key, **kw)

    def finalize(self):
        ccnt = {e: 0 for e in ENG_ATTR}
        dcnt = {}
        semnames = set()
        for I in self.ins:
            if I.key is not None:
                dcnt[I.key] = dcnt.get(I.key, 0) + 1
                I.seq = dcnt[I.key]
                semnames.add("d_%s_%d" % (I.key, (I.seq - 1) // DEPOCH))
            elif I.signal:
                ccnt[I.eng] += 1
                I.seq = ccnt[I.eng]
        self.ccnt, self.dcnt = ccnt, dcnt
        for e, n in ccnt.items():
            for ep in range((max(n, 1) - 1) // EPOCH + 1):
                semnames.add("c_%s_%d" % (e, ep))
        return sorted(semnames)

    def replay(self, eng, handle, sems, final_waits=()):
        waited = {}
        for I in self.ins:
            if I.eng != eng:
                continue
            for d in I.waits:
                if d.key is not None:
                    src = "d_" + d.key
                    if waited.get(src, 0) >= d.seq:
                        continue
                    waited[src] = d.seq
                    handle.wait_ge(sems["%s_%d" % (src, (d.seq - 1) // DEPOCH)], 16 * ((d.seq - 1) % DEPOCH + 1))
                else:
                    if waited.get(d.eng, 0) >= d.seq:
                        continue
                    waited[d.eng] = d.seq
                    ep = (d.seq - 1) // EPOCH
                    handle.wait_ge(sems["c_%s_%d" % (d.eng, ep)], (d.seq - 1) % EPOCH + 1)
            bi = getattr(handle, I.meth)(**I.kw)
            if I.key is not None:
                bi.then_inc(sems["d_%s_%d" % (I.key, (I.seq - 1) // DEPOCH)], 16)
            elif I.signal:
                ep = (I.seq - 1) // EPOCH
                bi.then_inc(sems["c_%s_%d" % (eng, ep)], 1)
        for k in final_waits:
            if k in self.dcnt:
                n = self.dcnt[k]
                for ep in range((n - 1) // DEPOCH + 1):
                    last = min(n, (ep + 1) * DEPOCH)
                    handle.wait_ge(sems["d_%s_%d" % (k, ep)], 16 * ((last - 1) % DEPOCH + 1))


def build(NSEQ=2, NTT=8, dbg=None, NL=2, SUB2=True, STOP_SSD=False):
    nc = bass.Bass("TRN2", target_bir_lowering=False)
    P = Prog()
    xT_d = nc.dram_tensor("xT", [2, 1024, 4096], F32, kind="ExternalInput").ap()
    yT_d = nc.dram_tensor("yT", [2, 1024, 4096], F32, kind="ExternalOutput").ap()
    wd, wb = {}, {}
    for name, shp in WSHAPES.items():
        wd[name] = nc.dram_tensor(name, [2] + shp, F32, kind="ExternalInput").ap()
        wb[name] = nc.dram_tensor(name + "_bf", [2] + shp, BF16).ap()
    adaw_d = nc.dram_tensor("ada_w", [2, 1024, 6144], F32, kind="ExternalInput").ap()
    pv_d = nc.dram_tensor("pv", [128, 2, NPV], F32, kind="ExternalInput").ap()
    tokb_d = nc.dram_tensor("tokb", [128, 2, 96], F32, kind="ExternalInput").ap()
    cT_d = nc.dram_tensor("cT", [128, 8, 2], F32, kind="ExternalInput").ap()
    consts_d = nc.dram_tensor("consts", [128, 4, 128], F32, kind="ExternalInput").ap()
    dbg_out = {}

    ps_t = nc.alloc_psum_tensor("ps", [128, 8, 512], F32)
    ps = ps_t[:]
    arena = nc.alloc_sbuf_tensor("arena", [128, 211600], U8)
    aoff = [0]

    def V(shape, dt):
        n = int(np.prod(shape)) * (4 if dt == F32 else 2)
        off = (aoff[0] + 63) // 64 * 64
        aoff[0] = off + n
        ap = arena[:, off:off + n].bitcast(dt)
        if len(shape) == 2:
            ap = ap.rearrange("p (a b) -> p a b", a=shape[0])
        elif len(shape) == 3:
            ap = ap.rearrange("p (a b c) -> p a b c", a=shape[0], b=shape[1])
        return ap

    def mark():
        return aoff[0]

    def reset(m):
        aoff[0] = m

    consts = V([4, 128], F32)
    ident_f, U_f, ones_f, mask01 = consts[:, 0, :], consts[:, 1, :], consts[:, 2, :], consts[:, 3, :]
    identb = V([128], BF16)
    onesb = V([128], BF16)
    negmask = V([4, 128], BF16)
    pv = V([2, NPV], F32)
    tokb = V([2, 96], F32)
    negA = V([2, 32], F32)
    ngs = V([2, 16], F32)
    cT = V([8, 2], F32)
    cact = V([8, 2], F32)
    modT = V([2, 48, 2], F32)
    modd = V([2, 4, 16], F32)
    mtmp = V([16], F32)
    epsc = V([16], F32)
    wdt = V([2, 8, 32], BF16)
    state = V([2, 2048], F32)
    state_bf = V([2048], BF16)
    halo_ssd = V([2, 24, 4], BF16)
    halo_sc = V([2, 8, 2], BF16)
    halo_ffn = V([2, 44, 2], BF16)
    xT = V([8, T], F32)
    hT = V([8, T], BF16)
    cin = [V([516], BF16) for _ in range(3)]
    accb = [V([T], F32) for _ in range(2)]
    dgs = [V([4, 128], BF16) for _ in range(3)]
    tmpA = [V([T], F32) for _ in range(2)]
    tmpAb = [t_.bitcast(BF16)[:, 0:T] for t_ in tmpA]
    rstd = V([T], F32)
    ring_b = []
    ring_f = []
    for s in range(NSLOT):
        m0 = mark()
        ring_b.append(V([8, 512], BF16))
        reset(m0)
        ring_f.append(V([8, 256], F32))
    xs_off = (mark() + 63) // 64 * 64
    xs = V([16, T], F32)
    BT = V([4, T], BF16)
    CT = V([4, T], BF16)
    zq_off = (mark() + 63) // 64 * 64
    zq = V([4, 2048], BF16)
    r1 = (mark() + 63) // 64 * 64
    dt_all = V([4, 32], F32)
    adt_all = V([4, 32], F32)
    dtr = V([4, 32], F32)
    lndt = V([4, 32], F32)
    smalls = V([8, 32], F32)
    ssq = V([16], F32)
    xdt = V([2048], BF16)
    xdtd = V([2048], BF16)
    Btok = V([512], BF16)
    scm = V([4, 128], F32)
    Dm = [V([4, 128], F32) for _ in range(3)]
    M4 = [V([4, 128], BF16) for _ in range(3)]
    ytmp = [V([512], F32) for _ in range(2)]
    yacc = [V([512], F32) for _ in range(2)]
    DI = V([32, 128], BF16)
    ynb = [V([512], BF16) for _ in range(2)]
    junk = V([512], BF16)
    r1_ssd_end = mark()
    reset(r1)
    sc_u = V([8, T], BF16)
    gtmp = [V([T], F32) for _ in range(2)]
    mixb = V([8, T], BF16)
    obuf2 = V([8, T], F32)
    r1_sc_end = mark()
    ARENA = 211600
    assert max(r1_ssd_end, r1_sc_end) <= ARENA, (r1_ssd_end, r1_sc_end)
    print("arena use", r1_ssd_end, r1_sc_end)
    reset(xs_off)
    fbuf = V([22, T], BF16)
    reset(xs_off)
    mixacc = V([8, T], F32)
    reset(zq_off)
    ynT = V([4, 16, 128], BF16)
    reset(ARENA)

    R = {}

    def res(name):
        if name not in R:
            R[name] = Res(name)
        return R[name]

    Rb = [res("bank%d" % i) for i in range(8)]
    Rring = [res("ring%d" % i) for i in range(NSLOT)]
    bankctr = [0]

    live = set()

    def nb(hold=False):
        for _ in range(9):
            b = bankctr[0]
            bankctr[0] = (b + 1) % 8
            if b not in live:
                break
        else:
            raise RuntimeError("no free psum bank")
        if hold:
            live.add(b)
        return b

    def rel(b):
        live.discard(b)

    ringctr = [0]

    def wload(name, l, kc0, nkc, c0, ncols):
        s = ringctr[0]
        ringctr[0] = (s + 1) % NSLOT
        tag = ("_g%d" % win_grp(c0)) if name == "w_in" else ""
        P.dma("sp", [res("wb_%s_%d%s" % (name, l, tag))], [Rring[s]], "ring%d" % s,
              out=ring_b[s][:, 0:nkc, 0:ncols],
              in_=wb[name][l, kc0 * 128:(kc0 + nkc) * 128, c0:c0 + ncols].rearrange("(k p) n -> p k n", p=128))
        return s

    def bc_heads(ap8, n=64):
        H = ap8.shape[1]
        return ap8.unsqueeze(2).to_broadcast([128, H, n])

    def v3(ap, a):
        return ap.rearrange("p (a b) -> p a b", a=a)

    kc_ = P.newkey()
    P.dma("sp", [], [res("consts")], kc_, out=consts, in_=consts_d)
    k = P.newkey()
    P.dma("sp", [], [res("pv")], k, out=pv, in_=pv_d)
    k = P.newkey()
    P.dma("sp", [], [res("tokb")], k, out=tokb, in_=tokb_d)
    k = P.newkey()
    P.dma("sp", [], [res("cT")], k, out=cT, in_=cT_d)
    WIN_GRP = [0, 2048, 5120, 8224, 10272]

    def win_grp(c0):
        for gi in range(4):
            if WIN_GRP[gi] <= c0 < WIN_GRP[gi + 1]:
                return gi
        raise ValueError(c0)

    def cast(name, l, c0=None, c1=None, tag=""):
        rows = WSHAPES[name][0]
        for r0 in range(0, rows, 512):
            r1_ = min(rows, r0 + 512)
            if c0 is None:
                o, i_ = wb[name][l, r0:r1_, :], wd[name][l, r0:r1_, :]
            else:
                o, i_ = wb[name][l, r0:r1_, c0:c1], wd[name][l, r0:r1_, c0:c1]
            P.dma("pool", [], [res("wb_%s_%d%s" % (name, l, tag))], "cast_%s_%d%s" % (name, l, tag), out=o, in_=i_)

    for l in range(2):
        for gi in (0, 1, 2):
            cast("w_in", l, WIN_GRP[gi], WIN_GRP[gi + 1], "_g%d" % gi)
        cast("w_sc_out", l)
        cast("w_in", l, WIN_GRP[3], WIN_GRP[4], "_g3")
        for name in ["w_ssd_out", "w_o", "w_up", "w_down"]:
            cast(name, l)
    P.add("dve", "tensor_copy", [res("consts")], [res("identb")], out=identb, in_=ident_f)
    P.add("dve", "tensor_copy", [res("consts")], [res("onesb")], out=onesb, in_=ones_f)
    P.add("dve", "tensor_scalar", [res("consts")], [res("negmask")], out=negmask,
          in0=mask01.unsqueeze(1).to_broadcast([128, 4, 128]), scalar1=-1.0, scalar2=30000.0,
          op0=ALU.add, op1=ALU.mult)
    P.act([res("cT")], [res("cact")], out=cact, in_=cT, func=AF.Silu)
    P.add("pool", "memset", [], [res("epsc")], ap=epsc[:, 0:1], constant=float(512 * EPS))
    P.add("pool", "memset", [], [res("epsc")], ap=epsc[:, 1:2], constant=float(1024 * EPS))
    def prologue_layer(l):
        for j in range(24):
            s = ringctr[0]
            ringctr[0] = (s + 1) % NSLOT
            P.dma("sp", [], [Rring[s]], "ring%d" % s, out=ring_f[s],
                  in_=adaw_d[l, :, j * 256:(j + 1) * 256].rearrange("(k p) n -> p k n", p=128))
            for nt in range(2):
                ntg = j * 2 + nt
                for kc in range(8):
                    P.mm([Rring[s], res("cact")], [Rb[4 + l]], out=ps[:, 4 + l, ntg * 2:ntg * 2 + 2],
                         lhsT=ring_f[s][:, kc, nt * 128:(nt + 1) * 128], rhs=cact[:, kc, :],
                         start=(kc == 0), stop=(kc == 7))
        P.add("dve", "tensor_tensor", [Rb[4 + l], res("pv")], [res("modT")], out=modT[:, l],
              in0=v3(ps[:, 4 + l, 0:96], 48), in1=pv[:, l, ADA_B:ADA_B + 48].unsqueeze(2).to_broadcast([128, 48, 2]),
              op=ALU.add)
        for which, (m0, goff, plus1) in enumerate([(8, G_MIXPRE, True), (32, G_FFNPRE, True),
                                                   (16, G_MIXPOST, False), (40, G_FFNPOST, False)]):
            P.add("dve", "tensor_scalar", [res("modT")], [res("mtmp")], out=v3(mtmp, 8),
                  in0=modT[:, l, m0:m0 + 8, :], scalar1=(1.0 if plus1 else 0.0), scalar2=32.0,
                  op0=ALU.add, op1=ALU.mult)
            P.add("dve", "tensor_tensor", [res("mtmp"), res("pv")], [res("modd")], out=v3(modd[:, l, which, :], 8),
                  in0=v3(mtmp, 8), in1=pv[:, l, goff:goff + 8].unsqueeze(2).to_broadcast([128, 8, 2]), op=ALU.mult)
        P.add("dve", "tensor_scalar", [res("pv")], [res("ngs")], out=ngs[:, l, :], in0=pv[:, l, SSD_NG:SSD_NG + 16],
              scalar1=float(np.sqrt(512.0)), scalar2=None, op0=ALU.mult)
        P.act([res("tokb")], [res("negA")], out=negA[:, l, :], in_=tokb[:, l, 32:64], func=AF.Exp)
        P.add("dve", "tensor_scalar", [res("negA")], [res("negA")], out=negA[:, l, :], in0=negA[:, l, :],
              scalar1=-1.0, scalar2=None, op0=ALU.mult)
        k = P.newkey()
        P.dma("sp", [res("wb_w_in_%d_g2" % l)], [res("wdt")], k, out=wdt[:, l],
              in_=wb["w_in"][l, :, C_DT:C_DT + 32].rearrange("(k p) n -> p k n", p=128))

    prologue_layer(0)
    prologue_layer(1)

    def G(l, which, kc, b):
        return modd[:, l, which, kc * 2 + b:kc * 2 + b + 1]

    Rx = [res("x%d" % i) for i in range(8)]
    Rh = [res("h%d" % i) for i in range(8)]
    Rsq = [res("tmpA%d" % i) for i in range(2)]

    def norm_stats(srcs, rsrcs):
        n = len(srcs)
        bk = nb()
        for i in range(n):
            P.act([rsrcs[i]], [Rsq[i % 2]], out=tmpAb[i % 2], in_=srcs[i], func=AF.Square)
            P.mm([Rsq[i % 2], res("onesb")], [Rb[bk]], out=ps[:, bk, :], lhsT=onesb, rhs=tmpAb[i % 2],
                 start=(i == 0), stop=(i == n - 1))
        assert n == 8
        P.act([Rb[bk], res("epsc")], [res("rstd")], out=rstd, in_=ps[:, bk, :], func=AF.Ln, bias=epsc[:, 1:2], scale=1.0)
        P.act([res("rstd")], [res("rstd")], out=rstd, in_=rstd, func=AF.Exp, scale=-0.5)

    def pre_norm(l, b, which, sh0):
        norm_stats([xT[:, kc, :] for kc in range(8)], Rx)
        for kc in range(8):
            P.add("dve", "tensor_tensor", [Rx[kc], res("rstd")], [Rsq[kc % 2]], out=tmpA[kc % 2], in0=xT[:, kc, :],
                  in1=rstd, op=ALU.mult)
            P.act([Rsq[kc % 2], res("modd"), res("modT")], [Rh[kc]], out=hT[:, kc, :], in_=tmpA[kc % 2],
                  func=AF.Identity, scale=G(l, which, kc, b), bias=modT[:, l, sh0 + kc, b:b + 1])

    def post_norm(l, b, which):
        norm_stats([mixacc[:, kc, :] for kc in range(8)], Rmix)
        for kc in range(8):
            P.add("dve", "tensor_tensor", [Rmix[kc], res("rstd")], [Rsq[kc % 2]], out=tmpA[kc % 2],
                  in0=mixacc[:, kc, :], in1=rstd, op=ALU.mult)
            P.add("dve", "scalar_tensor_tensor", [Rsq[kc % 2], res("modd"), Rx[kc]], [Rx[kc]], out=xT[:, kc, :],
                  in0=tmpA[kc % 2], scalar=G(l, which, kc, b), in1=xT[:, kc, :], op0=ALU.mult, op1=ALU.add)

    cinctr = [0]

    def conv_pe(l, src_fn, halo, hidx, K, w0, wstride):
        i = cinctr[0]
        cinctr[0] = (i + 1) % 3
        ci, rci = cin[i], res("cin%d" % i)
        dg, rdg = dgs[i], res("dg%d" % i)
        rh = res("halo_%s_%d" % (halo[1], l))
        P.add("pool", "tensor_tensor", [res("identb"), res("pv")], [rdg], out=dg[:, 0:K, :],
              in0=identb.unsqueeze(1).to_broadcast([128, K, 128]),
              in1=pv[:, l, w0:w0 + (K - 1) * wstride + 1:wstride].unsqueeze(2).to_broadcast([128, K, 128]), op=ALU.mult)
        src_fn(ci[:, K - 1:K - 1 + T], rci)
        P.add("pool", "tensor_copy", [rh], [rci], out=ci[:, 0:K - 1], in_=halo[0][:, l, hidx, 0:K - 1])
        P.add("pool", "tensor_copy", [rci], [rh], out=halo[0][:, l, hidx, 0:K - 1], in_=ci[:, T:T + K - 1])
        def fin():
            b2 = nb()
            for k in range(K):
                P.mm([rdg, rci], [Rb[b2]], out=ps[:, b2, :], lhsT=dg[:, k, :], rhs=ci[:, k:k + T], start=(k == 0), stop=(k == K - 1))
            return b2
        return fin

    def conv_from_bank(l, bank, halo, hidx, K, w0, wstride):
        def src(dst, rci):
            P.act([Rb[bank]], [rci], out=dst, in_=ps[:, bank, :], func=AF.Copy)
        return conv_pe(l, src, halo, hidx, K, w0, wstride)

    pend = []

    def defer(fin, post, depth=2):
        pend.append((fin, post))
        while len(pend) > depth:
            f, p = pend.pop(0)
            p(f())

    def flush():
        while pend:
            f, p = pend.pop(0)
            p(f())

    def mm8(bank, slot, j, rhs_fn, rres):
        for kc in range(8):
            P.mm([Rring[slot], rres[kc]], [Rb[bank]], out=ps[:, bank, :], lhsT=ring_b[slot][:, kc, j * 128:(j + 1) * 128],
                 rhs=rhs_fn(kc), start=(kc == 0), stop=(kc == 7))

    Rzq = [res("zq%d" % q) for q in range(4)]
    Rxs = [res("xs%d" % t) for t in range(16)]
    Rmix = Rxs[0:8]
    Rst = [[res("st%d_%d" % (l, g)) for g in range(4)] for l in range(2)]
    Rstb = [res("stb%d" % g) for g in range(4)]

    def ssd_chunk(l, q):
        qs = slice(q * 128, (q + 1) * 128)
        dt_q, adt_q = dt_all[:, q, :], adt_all[:, q, :]
        acs, negacs, dd, dst, cd, ea, dtd = [smalls[:, i, :] for i in range(7)]
        bk = nb()
        P.mm([res("adt"), res("consts")], [Rb[bk]], out=ps[:, bk, 0:32], lhsT=U_f, rhs=adt_q, start=True, stop=True)
        P.mm([res("adt"), res("consts")], [Rb[bk]], out=ps[:, bk, 32:64], lhsT=ones_f, rhs=adt_q, start=True, stop=True)
        P.add("dve", "tensor_copy", [Rb[bk]], [res("acs")], out=acs, in_=ps[:, bk, 0:32])
        P.add("dve", "tensor_tensor", [res("lndt"), res("acs")], [res("negacs")], out=negacs, in0=lndt[:, q, :],
              in1=acs, op=ALU.subtract)
        P.add("dve", "tensor_tensor", [Rb[bk], res("acs")], [res("dd")], out=dd, in0=ps[:, bk, 32:64], in1=acs,
              op=ALU.subtract)
        P.act([res("dd")], [res("dst")], out=dst, in_=dd, func=AF.Exp)
        P.act([Rb[bk]], [res("cd")], out=cd, in_=ps[:, bk, 32:64], func=AF.Exp)
        P.act([res("acs")], [res("ea")], out=ea, in_=acs, func=AF.Exp)
        P.add("dve", "tensor_tensor", [res("dt"), res("dst")], [res("dtd")], out=dtd, in0=dt_q, in1=dst, op=ALU.mult)
        P.add("pool", "memset", [], [res("ssq")], ap=ssq[:, 0:4], constant=0.0)
        bkB = nb()
        for g in range(4):
            P.mm([res("BT"), res("identb")], [Rb[bkB]], out=ps[:, bkB, g * 128:(g + 1) * 128], lhsT=BT[:, g, qs],
                 rhs=identb, start=True, stop=True)
        bkS = nb()
        for g in range(4):
            P.mm([res("BT"), res("CT")], [Rb[bkS]], out=ps[:, bkS, g * 128:(g + 1) * 128], lhsT=BT[:, g, qs],
                 rhs=CT[:, g, qs], start=True, stop=True)
        xbanks = []
        for g4 in range(4):
            bk = nb()
            xbanks.append(bk)
            for j in range(4):
                t = g4 * 4 + j
                P.add("pe", "transpose", [Rxs[t], res("consts")], [Rb[bk]], out=ps[:, bk, j * 128:(j + 1) * 128],
                      in_=xs[:, t, qs], identity=ident_f)
        P.act([Rb[bkB]], [res("Btok")], out=Btok, in_=ps[:, bkB, :], func=AF.Copy)
        P.add("dve", "tensor_tensor", [Rb[bkS], res("consts")], [res("scm")], out=scm, in0=v3(ps[:, bkS, :], 4),
              in1=mask01.unsqueeze(1).to_broadcast([128, 4, 128]), op=ALU.mult)
        for g4 in range(4):
            bk = xbanks[g4]
            gsl = slice(g4 * 512, (g4 + 1) * 512)
            hs = slice(g4 * 8, (g4 + 1) * 8)
            P.act([Rb[bk]], [res("xdt_%d" % g4)], out=xdt[:, gsl], in_=ps[:, bk, :], func=AF.Copy)
            P.add("pool", "tensor_tensor", [res("xdt_%d" % g4), res("dtd")], [res("xdtd_%d" % g4)],
                  out=v3(xdtd[:, gsl], 8), in0=v3(xdt[:, gsl], 8), in1=bc_heads(dtd[:, hs]), op=ALU.mult)

        abc_bank = {}

        def emit_abc(hq):
            bk = nb(hold=True)
            abc_bank[hq] = bk
            P.mm([res("identb"), res("negmask")], [Rb[bk]], out=ps[:, bk, :], lhsT=identb,
                 rhs=negmask.rearrange("p a b -> p (a b)"), start=True, stop=False)
            for i in range(4):
                h = hq * 4 + i
                P.mm([res("adt"), res("consts")], [Rb[bk]], out=ps[:, bk, i * 128:(i + 1) * 128],
                     lhsT=adt_q[:, h:h + 1].to_broadcast([128, 128]), rhs=U_f, start=False, stop=(i == 3))

        def emit_quad_rest(hq, ydb):
            g = hq // 2
            bk = abc_bank[hq]
            di = hq % 3
            rdm, rm4 = res("Dm%d" % di), res("M4%d" % di)
            for i in range(4):
                h = hq * 4 + i
                P.act([Rb[bk], res("negacs")], [rdm], out=Dm[di][:, i, :], in_=ps[:, bk, i * 128:(i + 1) * 128],
                      func=AF.Exp, bias=negacs[:, h:h + 1], scale=1.0)
            rel(bk)
            P.add("dve", "tensor_tensor", [rdm, res("scm")], [rm4], out=M4[di], in0=Dm[di],
                  in1=scm[:, g, :].unsqueeze(1).to_broadcast([128, 4, 128]), op=ALU.mult)
            for i in range(4):
                h = hq * 4 + i
                P.mm([rm4, res("xdt_%d" % (h // 8))], [Rb[ydb]], out=ps[:, ydb, (h % 8) * 64:(h % 8 + 1) * 64],
                     lhsT=M4[di][:, i, :], rhs=xdt[:, h * 64:(h + 1) * 64], start=True, stop=False)
                P.mm([res("DI"), res("xdt_%d" % (h // 8))], [Rb[ydb]], out=ps[:, ydb, (h % 8) * 64:(h % 8 + 1) * 64],
                     lhsT=DI[:, h, :], rhs=xdt[:, h * 64:(h + 1) * 64], start=False, stop=True)

        def tail_dve(g, ydb, yob):
            gsl = slice(g * 512, (g + 1) * 512)
            hs = slice(g * 8, (g + 1) * 8)
            yi = g % 2
            ryt, rya = res("ytmp%d" % yi), res("yacc%d" % yi)
            P.add("dve", "tensor_tensor", [Rb[yob], res("ea")], [ryt], out=v3(ytmp[yi], 8), in0=v3(ps[:, yob, :], 8),
                  in1=bc_heads(ea[:, hs]), op=ALU.mult)
            P.add("dve", "tensor_tensor", [Rb[ydb], ryt], [rya], out=yacc[yi], in0=ps[:, ydb, :], in1=ytmp[yi], op=ALU.add)
            P.add("dve", "tensor_tensor", [rya, Rzq[q]], [rya], out=yacc[yi], in0=yacc[yi], in1=zq[:, q, gsl], op=ALU.mult)

        def tail_act(g):
            yi = g % 2
            rya, ryn = res("yacc%d" % yi), res("ynb%d" % yi)
            P.act([rya], [res("junk"), res("ssq")], out=junk, in_=yacc[yi], func=AF.Square, accum_out=ssq[:, g:g + 1])
            P.act([res("ssq"), res("epsc")], [res("rsg")], out=ssq[:, 4 + g:5 + g], in_=ssq[:, g:g + 1], func=AF.Ln,
                  bias=epsc[:, 0:1], scale=1.0)
            P.act([res("rsg")], [res("rsg")], out=ssq[:, 4 + g:5 + g], in_=ssq[:, 4 + g:5 + g], func=AF.Exp, scale=-0.5)
            P.act([rya, res("rsg")], [ryn], out=ynb[yi], in_=yacc[yi], func=AF.Identity, scale=ssq[:, 4 + g:5 + g])

        def tail_pe(g):
            yi = g % 2
            ryn = res("ynb%d" % yi)
            bk = nb()
            for j in range(4):
                P.mm([ryn, res("identb")], [Rb[bk]], out=ps[:, bk, j * 128:(j + 1) * 128],
                     lhsT=ynb[yi][:, j * 128:(j + 1) * 128], rhs=identb, start=True, stop=True)
            for j in range(4):
                t = g * 4 + j
                P.act([Rb[bk], res("ngs")], [Rzq[q]], out=ynT[:, q, t, :], in_=ps[:, bk, j * 128:(j + 1) * 128],
                      func=AF.Identity, scale=ngs[:, l, t:t + 1])

        emit_abc(0)
        emit_abc(1)
        ydbs = {}

        def do_tail(g):
            yob = nb()
            gsl = slice(g * 512, (g + 1) * 512)
            P.mm([res("CT"), Rstb[g]], [Rb[yob]], out=ps[:, yob, :], lhsT=CT[:, g, qs], rhs=state_bf[:, gsl],
                 start=True, stop=True)
            tail_dve(g, ydbs[g], yob)
            rel(ydbs[g])

        for g in range(4):
            ydbs[g] = nb(hold=True)
            for hh in range(2):
                hq = g * 2 + hh
                if hq + 2 < 8:
                    emit_abc(hq + 2)
                emit_quad_rest(hq, ydbs[g])
                if hh == 0:
                    if g >= 1:
                        do_tail(g - 1)
                    if g >= 2:
                        tail_act(g - 2)
                else:
                    if g >= 2:
                        tail_pe(g - 2)
        do_tail(3)
        tail_act(2)
        tail_pe(2)
        tail_act(3)
        tail_pe(3)
        for g in range(4):
            gsl = slice(g * 512, (g + 1) * 512)
            hs = slice(g * 8, (g + 1) * 8)
            bk = nb()
            P.mm([res("Btok"), res("xdtd_%d" % g)], [Rb[bk]], out=ps[:, bk, :], lhsT=Btok[:, g * 128:(g + 1) * 128],
                 rhs=xdtd[:, gsl], start=True, stop=True)
            P.add("pool", "tensor_tensor", [Rst[l][g], res("cd")], [Rst[l][g]], out=v3(state[:, l, gsl], 8),
                  in0=v3(state[:, l, gsl], 8), in1=bc_heads(cd[:, hs]), op=ALU.mult)
            P.add("dve", "tensor_tensor", [Rb[bk], Rst[l][g]], [Rst[l][g]], out=state[:, l, gsl], in0=ps[:, bk, :],
                  in1=state[:, l, gsl], op=ALU.add)
            P.act([Rst[l][g]], [Rstb[g]], out=state_bf[:, gsl], in_=state[:, l, gsl], func=AF.Copy)

    def sub1(l, b):
        P.barrier()
        pre_norm(l, b, 0, 0)
        for zc in range(4):
            s = wload("w_in", l, 0, 8, C_Z + zc * 512, 512)
            for q in range(4):
                bk = nb()
                for kc in range(8):
                    P.mm([Rring[s], Rh[kc]], [Rb[bk]], out=ps[:, bk, :], lhsT=hT[:, kc, q * 128:(q + 1) * 128],
                         rhs=ring_b[s][:, kc, :], start=(kc == 0), stop=(kc == 7))
                P.act([Rb[bk]], [Rzq[q]], out=zq[:, q, zc * 512:(zc + 1) * 512], in_=ps[:, bk, :], func=AF.Silu)
        for xc in range(6):
            s = wload("w_in", l, 0, 8, C_XBC + xc * 512, 512)
            for j in range(4):
                t = xc * 4 + j
                bk = nb()
                mm8(bk, s, j, lambda kc: hT[:, kc, :], Rh)
                fin = conv_from_bank(l, bk, (halo_ssd, "ssd"), t, 4, SSD_CW + t, 24)

                def post(b2, t=t):
                    bias = pv[:, l, SSD_CB + t:SSD_CB + t + 1]
                    if t < 16:
                        P.act([Rb[b2], res("pv")], [Rxs[t]], out=xs[:, t, :], in_=ps[:, b2, :], func=AF.Silu, bias=bias, scale=1.0)
                    elif t < 20:
                        P.act([Rb[b2], res("pv")], [res("BT")], out=BT[:, t - 16, :], in_=ps[:, b2, :], func=AF.Silu, bias=bias, scale=1.0)
                    else:
                        P.act([Rb[b2], res("pv")], [res("CT")], out=CT[:, t - 20, :], in_=ps[:, b2, :], func=AF.Silu, bias=bias, scale=1.0)
                defer(fin, post)
        flush()
        bk = nb()
        for q in range(4):
            for kc in range(8):
                P.mm([res("wdt"), Rh[kc]], [Rb[bk]], out=ps[:, bk, q * 32:(q + 1) * 32], lhsT=hT[:, kc, q * 128:(q + 1) * 128],
                     rhs=wdt[:, l, kc, :], start=(kc == 0), stop=(kc == 7))
        P.add("dve", "tensor_tensor", [Rb[bk], res("tokb")], [res("dtr")], out=dtr, in0=v3(ps[:, bk, 0:128], 4),
              in1=tokb[:, l, 0:32].unsqueeze(1).to_broadcast([128, 4, 32]), op=ALU.add)
        P.act([res("dtr")], [res("dtr")], out=dtr, in_=dtr, func=AF.Exp)
        P.act([res("dtr")], [res("dt")], out=dt_all, in_=dtr, func=AF.Ln, bias=1.0)
        P.act([res("dt")], [res("lndt")], out=lndt, in_=dt_all, func=AF.Ln)
        P.add("dve", "tensor_tensor", [res("dt"), res("negA")], [res("adt")], out=adt_all, in0=dt_all,
              in1=negA[:, l, :].unsqueeze(1).to_broadcast([128, 4, 32]), op=ALU.mult)
        for g in range(4):
            gsl = slice(g * 512, (g + 1) * 512)
            P.act([Rst[l][g]], [Rstb[g]], out=state_bf[:, gsl], in_=state[:, l, gsl], func=AF.Copy)
        P.add("pool", "tensor_tensor", [res("identb"), res("tokb")], [res("DI")], out=DI,
              in0=identb.unsqueeze(1).to_broadcast([128, 32, 128]),
              in1=tokb[:, l, 64:96].unsqueeze(2).to_broadcast([128, 32, 128]), op=ALU.mult)
        for q in range(4):
            ssd_chunk(l, q)
        if STOP_SSD:
            return
        P.barrier()
        Rscu = [res("scu%d" % i) for i in range(8)]
        for half in range(2):
            sC = wload("w_in", l, 0, 8, C_SCC + half * 512, 512)
            sH = wload("w_in", l, 0, 8, C_SCH + half * 512, 512)
            sB = wload("w_in", l, 0, 8, C_SCB + half * 512, 512)
            for j in range(4):
                ct = half * 4 + j
                bC = nb()
                mm8(bC, sC, j, lambda kc: hT[:, kc, :], Rh)
                gi = ct % 2
                P.act([Rb[bC]], [res("gtmp%d" % gi)], out=gtmp[gi], in_=ps[:, bC, :], func=AF.Copy)
                bH = nb()
                mm8(bH, sH, j, lambda kc: hT[:, kc, :], Rh)
                def src(dst, rci, bH=bH, gi=gi):
                    P.add("dve", "tensor_tensor", [Rb[bH], res("gtmp%d" % gi)], [rci], out=dst, in0=ps[:, bH, :],
                          in1=gtmp[gi], op=ALU.mult)
                fin = conv_pe(l, src, (halo_sc, "sc"), ct, 3, SC_CW + ct, 8)
                bB = nb()
                mm8(bB, sB, j, lambda kc: hT[:, kc, :], Rh)

                def post(b2, ct=ct, bB=bB):
                    ai = ct % 2
                    P.act([Rb[b2]], [res("acc%d" % ai)], out=accb[ai], in_=ps[:, b2, :], func=AF.Copy)
                    P.add("dve", "tensor_tensor", [Rb[bB], res("acc%d" % ai)], [Rscu[ct]], out=sc_u[:, ct, :], in0=ps[:, bB, :],
                          in1=accb[ai], op=ALU.mult)
                defer(fin, post, depth=1)
        flush()
        for half in range(2):
            sO = wload("w_sc_out", l, 0, 8, half * 512, 512)
            sG = wload("w_in", l, 0, 8, C_GSC + half * 512, 512)
            for j in range(4):
                ct = half * 4 + j
                bY = nb()
                mm8(bY, sO, j, lambda kc: sc_u[:, kc, :], Rscu)
                bG = nb()
                mm8(bG, sG, j, lambda kc: hT[:, kc, :], Rh)
                gi = ct % 2
                P.act([Rb[bG]], [res("gtmp%d" % gi)], out=gtmp[gi], in_=ps[:, bG, :], func=AF.Sigmoid)
                P.add("dve", "tensor_tensor", [Rb[bY], res("gtmp%d" % gi)], [Rmix[ct]], out=mixacc[:, ct, :],
                      in0=ps[:, bY, :], in1=gtmp[gi], op=ALU.mult)
        Rmb = [res("mixb%d" % i) for i in range(8)]
        for half in range(2):
            sA = wload("w_ssd_out", l, 0, 8, half * 512, 512)
            sA2 = wload("w_ssd_out", l, 8, 8, half * 512, 512)
            sG = wload("w_in", l, 0, 8, C_GSSD + half * 512, 512)
            for j in range(4):
                ct = half * 4 + j
                bY = nb()
                for t in range(16):
                    sl = sA if t < 8 else sA2
                    P.mm([Rring[sl]] + Rzq, [Rb[bY]], out=v3(ps[:, bY, :], 4), lhsT=ring_b[sl][:, t % 8, j * 128:(j + 1) * 128],
                         rhs=ynT[:, :, t, :], start=(t == 0), stop=(t == 15))
                bG = nb()
                mm8(bG, sG, j, lambda kc: hT[:, kc, :], Rh)
                gi = ct % 2
                P.act([Rb[bG]], [res("gtmp%d" % gi)], out=gtmp[gi], in_=ps[:, bG, :], func=AF.Sigmoid)
                P.add("dve", "tensor_tensor", [Rb[bY], res("gtmp%d" % gi)], [res("gtmp%d" % gi)], out=gtmp[gi],
                      in0=ps[:, bY, :], in1=gtmp[gi], op=ALU.mult)
                P.add("pool", "tensor_tensor", [res("gtmp%d" % gi), Rmix[ct]], [Rmb[ct]], out=mixb[:, ct, :], in0=gtmp[gi],
                      in1=mixacc[:, ct, :], op=ALU.add)
        for half in range(2):
            s = wload("w_o", l, 0, 8, half * 512, 512)
            for j in range(4):
                ct = half * 4 + j
                bk = nb()
                mm8(bk, s, j, lambda kc: mixb[:, kc, :], Rmb)
                P.act([Rb[bk]], [Rmix[ct]], out=mixacc[:, ct, :], in_=ps[:, bk, :], func=AF.Copy)
        post_norm(l, b, 2)

    def sub2(l, b):
        P.barrier()
        pre_norm(l, b, 1, 24)
        Rf = [res("f%d" % i) for i in range(22)]
        blocks = [(0, 4), (4, 4), (8, 4), (12, 4), (16, 4), (20, 2)]
        for (t0, nt) in blocks:
            sg = wload("w_up", l, 0, 8, t0 * 128, nt * 128)
            sv = wload("w_up", l, 0, 8, 2816 + t0 * 128, nt * 128)
            for j in range(nt):
                i = t0 + j
                bG = nb()
                mm8(bG, sg, j, lambda kc: hT[:, kc, :], Rh)
                fing = conv_from_bank(l, bG, (halo_ffn, "ffn"), i, 3, FFN_CW + i, 44)

                def postg(b2g, i=i):
                    gi = i % 2
                    P.act([Rb[b2g], res("pv")], [res("gtmp%d" % gi)], out=gtmp[gi], in_=ps[:, b2g, :], func=AF.Silu,
                          bias=pv[:, l, FFN_CB + i:FFN_CB + i + 1], scale=1.0)
                defer(fing, postg)
                bV = nb()
                mm8(bV, sv, j, lambda kc: hT[:, kc, :], Rh)
                finv = conv_from_bank(l, bV, (halo_ffn, "ffn"), 22 + i, 3, FFN_CW + 22 + i, 44)

                def postv(b2v, i=i):
                    gi = i % 2
                    P.add("dve", "scalar_tensor_tensor", [Rb[b2v], res("pv"), res("gtmp%d" % gi)], [Rf[i]], out=fbuf[:, i, :],
                          in0=ps[:, b2v, :], scalar=pv[:, l, FFN_CB + 22 + i:FFN_CB + 22 + i + 1], in1=gtmp[gi],
                          op0=ALU.add, op1=ALU.mult)
                defer(finv, postv)
        flush()
        for half in range(2):
            ss = [wload("w_down", l, 0, 8, half * 512, 512), wload("w_down", l, 8, 8, half * 512, 512),
                  wload("w_down", l, 16, 6, half * 512, 512)]
            for j in range(4):
                ct = half * 4 + j
                bk = nb()
                for kc in range(22):
                    sl = ss[kc // 8]
                    P.mm([Rring[sl], Rf[kc]], [Rb[bk]], out=ps[:, bk, :], lhsT=ring_b[sl][:, kc % 8, j * 128:(j + 1) * 128],
                         rhs=fbuf[:, kc, :], start=(kc == 0), stop=(kc == 21))
                P.act([Rb[bk]], [res("obuf2_%d" % ct)], out=obuf2[:, ct, :], in_=ps[:, bk, :], func=AF.Copy)
        norm_stats([obuf2[:, kc, :] for kc in range(8)], [res("obuf2_%d" % kc) for kc in range(8)])
        for kc in range(8):
            P.add("dve", "tensor_tensor", [res("obuf2_%d" % kc), res("rstd")], [Rsq[kc % 2]], out=tmpA[kc % 2],
                  in0=obuf2[:, kc, :], in1=rstd, op=ALU.mult)
            P.add("dve", "scalar_tensor_tensor", [Rsq[kc % 2], res("modd"), Rx[kc]], [Rx[kc]], out=xT[:, kc, :],
                  in0=tmpA[kc % 2], scalar=G(l, 3, kc, b), in1=xT[:, kc, :], op0=ALU.mult, op1=ALU.add)

    for b in range(NSEQ):
        for l in range(2):
            for g in range(4):
                P.add("pool", "memset", [], [Rst[l][g]], ap=state[:, l, g * 512:(g + 1) * 512], constant=0.0)
            P.add("pool", "memset", [], [res("halo_ssd_%d" % l)], ap=halo_ssd[:, l], constant=0.0)
            P.add("pool", "memset", [], [res("halo_sc_%d" % l)], ap=halo_sc[:, l], constant=0.0)
            P.add("pool", "memset", [], [res("halo_ffn_%d" % l)], ap=halo_ffn[:, l], constant=0.0)
        for tt in range(NTT):
            P.dma("sp", [], Rx, "xload", out=xT,
                  in_=xT_d[b, :, tt * T:(tt + 1) * T].rearrange("(k p) t -> p k t", p=128))
            for l in range(NL):
                sub1(l, b)
                if SUB2:
                    sub2(l, b)
            P.dma("pool", Rx, [], "store", out=yT_d[b, :, tt * T:(tt + 1) * T].rearrange("(k p) t -> p k t", p=128),
                  in_=xT)

    if dbg:
        for name in dbg:
            ap = {"hT": hT, "zq": zq, "xs": xs, "BT": BT, "CT": CT, "dt": dt_all, "adt": adt_all, "ynT": ynT, "mixb": mixb,
                  "mixacc": mixacc, "xT": xT, "negmask": negmask, "identb": identb, "smalls": smalls, "ssq": ssq, "scm": scm, "Dm1": Dm[1], "M41": M4[1],
                  "yacc1": yacc[1], "ytmp1": ytmp[1], "ynb1": ynb[1], "xdt": xdt, "Btok": Btok, "modT": modT, "modd": modd, "state": state, "fbuf": fbuf, "scu": sc_u}[name]
            shp = list(ap.shape)
            dt_ = ap.dtype
            o = nc.dram_tensor("dbg_" + name, shp, dt_, kind="ExternalOutput").ap()
            P.dma("pool", list(R.values()), [], "store", out=o, in_=ap)

    semnames = P.finalize()
    import contextlib
    with contextlib.ExitStack() as es:
        sems = {n: es.enter_context(nc.semaphore(n)) for n in semnames}
        block = es.enter_context(nc.Block())

        @block.tensor
        def _(e):
            P.replay("pe", e, sems)

        @block.scalar
        def _(e):
            P.replay("act", e, sems)

        @block.vector
        def _(e):
            P.replay("dve", e, sems)

        @block.gpsimd
        def _(e):
            P.replay("pool", e, sems, final_waits=["store"])

        @block.sync
        def _(e):
            P.replay("sp", e, sems)
    return nc, P


def host_prep(inp):
    f = lambda a: np.ascontiguousarray(np.asarray(a, dtype=np.float32))
    pvs, tokbs = [], []
    for l in range(2):
        pvl = np.zeros((128, NPV), np.float32)

        def put(off, vec):
            v = np.asarray(vec, np.float32).reshape(-1, 128).T
            pvl[:, off:off + v.shape[1]] = v

        put(G_MIXPRE, inp["mix_pre_g"][l]); put(G_MIXPOST, inp["mix_post_g"][l])
        put(G_FFNPRE, inp["ffn_pre_g"][l]); put(G_FFNPOST, inp["ffn_post_g"][l])
        for k in range(4):
            put(SSD_CW + k * 24, inp["ssd_conv_w"][l][k])
        put(SSD_CB, inp["ssd_conv_b"][l]); put(SSD_NG, inp["ssd_norm_g"][l])
        for k in range(3):
            put(SC_CW + k * 8, inp["sc_conv_w"][l][k])
            put(FFN_CW + k * 44, inp["ffn_conv_w"][l][k])
        put(FFN_CB, inp["ffn_conv_b"][l]); put(ADA_B, inp["ada_b"][l])
        pvs.append(pvl)
        tb = np.zeros((128, 96), np.float32)
        tb[:, 0:32] = np.asarray(inp["ssd_dt_bias"][l])[None, :]
        tb[:, 32:64] = np.asarray(inp["ssd_a_log"][l])[None, :]
        tb[:, 64:96] = np.asarray(inp["ssd_d"][l])[None, :]
        tokbs.append(tb)
    pv = np.ascontiguousarray(np.stack(pvs, 1))
    tokb = np.ascontiguousarray(np.stack(tokbs, 1))
    consts = np.zeros((128, 4, 128), np.float32)
    consts[:, 0] = np.eye(128)
    consts[:, 1] = np.triu(np.ones((128, 128)))
    consts[:, 2] = 1.0
    consts[:, 3] = np.triu(np.ones((128, 128)))
    shared = {"pv": pv, "tokb": tokb, "consts": consts, "ada_w": f(inp["ada_w"])}
    for name in WSHAPES:
        shared[name] = f(inp[name])
    x = np.asarray(inp["x"], np.float32)
    c = np.asarray(inp["c"], np.float32)
    maps = []
    for i in range(8):
        m = dict(shared)
        m["xT"] = np.ascontiguousarray(x[2 * i:2 * i + 2].transpose(0, 2, 1))
        m["cT"] = np.ascontiguousarray(c[2 * i:2 * i + 2].reshape(2, 8, 128).transpose(2, 1, 0))
        maps.append(m)
    return maps


def kernel(**inputs):
    maps = host_prep(inputs)
    nc, _ = build()
    res = run_bass_kernel_spmd(nc, maps, core_ids=list(range(8)))
    out = np.empty((16, 4096, 1024), np.float32)
    for i in range(8):
        out[2 * i:2 * i + 2] = res.results[i]["yT"].transpose(0, 2, 1)
    return out
```
